# Optimizing a Trainium2 kernel written in Bass

```python
import math
import jax, jax.numpy as jnp
from jax import lax
import numpy as np

D_MODEL = 1024
BATCH = 8
SEQ = 2048
DEPTH = 2

CHUNK = 64
EPS = 1e-6
A_HEADS = 8
A_DK = 128
A_DV = 128
A_QK = A_HEADS * A_DK
A_VW = A_HEADS * A_DV
CONV_K = 4
A_CONV_CH = 2 * A_QK + A_VW
B_HEADS = 16
B_DH = 64
B_W = B_HEADS * B_DH
BAND_CHUNKS = 8
BAND_LEN = (BAND_CHUNKS + 1) * CHUNK
MAX_REL = 256
N_REL = (CHUNK - 1) + MAX_REL + 1
MEM_LEN = 256
M_HEADS = 4
M_DH = D_MODEL // M_HEADS
D_FF = ((8 * D_MODEL + 3 * 256 - 1) // (3 * 256)) * 256
IN_SPLITS = (A_QK, A_QK, A_VW, A_VW, A_HEADS, A_HEADS, B_W, B_W, B_W, D_MODEL, D_MODEL)
N_IN = sum(IN_SPLITS)

kernel_name = "hybrid_stream_delta_band_block"


def rmsnorm(x, g):
    xf = x.astype(jnp.float32)
    y = xf * lax.rsqrt(jnp.mean(xf * xf, axis=-1, keepdims=True) + EPS)
    return (y * g.astype(jnp.float32)).astype(x.dtype)


def _l2norm(x):
    return x * lax.rsqrt(jnp.sum(x * x, axis=-1, keepdims=True) + EPS)


def _split_cols(t, sizes):
    outs, start = [], 0
    for n in sizes:
        outs.append(t[..., start:start + n])
        start += n
    return outs


def _to_chunks(t, h, d):
    b, s, _ = t.shape
    return t.reshape(b, s // CHUNK, CHUNK, h, d).transpose(0, 3, 1, 2, 4)


def _heads_to_chunks(t):
    b, s, h = t.shape
    return t.reshape(b, s // CHUNK, CHUNK, h).transpose(0, 3, 1, 2)


def _causal_dwconv(t, w):
    return lax.conv_general_dilated(
        t, w[:, None, :], window_strides=(1,), padding=[(CONV_K - 1, 0)],
        dimension_numbers=("NWC", "WIO", "NWC"), feature_group_count=t.shape[-1])


def _gated_deltanet(q, k, v, alpha, beta, a_log, dt_bias):
    dtype = v.dtype
    b, s, _ = v.shape
    f32 = jnp.float32
    q = _l2norm(_to_chunks(q.astype(f32), A_HEADS, A_DK)) * (A_DK ** -0.5)
    k = _l2norm(_to_chunks(k.astype(f32), A_HEADS, A_DK))
    v = _to_chunks(v.astype(f32), A_HEADS, A_DV)
    beta = _heads_to_chunks(jax.nn.sigmoid(beta.astype(f32)))
    g = -jnp.exp(a_log.astype(f32)) * jax.nn.softplus(alpha.astype(f32) + dt_bias.astype(f32))
    G = jnp.cumsum(_heads_to_chunks(g), axis=-1)
    incl = jnp.tril(jnp.ones((CHUNK, CHUNK), dtype=bool))
    strict = jnp.tril(jnp.ones((CHUNK, CHUNK), dtype=bool), -1)
    diff = G[..., :, None] - G[..., None, :]
    decay = jnp.where(incl, jnp.exp(jnp.where(incl, diff, 0.0)), 0.0)
    kk = jnp.einsum("bhncd,bhnjd->bhncj", k, k)
    L = jnp.where(strict, beta[..., :, None] * kk * decay, 0.0)
    rhs = jnp.concatenate([beta[..., None] * v, (beta * jnp.exp(G))[..., None] * k], axis=-1)
    sol = lax.linalg.triangular_solve(jnp.eye(CHUNK, dtype=f32) + L, rhs, left_side=True,
                                      lower=True, unit_diagonal=True)
    u_t, w = sol[..., :A_DV], sol[..., A_DV:]
    p_intra = jnp.einsum("bhncd,bhnjd->bhncj", q, k) * decay
    q_dec = q * jnp.exp(G)[..., None]
    g_last = G[..., -1]
    k_dec = k * jnp.exp(g_last[..., None] - G)[..., None]
    xs = tuple(jnp.moveaxis(t, 2, 0) for t in (u_t, w, p_intra, q_dec, k_dec, jnp.exp(g_last)))

    def step(S, inp):
        ut, wt, pt, qt, kt, et = inp
        U = ut - jnp.einsum("bhck,bhkv->bhcv", wt, S)
        o = jnp.einsum("bhck,bhkv->bhcv", qt, S) + jnp.einsum("bhcj,bhjv->bhcv", pt, U)
        S = S * et[..., None, None] + jnp.einsum("bhck,bhcv->bhkv", kt, U)
        return S, o

    S0 = jnp.zeros((b, A_HEADS, A_DK, A_DV), f32)
    _, o = lax.scan(step, S0, xs)
    return o.transpose(1, 0, 3, 2, 4).reshape(b, s, A_HEADS, A_DV).astype(dtype)


def _band_bias(rel_bias):
    i = jnp.arange(CHUNK)[:, None]
    kpos = jnp.arange(BAND_LEN)[None, :]
    r = (BAND_CHUNKS - kpos // CHUNK) * CHUNK + i - kpos % CHUNK
    idx = jnp.clip(r, -(CHUNK - 1), MAX_REL) + (CHUNK - 1)
    return rel_bias[:, idx].astype(jnp.float32)


def _chunk_band_attention(q, k, v, bias):
    b, s, _ = q.shape
    q = _to_chunks(q, B_HEADS, B_DH)
    pad = ((0, 0), (0, 0), (BAND_CHUNKS, 0), (0, 0), (0, 0))
    kp = jnp.pad(_to_chunks(k, B_HEADS, B_DH), pad)
    vp = jnp.pad(_to_chunks(v, B_HEADS, B_DH), pad)
    key_chunk_off = jnp.repeat(jnp.arange(BAND_CHUNKS + 1), CHUNK) - BAND_CHUNKS
    scale = B_DH ** -0.5

    def one_chunk(n):
        qn = lax.dynamic_index_in_dim(q, n, axis=2, keepdims=False)
        kb = lax.dynamic_slice_in_dim(kp, n, BAND_CHUNKS + 1, axis=2).reshape(b, B_HEADS, BAND_LEN, B_DH)
        vb = lax.dynamic_slice_in_dim(vp, n, BAND_CHUNKS + 1, axis=2).reshape(b, B_HEADS, BAND_LEN, B_DH)
        sc = jnp.einsum("bhqd,bhkd->bhqk", qn, kb).astype(jnp.float32) * scale + bias
        sc = jnp.where((n + key_chunk_off) >= 0, sc, -1e30)
        p = jax.nn.softmax(sc, axis=-1).astype(vb.dtype)
        return jnp.einsum("bhqk,bhkd->bhqd", p, vb)

    o = lax.map(one_chunk, jnp.arange(s // CHUNK))
    return o.transpose(1, 0, 3, 2, 4).reshape(b, s, B_W)


def _memory_xattn(h, memn, w_q, w_kv, w_out):
    b, s, _ = h.shape
    m = memn.shape[1]
    q = (h @ w_q).reshape(b, s, M_HEADS, M_DH)
    kk, vv = _split_cols(memn @ w_kv, (M_HEADS * M_DH, M_HEADS * M_DH))
    kk = kk.reshape(b, m, M_HEADS, M_DH)
    vv = vv.reshape(b, m, M_HEADS, M_DH)
    sc = jnp.einsum("bqhd,bkhd->bhqk", q, kk).astype(jnp.float32) * (M_DH ** -0.5)
    p = jax.nn.softmax(sc, axis=-1).astype(vv.dtype)
    o = jnp.einsum("bhqk,bkhd->bqhd", p, vv).reshape(b, s, M_HEADS * M_DH)
    return o @ w_out


def setup_inputs(seed: int = 0) -> dict:
    key = jax.random.key(seed)
    ks = jax.random.split(key, 24)
    f32 = jnp.float32

    def nrm(k, shape, scale):
        return jax.random.normal(k, shape, f32) * scale

    def gain(k, shape):
        return 1.0 + 0.02 * jax.random.normal(k, shape, f32)

    dt = jnp.exp(jax.random.uniform(ks[6], (DEPTH, A_HEADS), f32, math.log(1e-3), math.log(1e-1)))
    return {
        "x": nrm(ks[0], (BATCH, SEQ, D_MODEL), 1.0),
        "mem": nrm(ks[1], (BATCH, MEM_LEN, D_MODEL), 1.0),
        "norm_mix": gain(ks[2], (DEPTH, D_MODEL)),
        "w_in": nrm(ks[3], (DEPTH, D_MODEL, N_IN), D_MODEL ** -0.5),
        "conv_w": nrm(ks[4], (DEPTH, CONV_K, A_CONV_CH), CONV_K ** -0.5),
        "a_log": jnp.log(jax.random.uniform(ks[5], (DEPTH, A_HEADS), f32, 1.0, 16.0)),
        "dt_bias": dt + jnp.log(-jnp.expm1(-dt)),
        "head_norm": gain(ks[7], (DEPTH, A_DV)),
        "w_a_out": nrm(ks[8], (DEPTH, A_VW, D_MODEL), A_VW ** -0.5),
        "w_b_out": nrm(ks[9], (DEPTH, B_W, D_MODEL), B_W ** -0.5),
        "rel_bias": nrm(ks[10], (B_HEADS, N_REL), 0.5),
        "w_o": nrm(ks[11], (DEPTH, D_MODEL, D_MODEL), D_MODEL ** -0.5),
        "norm_xattn": gain(ks[12], (DEPTH, D_MODEL)),
        "norm_mem": gain(ks[13], (DEPTH, D_MODEL)),
        "w_mq": nrm(ks[14], (DEPTH, D_MODEL, M_HEADS * M_DH), D_MODEL ** -0.5),
        "w_mkv": nrm(ks[15], (DEPTH, D_MODEL, 2 * M_HEADS * M_DH), D_MODEL ** -0.5),
        "w_mo": nrm(ks[16], (DEPTH, M_HEADS * M_DH, D_MODEL), (M_HEADS * M_DH) ** -0.5),
        "norm_ffn": gain(ks[17], (DEPTH, D_MODEL)),
        "w_gate_up": nrm(ks[18], (DEPTH, D_MODEL, 2 * D_FF), D_MODEL ** -0.5),
        "w_down": nrm(ks[19], (DEPTH, D_FF, D_MODEL), D_FF ** -0.5),
        "norm_final": gain(ks[20], (D_MODEL,)),
    }


def reference(x, mem, norm_mix, w_in, conv_w, a_log, dt_bias, head_norm, w_a_out, w_b_out,
              rel_bias, w_o, norm_xattn, norm_mem, w_mq, w_mkv, w_mo, norm_ffn, w_gate_up,
              w_down, norm_final):
    b, s, _ = x.shape
    band_bias = _band_bias(rel_bias)
    for l in range(DEPTH):
        h = rmsnorm(x, norm_mix[l])
        qa, ka, va, za, alpha, beta, qb, kb, vb, ga, gb = _split_cols(h @ w_in[l], IN_SPLITS)
        qkv = jax.nn.silu(_causal_dwconv(jnp.concatenate([qa, ka, va], axis=-1), conv_w[l]))
        qa, ka, va = _split_cols(qkv, (A_QK, A_QK, A_VW))
        oa = _gated_deltanet(qa, ka, va, alpha, beta, a_log[l], dt_bias[l])
        oa = rmsnorm(oa, head_norm[l]) * jax.nn.silu(za.reshape(b, s, A_HEADS, A_DV))
        oa = oa.reshape(b, s, A_VW)
        ob = _chunk_band_attention(qb, kb, vb, band_bias)
        y = jax.nn.sigmoid(ga) * (oa @ w_a_out[l]) + jax.nn.sigmoid(gb) * (ob @ w_b_out[l])
        x = x + y @ w_o[l]
        x = x + _memory_xattn(rmsnorm(x, norm_xattn[l]), rmsnorm(mem, norm_mem[l]),
                              w_mq[l], w_mkv[l], w_mo[l])
        gate, up = _split_cols(rmsnorm(x, norm_ffn[l]) @ w_gate_up[l], (D_FF, D_FF))
        x = x + (jax.nn.silu(gate) * up) @ w_down[l]
    return rmsnorm(x, norm_final)
```

```python
import numpy as np
import concourse.bass as bass
import concourse.mybir as mybir
from concourse.bass_utils import run_bass_kernel_spmd

F32 = mybir.dt.float32
BF16 = mybir.dt.bfloat16
U8 = mybir.dt.uint8
AF = mybir.ActivationFunctionType
ALU = mybir.AluOpType
AX = mybir.AxisListType

S = 2048
D = 1024
NT = 16
DEPTH = 2
EPS = 1e-6
NIN = 9232
OFF_QA, OFF_KA, OFF_VA, OFF_ZA, OFF_AL, OFF_QB, OFF_KB, OFF_VB, OFF_GA, OFF_GB = (
    0, 1024, 2048, 3072, 4096, 4112, 5136, 6160, 7184, 8208)
DFF = 2816
SP_GMIX, SP_GXA, SP_GMEM, SP_GFFN, SP_CW, SP_HN, SP_ALOG, SP_DTB = 0, 8, 16, 24, 32, 128, 129, 137
SP_N = 145
ARENA = 200 * 1024


class Buf:
    __slots__ = ("name", "w", "r")

    def __init__(self, name):
        self.name = name
        self.w = None
        self.r = []


class Eng:
    def __init__(self, name, handle, sem):
        self.name = name
        self.h = handle
        self.sem = sem
        self.count = 0
        self.seen = {}
        self.pending = False


class KB:
    def __init__(self, nc, same_engine_sync=True, dq_n=20):
        self.nc = nc
        self.same = same_engine_sync
        self.sems = {}
        self.E = {}
        for nm, h in (("pe", nc.tensor), ("act", nc.scalar), ("dve", nc.vector), ("pool", nc.gpsimd)):
            s = nc.semaphore("sem_" + nm).__enter__()
            self.E[nm] = Eng(nm, h, s)
        self.E["sp"] = Eng("sp", nc.sync, None)
        self.dq = {}
        for q in ("sp", "pool"):
            sl = [nc.semaphore(f"dq_{q}{i}").__enter__() for i in range(dq_n)]
            self.dq[q] = {"sems": sl, "cum": [0] * len(sl), "i": 0}
        self.all_dma_tokens = []
        self.fresh = False
        self.fresh_sems = []

    def _wait(self, e, tok):
        if tok is None:
            return
        sem, val, owner = tok
        key = id(sem)
        if e.seen.get(key, 0) >= val:
            return
        if owner is not None:
            oe = self.E[owner]
            if owner == e.name:
                if e.name == "pe" or not self.same:
                    return
            assert oe.count >= val, f"wait on unsignalled future {owner} {val} > {oe.count}"
        e.h.wait_ge(sem, val)
        e.seen[key] = val

    def _deps(self, e, reads, writes):
        for b in reads:
            self._wait(e, b.w)
        for b in writes:
            self._wait(e, b.w)
            for t in b.r:
                self._wait(e, t)

    def _commit(self, tok, reads, writes):
        for b in reads:
            b.r.append(tok)
            if len(b.r) > 24:
                b.r = b.r[-24:] if False else self._compact(b.r)
        for b in writes:
            b.w = tok
            b.r = []

    @staticmethod
    def _compact(toks):
        best = {}
        for t in toks:
            k = id(t[0])
            if k not in best or best[k][1] < t[1]:
                best[k] = t
        return list(best.values())

    def op(self, eng, fn, reads=(), writes=(), signal=True):
        e = self.E[eng]
        self._deps(e, reads, writes)
        ins = fn(e.h)
        tok = (e.sem, e.count + 1, eng)
        if signal:
            ins.then_inc(e.sem, 1)
            e.count += 1
            e.pending = False
        else:
            e.pending = True
        self._commit(tok, reads, writes)
        return ins

    def dma(self, q, out, in_, reads=(), writes=()):
        e = self.E[q]
        if self.fresh:
            sem = self.nc.semaphore(f"dqf_{len(self.fresh_sems)}").__enter__()
            self.fresh_sems.append(sem)
            self._deps(e, reads, writes)
            e.h.dma_start(out=out, in_=in_).then_inc(sem, 16)
            tok = (sem, 16, None)
            self.all_dma_tokens.append(tok)
            self._commit(tok, reads, writes)
            return tok
        d = self.dq[q]
        i = d["i"] % len(d["sems"])
        d["i"] += 1
        sem = d["sems"][i]
        self._wait(e, (sem, d["cum"][i], None))
        self._deps(e, reads, writes)
        d["cum"][i] += 16
        e.h.dma_start(out=out, in_=in_).then_inc(sem, 16)
        tok = (sem, d["cum"][i], None)
        self._commit(tok, reads, writes)
        return tok

    def barrier(self):
        toks = []
        for nm in ("pe", "act", "dve", "pool"):
            oe = self.E[nm]
            assert not oe.pending, f"{nm} has unsignalled trailing instruction at barrier"
            if oe.count:
                toks.append((oe.sem, oe.count, nm))
        for q in ("sp", "pool"):
            d = self.dq[q]
            for s, c in zip(d["sems"], d["cum"]):
                if c:
                    toks.append((s, c, None))
        toks.extend(self.all_dma_tokens)
        self.all_dma_tokens = []
        for nm, e in self.E.items():
            for t in toks:
                if t[2] == nm:
                    continue
                self._wait(e, t)


def build_program(dbg=None):
    dbg = dbg or {}
    stop = dbg.get("stop")
    nlayers = dbg.get("layers", DEPTH)
    nc = bass.Bass("TRN2", target_bir_lowering=False)
    kb = KB(nc, same_engine_sync=dbg.get("same", True), dq_n=dbg.get("dq_n", 20))

    def din(name, shape):
        return nc.dram_tensor(name, list(shape), F32, kind="ExternalInput").ap()

    x_d = din("x", [S, D])
    mem_d = din("mem", [256, D])
    w_in = din("w_in", [DEPTH, D, NIN])
    w_a = din("w_a_out", [DEPTH, D, D])
    w_b = din("w_b_out", [DEPTH, D, D])
    w_o = din("w_o", [DEPTH, D, D])
    w_mq = din("w_mq", [DEPTH, D, D])
    w_mkv = din("w_mkv", [DEPTH, D, 2 * D])
    w_mo = din("w_mo", [DEPTH, D, D])
    w_gu = din("w_gate_up", [DEPTH, D, 2 * DFF])
    w_dn = din("w_down", [DEPTH, DFF, D])
    smallp_d = din("smallp", [DEPTH, 128, SP_N])
    gfin_d = din("gfinal", [128, D])
    bias_d = din("biasq", [16, 128, 640])
    cmask_d = din("cmask", [128, 3, 128])
    out_d = nc.dram_tensor("out", [S, D], F32, kind="ExternalOutput").ap()
    xs_d = nc.dram_tensor("x_spill", [S, D], F32, kind="Internal").ap()
    dumps = {}

    arena = nc.sbuf_tensor("arena", [128, ARENA], U8).__enter__()
    psum = nc.psum_tensor("psum", [128, 4096], F32).__enter__()
    PB = [Buf(f"pb{i}") for i in range(8)]

    def bank(i, dt=F32, n=None):
        ap = psum[:, i * 512:(i + 1) * 512]
        if dt == BF16:
            ap = ap.bitcast(BF16)
        return ap

    def quad(qi):
        return psum[:, qi * 2048:(qi + 1) * 2048], PB[qi * 4:(qi + 1) * 4]

    class Alloc:
        def __init__(self, lo, hi):
            self.lo, self.hi, self.p = lo, hi, lo

        def get(self, shape, dt, name=None):
            esz = 4 if dt == F32 else 2
            n = int(np.prod(shape[1:])) * esz
            off = (self.p + 63) // 64 * 64
            assert off + n <= self.hi, f"arena overflow {name} {off + n} > {self.hi}"
            self.p = off + n
            ap = arena[:, off:off + n].bitcast(dt)
            if len(shape) == 3:
                ap = ap.rearrange("p (a b) -> p a b", b=shape[2])
            elif len(shape) == 4:
                ap = ap.rearrange("p (a b c) -> p a b c", b=shape[2], c=shape[3])
            return ap

    CONST_SZ = 13 * 1024
    X_OFF = CONST_SZ
    X_SZ = 64 * 1024
    calloc = Alloc(0, CONST_SZ)
    ident = calloc.get([128, 128], BF16)
    ones_b = calloc.get([128, 128], BF16)
    ones_f = calloc.get([128, 128], F32)
    maskI = calloc.get([128, 128], F32)
    maskS = calloc.get([128, 128], F32)
    maskL = calloc.get([128, 128], F32)
    ident4 = calloc.get([128, 4, 128], BF16)
    maskI4 = calloc.get([128, 4, 128], F32)
    maskS4 = calloc.get([128, 4, 128], F32)
    smallp = calloc.get([128, DEPTH, SP_N], F32)
    bd32 = calloc.get([128, 4, 128], BF16)
    m64T = calloc.get([128, 4, 128], BF16)
    m128T = calloc.get([128, 4, 128], BF16)
    CB = Buf("consts")
    x_sb = arena[:, X_OFF:X_OFF + X_SZ].bitcast(F32).rearrange("p (a b) -> p a b", b=D)
    XT = [Buf(f"x{t}") for t in range(NT)]

    tmpf = arena[:, X_OFF:X_OFF + 512].bitcast(F32)
    TB = Buf("tmpf")

    def mk_mask(dst, cmp_op, base, chmul, patt_step):
        kb.op("pool", lambda e: e.memset(dst, 1.0), writes=[CB])
        kb.op("pool", lambda e: e.affine_select(out=dst, in_=dst, pattern=[[patt_step, 128]],
                                                 compare_op=cmp_op, fill=0.0, base=base,
                                                 channel_multiplier=chmul), reads=[CB], writes=[CB])

    mk_mask(maskI, ALU.is_ge, 0, -1, 1)
    mk_mask(maskS, ALU.is_gt, 0, -1, 1)
    mk_mask(maskL, ALU.is_gt, 0, 1, -1)
    kb.op("pool", lambda e: e.memset(ones_f, 1.0), writes=[CB])
    kb.op("pool", lambda e: e.memset(ones_b, 1.0), writes=[CB])
    kb.op("pool", lambda e: e.memset(tmpf, 1.0), writes=[TB])
    kb.op("pool", lambda e: e.affine_select(out=tmpf, in_=tmpf, pattern=[[-1, 128]], compare_op=ALU.is_equal,
                                             fill=0.0, base=0, channel_multiplier=1), reads=[TB], writes=[TB])
    kb.op("dve", lambda e: e.tensor_copy(out=ident, in_=tmpf), reads=[TB], writes=[CB])
    for c in range(4):
        kb.op("dve", lambda e, c=c: e.tensor_copy(out=ident4[:, c, :], in_=tmpf), reads=[TB], writes=[CB])
        kb.op("dve", lambda e, c=c: e.tensor_copy(out=maskI4[:, c, :], in_=maskI), reads=[CB], writes=[CB])
        kb.op("dve", lambda e, c=c: e.tensor_copy(out=maskS4[:, c, :], in_=maskS), reads=[CB], writes=[CB])
    kb.dma("sp", smallp, smallp_d.rearrange("l p n -> p l n"), writes=[CB])
    cm_stage = arena[:, X_OFF + 4096:X_OFF + 4096 + 3 * 128 * 4].bitcast(F32).rearrange("p (a b) -> p a b", b=128)
    kb.dma("sp", cm_stage, cmask_d, writes=[TB])
    for mi, dstm in enumerate((bd32, m64T, m128T)):
        for c in range(4):
            kb.op("dve", lambda e: e.tensor_copy(out=dstm[:, c, :], in_=cm_stage[:, mi, :]), reads=[TB], writes=[CB])
    kb.barrier()

    def sp_col(l, c0, n=1):
        return smallp[:, l, c0:c0 + n]

    for t in range(NT):
        kb.dma("sp", x_sb[:, t, :], x_d[t * 128:(t + 1) * 128, :], writes=[XT[t]])

    WSLOTS = 4

    class Ctx:
        pass

    def weight_ring(al, n=WSLOTS, kch=8, ncol=128):
        r = Ctx()
        r.aps = [al.get([128, kch, ncol], BF16, "wslot") for _ in range(n)]
        r.bufs = [Buf(f"ws{i}") for i in range(n)]
        r.i = 0
        return r

    def load_w(ring, w2d, c0, ncol=128, kch=8, r0=0):
        i = ring.i % len(ring.aps)
        ring.i += 1
        src = w2d[r0:r0 + kch * 128, c0:c0 + ncol].rearrange("(k p) n -> p k n", p=128)
        kb.dma("pool", ring.aps[i][:, 0:kch, 0:ncol], src, writes=[ring.bufs[i]])
        return ring.aps[i], ring.bufs[i]

    def proj_fm(wt, wb, rhsT, rhsb, qi, ntok=2048, kch=8):
        qap, qb = quad(qi)
        ng = (ntok + 511) // 512
        for g in range(ng):
            n = min(512, ntok - g * 512)
            for k in range(kch):
                kb.op("pe", lambda e, g=g, k=k, n=n: e.matmul(qap[:, g * 512:g * 512 + n], wt[:, k, :],
                                                              rhsT[:, k, g * 512:g * 512 + n],
                                                              start=(k == 0), stop=(k == kch - 1)),
                      reads=[wb, rhsb], writes=[qb[g]], signal=(k == kch - 1))
        return qap, qb[:ng]

    def norm_to_T(al, l, gcol0, src_tiles, src_bufs, dstT, dstb, pbank=7):
        nt = len(src_tiles)
        ss = al.get([128, 16], F32, "ss")
        rs = al.get([128, 16], F32, "rs")
        junk = al.get([128, D], BF16, "junk")
        xn = [al.get([128, D], BF16, "xn") for _ in range(2)]
        gexp = al.get([128, 8, 128], F32, "gexp")
        SB, JB, GB = Buf("ss"), Buf("junk"), Buf("gexp")
        XN = [Buf("xn0"), Buf("xn1")]
        kb.op("dve", lambda e: e.tensor_copy(out=gexp, in_=sp_col(l, gcol0, 8).unsqueeze(2).to_broadcast([128, 8, 128])),
              reads=[CB], writes=[GB])
        for t in range(nt):
            kb.op("act", lambda e, t=t: e.activation(out=junk, in_=src_tiles[t], func=AF.Square,
                                                     accum_out=ss[:, t:t + 1]),
                  reads=[src_bufs[t]], writes=[JB, SB])
        kb.op("dve", lambda e: e.tensor_scalar(out=rs[:, 0:nt], in0=ss[:, 0:nt], scalar1=1.0 / D, scalar2=EPS,
                                               op0=ALU.mult, op1=ALU.add), reads=[SB], writes=[SB])
        kb.op("act", lambda e: e.activation(out=rs[:, 0:nt], in_=rs[:, 0:nt], func=AF.Sqrt), reads=[SB], writes=[SB])
        kb.op("dve", lambda e: e.reciprocal(out=rs[:, 0:nt], in_=rs[:, 0:nt]), reads=[SB], writes=[SB])
        pb = bank(pbank, BF16)
        for t in range(nt):
            kb.op("act", lambda e, t=t: e.activation(out=xn[t % 2], in_=src_tiles[t], func=AF.Copy,
                                                     scale=rs[:, t:t + 1]),
                  reads=[src_bufs[t], SB], writes=[XN[t % 2]])
            for c in range(8):
                kb.op("pe", lambda e, t=t, c=c: e.transpose(pb[:, c * 128:(c + 1) * 128], xn[t % 2][:, c * 128:(c + 1) * 128], ident),
                      reads=[XN[t % 2], CB], writes=[PB[pbank]], signal=(c == 7))
            kb.op("dve", lambda e, t=t: e.tensor_tensor(out=dstT[:, :, t * 128:(t + 1) * 128],
                                                        in0=pb.rearrange("p (a b) -> p a b", b=128), in1=gexp, op=ALU.mult),
                  reads=[PB[pbank], GB], writes=[dstb])

    def resid_proj(l_w2d, kch, lhsT, lhsb, al, rows0=0):
        wres = al.get([128, kch, D], BF16, "wres")
        WB = Buf("wres")
        for hlf in range(2):
            src = l_w2d[rows0:rows0 + kch * 128, hlf * 512:(hlf + 1) * 512].rearrange("(k p) n -> p k n", p=128)
            kb.dma("pool", wres[:, :, hlf * 512:(hlf + 1) * 512], src, writes=[WB])
        for t in range(NT):
            for hlf in range(2):
                bi = (t * 2 + hlf) % 8
                for k in range(kch):
                    kb.op("pe", lambda e, t=t, hlf=hlf, k=k, bi=bi: e.matmul(
                        bank(bi), lhsT[:, k, t * 128:(t + 1) * 128], wres[:, k, hlf * 512:(hlf + 1) * 512],
                        start=(k == 0), stop=(k == kch - 1)),
                        reads=[lhsb, WB], writes=[PB[bi]], signal=(k == kch - 1))
                kb.op("dve", lambda e, t=t, hlf=hlf, bi=bi: e.tensor_tensor(
                    out=x_sb[:, t, hlf * 512:(hlf + 1) * 512], in0=bank(bi), in1=x_sb[:, t, hlf * 512:(hlf + 1) * 512],
                    op=ALU.add), reads=[PB[bi], XT[t]], writes=[XT[t]])

    def dump(name, ap, shape, bufs):
        dt = nc.dram_tensor("dbg_" + name, list(shape), ap.dtype, kind="ExternalOutput").ap()
        kb.dma("sp", dt, ap, reads=bufs)
        dumps[name] = "dbg_" + name

    def finish():
        kb.barrier()
        return nc, dumps

    stop_req = stop
    for l in range(nlayers):
        stop = stop_req if l == nlayers - 1 else None
        al = Alloc(X_OFF + X_SZ, ARENA)
        hT = al.get([128, 8, S], BF16, "hT")
        HB = Buf("hT")
        mark = al.p
        norm_to_T(al, l, SP_GMIX, [x_sb[:, t, :] for t in range(NT)], XT, hT, HB)
        for t in range(NT):
            kb.dma("sp", xs_d[t * 128:(t + 1) * 128, :], x_sb[:, t, :], reads=[XT[t]])
        kb.barrier()
        if stop == "norm":
            dump("hT", hT, [128, 8, S], [HB])
            return finish()
        al.p = mark
        oT = al.get([128, 8, S], BF16, "oT")
        OB = Buf("oT")
        ring = weight_ring(al)
        ax = Alloc(X_OFF, X_OFF + X_SZ)
        wl = w_in[l]

        wab, wabb = load_w(ring, wl, OFF_AL, ncol=16)
        gall = al.get([128, 16, 8], F32, "gall")
        beta = al.get([128, 16, 8], F32, "beta")
        Gc = al.get([128, 16, 8], F32, "Gc")
        eG = al.get([128, 16, 8], F32, "eG")
        eD = al.get([128, 16, 8], F32, "eD")
        eL = al.get([128, 16, 8], F32, "eL")
        tz = al.get([128, 16, 8], F32, "tz")
        tz2 = al.get([128, 16, 8], F32, "tz2")
        AB = Buf("ab")
        pab = bank(0).rearrange("p (a b) -> p a b", b=32)
        for t in range(NT):
            for k in range(8):
                kb.op("pe", lambda e, t=t, k=k: e.matmul(pab[:, t, 0:16], hT[:, k, t * 128:(t + 1) * 128], wab[:, k, 0:16],
                                                         start=(k == 0), stop=(k == 7)),
                      reads=[HB, wabb], writes=[PB[0]], signal=(k == 7))
        kb.op("act", lambda e: e.activation(out=beta, in_=pab[:, :, 8:16], func=AF.Sigmoid), reads=[PB[0]], writes=[AB])
        kb.op("dve", lambda e: e.tensor_tensor(out=tz, in0=pab[:, :, 0:8],
                                               in1=sp_col(l, SP_DTB, 8).unsqueeze(1).to_broadcast([128, 16, 8]), op=ALU.add),
              reads=[PB[0], CB], writes=[AB])
        kb.op("act", lambda e: e.activation(out=tz2, in_=tz, func=AF.Abs), reads=[AB], writes=[AB])
        kb.op("act", lambda e: e.activation(out=tz2, in_=tz2, func=AF.Exp, scale=-1.0), reads=[AB], writes=[AB])
        kb.op("act", lambda e: e.activation(out=tz2, in_=tz2, func=AF.Ln, bias=1.0), reads=[AB], writes=[AB])
        kb.op("dve", lambda e: e.scalar_tensor_tensor(out=tz, in0=tz, scalar=0.0, in1=tz2, op0=ALU.max, op1=ALU.add),
              reads=[AB], writes=[AB])
        kb.op("act", lambda e: e.activation(out=eL[:, 0, :], in_=sp_col(l, SP_ALOG, 8), func=AF.Exp), reads=[CB], writes=[AB])
        kb.op("dve", lambda e: e.scalar_tensor_tensor(out=gall, in0=tz, scalar=-1.0,
                                                      in1=eL[:, 0:1, :].to_broadcast([128, 16, 8]),
                                                      op0=ALU.mult, op1=ALU.mult), reads=[AB], writes=[AB])
        g2 = gall.rearrange("p a b -> p (a b)")
        kb.op("pe", lambda e: e.matmul(bank(1)[:, 0:128], maskI, g2, start=True, stop=True), reads=[CB, AB], writes=[PB[1]])
        kb.op("pe", lambda e: e.matmul(bank(1)[:, 128:256], ones_f, g2, start=True, stop=True), reads=[CB, AB], writes=[PB[1]])
        kb.op("dve", lambda e: e.tensor_copy(out=Gc.rearrange("p a b -> p (a b)"), in_=bank(1)[:, 0:128]), reads=[PB[1]], writes=[AB])
        kb.op("act", lambda e: e.activation(out=eG.rearrange("p a b -> p (a b)"), in_=bank(1)[:, 0:128], func=AF.Exp),
              reads=[PB[1]], writes=[AB])
        kb.op("act", lambda e: e.activation(out=eL.rearrange("p a b -> p (a b)"), in_=bank(1)[:, 128:256], func=AF.Exp),
              reads=[PB[1]], writes=[AB])
        kb.op("dve", lambda e: e.tensor_tensor(out=tz.rearrange("p a b -> p (a b)"), in0=bank(1)[:, 128:256],
                                               in1=Gc.rearrange("p a b -> p (a b)"), op=ALU.subtract),
              reads=[PB[1], AB], writes=[AB])
        kb.op("act", lambda e: e.activation(out=eD, in_=tz, func=AF.Exp), reads=[AB], writes=[AB])
        if stop == "ab":
            dump("gall", gall, [128, 16, 8], [AB]); dump("beta", beta, [128, 16, 8], [AB])
            dump("eG", eG, [128, 16, 8], [AB]); dump("eD", eD, [128, 16, 8], [AB]); dump("eL", eL, [128, 16, 8], [AB])
            return finish()

        class Alloc2:
            def __init__(self, allocs):
                self.allocs = allocs

            def get(self, shape, dt, name=None):
                for a in self.allocs:
                    save = a.p
                    try:
                        return a.get(shape, dt, name)
                    except AssertionError:
                        a.p = save
                raise AssertionError(f"arena overflow (multi) {name}")

        dmark = al.p
        A2 = Alloc2([ax, al])

        def mk_slot(si):
            s = Ctx()
            s.qi = si
            s.banks = [si * 4 + j for j in range(4)]
            s.raw = [A2.get([128, S], BF16, "raw") for _ in range(3)]
            s.RAWB = [Buf(f"raw{si}{i}") for i in range(3)]
            s.cvt = A2.get([128, 1024], F32, "cvt")
            s.CVB = Buf("cvt")
            s.ut = A2.get([128, 4, 128], F32, "ut")
            s.wT = A2.get([128, 4, 128], BF16, "wT")
            s.pT = A2.get([128, 4, 128], BF16, "pT")
            s.qdT = A2.get([128, 4, 128], BF16, "qdT")
            s.kd = A2.get([128, 4, 128], BF16, "kd")
            s.SCB = Buf("scan_in")
            s.tm = [A2.get([128, 4, 128], BF16, "tm") for _ in range(6)]
            s.TMB = [Buf(f"tm{i}") for i in range(6)]
            s.fm = [A2.get([128, 4, 128], BF16, "fm") for _ in range(3)]
            s.FMB = [Buf(f"fm{i}") for i in range(3)]
            s.gtri = A2.get([128, 4, 128], F32, "gtri")
            s.dS = s.gtri
            s.decT = A2.get([128, 4, 128], F32, "decT")
            s.dI = A2.get([128, 4, 128], F32, "dI")
            s.DCB = Buf("dec")
            s.Am = [A2.get([128, 4, 128], BF16, "Am") for _ in range(2)]
            s.Bm = [A2.get([128, 4, 128], BF16, "Bm") for _ in range(2)]
            s.AMB = [Buf("A0"), Buf("A1")]
            s.BMB = [Buf("B0"), Buf("B1")]
            s.A0 = A2.get([128, 4, 128], BF16, "A0")
            s.P64 = A2.get([128, 4, 128], BF16, "P64")
            s.P128 = A2.get([128, 4, 128], BF16, "P128")
            s.A0B, s.PMB = Buf("A0"), Buf("Pm")
            s.Rf = A2.get([128, 4, 128], F32, "Rf")
            s.Rh = A2.get([128, 4, 128], BF16, "Rh")
            s.Rl = A2.get([128, 4, 128], BF16, "Rl")
            s.RFB, s.RHB, s.RLB = Buf("Rf"), Buf("Rh"), Buf("Rl")
            s.Sf = [A2.get([128, 128], F32, "Sf") for _ in range(2)]
            s.SFB = [Buf("S0"), Buf("S1")]
            s.Sb = A2.get([128, 128], BF16, "Sb")
            s.Ub = A2.get([128, 128], BF16, "Ub")
            s.SBB, s.UBB = Buf("Sb"), Buf("Ub")
            s.obuf = A2.get([128, 4, 128], F32, "obuf")
            s.onb = A2.get([128, 4, 128], BF16, "onb")
            s.OBB, s.ONB, s.OSB = Buf("obuf"), Buf("onb"), Buf("os8")
            s.os8 = A2.get([128, 8], F32, "os8")
            s.ss8 = A2.get([128, 8], F32, "ss8")
            s.sc = A2.get([128, 7, 4], F32, "sc")
            s.GT = Buf("grp_tmp")
            return s

        def proj_fm_g(wt, wb, rhsT, rhsb, qi, kch=8):
            qap, qb = quad(qi)
            for g in range(4):
                for k in range(kch):
                    kb.op("pe", lambda e: e.matmul(qap[:, g * 512:(g + 1) * 512], wt[:, k, :], rhsT[:, k, g * 512:(g + 1) * 512],
                                                   start=(k == 0), stop=(k == kch - 1)),
                          reads=[wb, rhsb], writes=[qb[g]], signal=(k == kch - 1))
                yield

        def v4(bi, dt=F32):
            return bank(bi, dt).rearrange("p (a b) -> p a b", b=128)

        def head_gen(h, s):
            b0, b1, b2, b3 = s.banks
            qap, qbs = quad(s.qi)
            raw, RAWB, cvt, CVB = s.raw, s.RAWB, s.cvt, s.CVB
            tm, TMB, fm, FMB = s.tm, s.TMB, s.fm, s.FMB
            Am, Bm, AMB, BMB = s.Am, s.Bm, s.AMB, s.BMB
            sc, ss8, GT, DCB, SCB = s.sc, s.ss8, s.GT, s.DCB, s.SCB
            for i, c0 in enumerate((OFF_QA, OFF_KA, OFF_VA)):
                wt, wb = load_w(ring, wl, c0 + h * 128)
                yield from proj_fm_g(wt, wb, hT, HB, s.qi)
                cwc = SP_CW + (i * 8 + h) * 4
                for half in range(2):
                    t0 = half * 1024
                    kb.op("dve", lambda e: e.tensor_scalar(out=cvt, in0=qap[:, t0:t0 + 1024], scalar1=sp_col(l, cwc + 3), scalar2=None,
                                                           op0=ALU.mult), reads=qbs + [CB], writes=[CVB])
                    for sh in (1, 2, 3):
                        lo = max(t0, sh)
                        kb.op("dve", lambda e: e.scalar_tensor_tensor(
                            out=cvt[:, lo - t0:1024], in0=qap[:, lo - sh:t0 + 1024 - sh], scalar=sp_col(l, cwc + 3 - sh),
                            in1=cvt[:, lo - t0:1024], op0=ALU.mult, op1=ALU.add), reads=qbs + [CB, CVB], writes=[CVB])
                    kb.op("act", lambda e: e.activation(out=raw[i][:, t0:t0 + 1024], in_=cvt, func=AF.Silu), reads=[CVB], writes=[RAWB[i]])
                    yield
            kb.op("pool", lambda e: e.memset(s.Sf[0], 0.0), writes=[s.SFB[0]])
            kb.op("pool", lambda e: e.memset(s.Sb, 0.0), writes=[s.SBB])
            scur = 0
            for g in range(4):
                t0 = g * 512
                c0 = g * 4
                pqk = v4(b0, BF16)
                pv = v4(b1, BF16)
                for i in range(2):
                    for cc in range(4):
                        kb.op("pe", lambda e: e.transpose(pqk[:, i * 4 + cc, :], raw[i][:, t0 + cc * 128:t0 + (cc + 1) * 128], ident),
                              reads=[RAWB[i], CB], writes=[PB[b0]], signal=(i == 1 and cc == 3))
                for cc in range(4):
                    kb.op("pe", lambda e: e.transpose(pv[:, cc, :], raw[2][:, t0 + cc * 128:t0 + (cc + 1) * 128], ident),
                          reads=[RAWB[2], CB], writes=[PB[b1]], signal=(cc == 3))
                yield
                sqv = cvt.rearrange("p (a b) -> p a b", b=128)
                kb.op("act", lambda e: e.activation(out=sqv, in_=pqk, func=AF.Square), reads=[PB[b0]], writes=[CVB])
                kb.op("dve", lambda e: e.tensor_reduce(out=ss8, in_=sqv, op=ALU.add, axis=AX.X), reads=[CVB], writes=[GT])
                kb.op("dve", lambda e: e.tensor_scalar(out=ss8, in0=ss8, scalar1=EPS, scalar2=None, op0=ALU.add), reads=[GT], writes=[GT])
                kb.op("act", lambda e: e.activation(out=ss8, in_=ss8, func=AF.Sqrt), reads=[GT], writes=[GT])
                kb.op("dve", lambda e: e.reciprocal(out=ss8, in_=ss8), reads=[GT], writes=[GT])
                yield
                bh = beta[:, c0:c0 + 4, h]
                egh = eG[:, c0:c0 + 4, h]
                edh = eD[:, c0:c0 + 4, h]
                kb.op("dve", lambda e: e.tensor_scalar(out=sc[:, 0, :], in0=ss8[:, 0:4], scalar1=float(128 ** -0.5), scalar2=None, op0=ALU.mult),
                      reads=[GT], writes=[GT])
                kb.op("dve", lambda e: e.tensor_tensor(out=sc[:, 1, :], in0=sc[:, 0, :], in1=egh, op=ALU.mult), reads=[GT, AB], writes=[GT])
                kb.op("dve", lambda e: e.tensor_copy(out=sc[:, 2, :], in_=ss8[:, 4:8]), reads=[GT], writes=[GT])
                kb.op("dve", lambda e: e.tensor_tensor(out=sc[:, 3, :], in0=ss8[:, 4:8], in1=bh, op=ALU.mult), reads=[GT, AB], writes=[GT])
                kb.op("dve", lambda e: e.tensor_tensor(out=sc[:, 4, :], in0=sc[:, 3, :], in1=egh, op=ALU.mult), reads=[GT, AB], writes=[GT])
                kb.op("dve", lambda e: e.tensor_tensor(out=sc[:, 5, :], in0=ss8[:, 4:8], in1=edh, op=ALU.mult), reads=[GT, AB], writes=[GT])
                kb.op("dve", lambda e: e.tensor_copy(out=sc[:, 6, :], in_=bh), reads=[GT, AB], writes=[GT])
                yield

                def scaled(dst, dstb, src, srcb, row):
                    kb.op("dve", lambda e: e.tensor_tensor(out=dst, in0=src, in1=sc[:, row, :].unsqueeze(2).to_broadcast([128, 4, 128]),
                                                           op=ALU.mult), reads=[srcb, GT], writes=[dstb])
                scaled(tm[0], TMB[0], pqk[:, 0:4, :], PB[b0], 0)
                scaled(tm[1], TMB[1], pqk[:, 0:4, :], PB[b0], 1)
                scaled(tm[2], TMB[2], pqk[:, 4:8, :], PB[b0], 2)
                yield
                scaled(tm[3], TMB[3], pqk[:, 4:8, :], PB[b0], 3)
                scaled(tm[4], TMB[4], pqk[:, 4:8, :], PB[b0], 4)
                scaled(s.kd, SCB, pqk[:, 4:8, :], PB[b0], 5)
                scaled(tm[5], TMB[5], pv[:, 0:4, :], PB[b1], 6)
                yield
                p4 = v4(b2, BF16)
                p5 = v4(b3, BF16)
                for (src, dstp, pbi, slot) in ((0, p4, b2, 0), (1, p4, b2, 4), (2, p5, b3, 0), (3, p5, b3, 4)):
                    for cc in range(4):
                        kb.op("pe", lambda e: e.transpose(dstp[:, slot + cc, :], tm[src][:, cc, :], ident),
                              reads=[TMB[src], CB], writes=[PB[pbi]], signal=(cc == 3))
                yield
                kb.op("act", lambda e: e.activation(out=fm[0], in_=p4[:, 0:4, :], func=AF.Copy), reads=[PB[b2]], writes=[FMB[0]])
                kb.op("act", lambda e: e.activation(out=s.qdT, in_=p4[:, 4:8, :], func=AF.Copy), reads=[PB[b2]], writes=[SCB])
                kb.op("dve", lambda e: e.tensor_copy(out=fm[1], in_=p5[:, 0:4, :]), reads=[PB[b3]], writes=[FMB[1]])
                kb.op("dve", lambda e: e.tensor_copy(out=fm[2], in_=p5[:, 4:8, :]), reads=[PB[b3]], writes=[FMB[2]])
                yield
                pB = v4(b0)
                pQ = v4(b1)
                for cc in range(4):
                    kb.op("pe", lambda e: e.matmul(pB[:, cc, :], fm[1][:, cc, :], fm[2][:, cc, :], start=True, stop=True),
                          reads=[FMB[1], FMB[2]], writes=[PB[b0]], signal=(cc == 3))
                for cc in range(4):
                    kb.op("pe", lambda e: e.matmul(pQ[:, cc, :], fm[1][:, cc, :], fm[0][:, cc, :], start=True, stop=True),
                          reads=[FMB[1], FMB[0]], writes=[PB[b1]], signal=(cc == 3))
                kb.op("dve", lambda e: e.tensor_tensor(out=s.gtri, in0=maskI4,
                                                       in1=gall[:, c0:c0 + 4, h].unsqueeze(2).to_broadcast([128, 4, 128]), op=ALU.mult),
                      reads=[CB, AB], writes=[DCB])
                kb.op("pe", lambda e: e.matmul(bank(b2), maskL, s.gtri.rearrange("p a b -> p (a b)"), start=True, stop=True),
                      reads=[CB, DCB], writes=[PB[b2]])
                yield
                kb.op("act", lambda e: e.activation(out=s.decT.rearrange("p a b -> p (a b)"), in_=bank(b2), func=AF.Exp),
                      reads=[PB[b2]], writes=[DCB])
                kb.op("pool", lambda e: e.tensor_tensor(out=s.dS, in0=s.decT, in1=maskS4, op=ALU.mult), reads=[DCB, CB], writes=[DCB])
                kb.op("pool", lambda e: e.tensor_tensor(out=s.dI, in0=s.decT, in1=maskI4, op=ALU.mult), reads=[DCB, CB], writes=[DCB])
                kb.op("dve", lambda e: e.scalar_tensor_tensor(out=Bm[0], in0=pB, scalar=-1.0, in1=s.dS, op0=ALU.mult, op1=ALU.mult),
                      reads=[PB[b0], DCB], writes=[BMB[0]])
                kb.op("dve", lambda e: e.tensor_tensor(out=s.pT, in0=pQ, in1=s.dI, op=ALU.mult),
                      reads=[PB[b1], DCB], writes=[SCB])
                yield
                p3 = v4(b3, BF16)
                for cc in range(4):
                    kb.op("pe", lambda e: e.transpose(p3[:, cc, :], Bm[0][:, cc, :], ident), reads=[BMB[0], CB], writes=[PB[b3]],
                          signal=(cc == 3))
                kb.op("act", lambda e: e.activation(out=s.A0, in_=p3[:, 0:4, :], func=AF.Copy), reads=[PB[b3]], writes=[s.A0B])
                kb.op("pool", lambda e: e.tensor_tensor(out=Bm[1], in0=Bm[0], in1=bd32, op=ALU.mult), reads=[BMB[0], CB], writes=[BMB[1]])
                kb.op("pool", lambda e: e.tensor_tensor(out=Am[1], in0=s.A0, in1=bd32, op=ALU.mult), reads=[s.A0B, CB], writes=[AMB[1]])
                kb.op("pool", lambda e: e.tensor_tensor(out=s.Rf, in0=Bm[1], in1=ident4, op=ALU.add), reads=[BMB[1], CB], writes=[s.RFB])
                kb.op("pool", lambda e: e.tensor_tensor(out=s.Rh, in0=Bm[1], in1=ident4, op=ALU.add), reads=[BMB[1], CB], writes=[s.RHB])
                kb.op("pool", lambda e: e.tensor_tensor(out=s.P64, in0=s.A0, in1=m64T, op=ALU.mult), reads=[s.A0B, CB], writes=[s.PMB])
                kb.op("pool", lambda e: e.tensor_tensor(out=s.P128, in0=s.A0, in1=m128T, op=ALU.mult), reads=[s.A0B, CB], writes=[s.PMB])
                yield
                cur = 1
                for lev in range(1, 5):
                    nxt = 1 - cur
                    pa, pbk, pr = v4(b0), v4(b1), v4(b2)
                    for cc in range(4):
                        kb.op("pe", lambda e: e.matmul(pa[:, cc, :], Bm[cur][:, cc, :], Am[cur][:, cc, :], start=True, stop=True),
                              reads=[BMB[cur], AMB[cur]], writes=[PB[b0]], signal=(cc == 3))
                    if lev < 4:
                        for cc in range(4):
                            kb.op("pe", lambda e: e.matmul(pbk[:, cc, :], Am[cur][:, cc, :], Bm[cur][:, cc, :], start=True, stop=True),
                                  reads=[BMB[cur], AMB[cur]], writes=[PB[b1]], signal=(cc == 3))
                    kb.op("act", lambda e: e.activation(out=Am[nxt], in_=pa, func=AF.Copy), reads=[PB[b0]], writes=[AMB[nxt]])
                    if lev < 4:
                        kb.op("dve", lambda e: e.tensor_copy(out=Bm[nxt], in_=pbk), reads=[PB[b1]], writes=[BMB[nxt]])
                    yield
                    for cc in range(4):
                        kb.op("pe", lambda e: e.matmul(pr[:, cc, :], Am[nxt][:, cc, :], s.Rh[:, cc, :], start=True, stop=True),
                              reads=[AMB[nxt], s.RHB], writes=[PB[b2]], signal=(cc == 3))
                    kb.op("dve", lambda e: e.tensor_tensor(out=s.Rf, in0=pr, in1=s.Rf, op=ALU.add),
                          reads=[PB[b2], s.RFB], writes=[s.RFB])
                    kb.op("act", lambda e: e.activation(out=s.Rh, in_=s.Rf, func=AF.Copy), reads=[s.RFB], writes=[s.RHB])
                    cur = nxt
                    yield
                for Pm in (s.P64, s.P128):
                    px, pd, py = v4(b0), v4(b1, BF16), v4(b2)
                    for cc in range(4):
                        kb.op("pe", lambda e: e.matmul(px[:, cc, :], Pm[:, cc, :], s.Rh[:, cc, :], start=True, stop=True),
                              reads=[s.PMB, s.RHB], writes=[PB[b0]], signal=(cc == 3))
                    for cc in range(4):
                        kb.op("pe", lambda e: e.transpose(pd[:, cc, :], s.Rh[:, cc, :], ident), reads=[s.RHB, CB], writes=[PB[b1]],
                              signal=(cc == 3))
                    kb.op("dve", lambda e: e.tensor_copy(out=Bm[0], in_=px), reads=[PB[b0]], writes=[BMB[0]])
                    kb.op("act", lambda e: e.activation(out=Am[0], in_=pd[:, 0:4, :], func=AF.Copy), reads=[PB[b1]], writes=[AMB[0]])
                    yield
                    for cc in range(4):
                        kb.op("pe", lambda e: e.matmul(py[:, cc, :], Am[0][:, cc, :], Bm[0][:, cc, :], start=True, stop=True),
                              reads=[AMB[0], BMB[0]], writes=[PB[b2]], signal=(cc == 3))
                    kb.op("dve", lambda e: e.tensor_tensor(out=s.Rf, in0=py, in1=s.Rf, op=ALU.add),
                          reads=[PB[b2], s.RFB], writes=[s.RFB])
                    kb.op("act", lambda e: e.activation(out=s.Rh, in_=s.Rf, func=AF.Copy), reads=[s.RFB], writes=[s.RHB])
                    yield
                kb.op("dve", lambda e: e.tensor_tensor(out=s.Rl, in0=s.Rf, in1=s.Rh, op=ALU.subtract), reads=[s.RFB, s.RHB], writes=[s.RLB])
                pu, pw = v4(b3), v4(b0)
                for cc in range(4):
                    kb.op("pe", lambda e: e.matmul(pu[:, cc, :], s.Rh[:, cc, :], tm[5][:, cc, :], start=True, stop=False),
                          reads=[s.RHB, TMB[5]], writes=[PB[b3]], signal=False)
                    kb.op("pe", lambda e: e.matmul(pu[:, cc, :], s.Rl[:, cc, :], tm[5][:, cc, :], start=False, stop=True),
                          reads=[s.RLB, TMB[5]], writes=[PB[b3]], signal=(cc == 3))
                for cc in range(4):
                    kb.op("pe", lambda e: e.matmul(pw[:, cc, :], tm[4][:, cc, :], s.Rh[:, cc, :], start=True, stop=False),
                          reads=[s.RHB, TMB[4]], writes=[PB[b0]], signal=False)
                    kb.op("pe", lambda e: e.matmul(pw[:, cc, :], tm[4][:, cc, :], s.Rl[:, cc, :], start=False, stop=True),
                          reads=[s.RLB, TMB[4]], writes=[PB[b0]], signal=(cc == 3))
                kb.op("act", lambda e: e.activation(out=s.ut, in_=pu, func=AF.Copy), reads=[PB[b3]], writes=[SCB])
                kb.op("dve", lambda e: e.tensor_copy(out=s.wT, in_=pw), reads=[PB[b0]], writes=[SCB])
                yield
                for cc in range(4):
                    c = c0 + cc
                    if c > 0:
                        kb.op("pe", lambda e: e.matmul(bank(b1)[:, 0:128], s.wT[:, cc, :], s.Sb, start=True, stop=True),
                              reads=[SCB, s.SBB], writes=[PB[b1]])
                        kb.op("dve", lambda e: e.tensor_tensor(out=s.Ub, in0=s.ut[:, cc, :], in1=bank(b1)[:, 0:128], op=ALU.subtract),
                              reads=[SCB, PB[b1]], writes=[s.UBB])
                    else:
                        kb.op("dve", lambda e: e.tensor_copy(out=s.Ub, in_=s.ut[:, cc, :]), reads=[SCB], writes=[s.UBB])
                    po = bank(b2)[:, cc * 128:(cc + 1) * 128]
                    if c > 0:
                        kb.op("pe", lambda e: e.matmul(po, s.qdT[:, cc, :], s.Sb, start=True, stop=False),
                              reads=[SCB, s.SBB], writes=[PB[b2]], signal=False)
                    kb.op("pe", lambda e: e.matmul(po, s.pT[:, cc, :], s.Ub, start=(c == 0), stop=True),
                          reads=[SCB, s.UBB], writes=[PB[b2]])
                    kb.op("pe", lambda e: e.matmul(bank(b3)[:, 0:128], s.kd[:, cc, :], s.Ub, start=True, stop=True),
                          reads=[SCB, s.UBB], writes=[PB[b3]])
                    snx = 1 - scur
                    kb.op("dve", lambda e: e.scalar_tensor_tensor(
                        out=s.Sf[snx], in0=s.Sf[scur], scalar=eL[:, c, h:h + 1], in1=bank(b3)[:, 0:128], op0=ALU.mult, op1=ALU.add),
                        reads=[s.SFB[scur], PB[b3], AB], writes=[s.SFB[snx]])
                    kb.op("act", lambda e: e.activation(out=s.Sb, in_=s.Sf[snx], func=AF.Copy), reads=[s.SFB[snx]], writes=[s.SBB])
                    scur = snx
                    yield
                kb.op("act", lambda e: e.activation(out=s.obuf, in_=v4(b2), func=AF.Copy), reads=[PB[b2]], writes=[s.OBB])
                sqo = cvt[:, 0:512].rearrange("p (a b) -> p a b", b=128)
                kb.op("act", lambda e: e.activation(out=sqo, in_=s.obuf, func=AF.Square), reads=[s.OBB], writes=[CVB])
                kb.op("dve", lambda e: e.tensor_reduce(out=s.os8[:, 0:4], in_=sqo, op=ALU.add, axis=AX.X), reads=[CVB], writes=[s.OSB])
                kb.op("dve", lambda e: e.tensor_scalar(out=s.os8[:, 0:4], in0=s.os8[:, 0:4], scalar1=1.0 / 128, scalar2=EPS,
                                                       op0=ALU.mult, op1=ALU.add), reads=[s.OSB], writes=[s.OSB])
                kb.op("act", lambda e: e.activation(out=s.os8[:, 0:4], in_=s.os8[:, 0:4], func=AF.Sqrt), reads=[s.OSB], writes=[s.OSB])
                kb.op("dve", lambda e: e.reciprocal(out=s.os8[:, 0:4], in_=s.os8[:, 0:4]), reads=[s.OSB], writes=[s.OSB])
                kb.op("dve", lambda e: e.tensor_tensor(out=s.onb, in0=s.obuf, in1=s.os8[:, 0:4].unsqueeze(2).to_broadcast([128, 4, 128]),
                                                       op=ALU.mult), reads=[s.OBB, s.OSB], writes=[s.ONB])
                yield
                p3 = v4(b1, BF16)
                for c4 in range(4):
                    kb.op("pe", lambda e: e.transpose(p3[:, c4, :], s.onb[:, c4, :], ident), reads=[s.ONB, CB], writes=[PB[b1]],
                          signal=(c4 == 3))
                kb.op("act", lambda e: e.activation(out=oT[:, h, g * 512:(g + 1) * 512],
                                                    in_=bank(b1, BF16)[:, 0:512], func=AF.Copy, scale=sp_col(l, SP_HN)),
                      reads=[PB[b1], CB], writes=[OTB[h]])
                yield
            wt, wb = load_w(ring, wl, OFF_ZA + h * 128)
            yield from proj_fm_g(wt, wb, hT, HB, s.qi)
            kb.op("act", lambda e: e.activation(out=raw[0], in_=qap, func=AF.Silu), reads=qbs, writes=[RAWB[0]])
            kb.op("dve", lambda e: e.tensor_tensor(out=oT[:, h, :], in0=oT[:, h, :], in1=raw[0], op=ALU.mult),
                  reads=[RAWB[0], OTB[h]], writes=[OTB[h]])
            yield

        def slot_gen(s, heads):
            for h in heads:
                yield from head_gen(h, s)

        OTB = [Buf(f"oT{h}") for h in range(8)]
        slots = [mk_slot(0), mk_slot(1)]
        kb.fresh = dbg.get("fresh", False)
        nheads = dbg.get("nheads", 8 if stop != "head0" else 2)
        if dbg.get("single", True):
            gens = [slot_gen(slots[dbg.get("slot", 0)], list(range(nheads)))]
        else:
            gens = [slot_gen(slots[0], list(range(0, nheads, 2))), slot_gen(slots[1], list(range(1, nheads, 2)))]
        lead = dbg.get("lead", 40)
        for _ in range(lead):
            try:
                next(gens[0])
            except StopIteration:
                break
        active = list(gens)
        while active:
            for gsel in list(active):
                try:
                    next(gsel)
                except StopIteration:
                    active.remove(gsel)
        kb.fresh = False
        kb.barrier()
        if stop in ("head0", "delta"):
            dump("oT", oT, [128, 8, S], [OB])
            return finish()
        al.p = dmark
        yT = al.get([128, 8, S], BF16, "yT")
        YB = Buf("yT")
        raw = [arena[:, X_OFF + i * 4096:X_OFF + (i + 1) * 4096].bitcast(BF16) for i in range(3)]
        RAWB = [Buf(f"rawy{i}") for i in range(3)]


        sg = raw[1]
        SGB = RAWB[1]

        def ypass(gate_off, wmat, first):
            for m in range(8):
                wt, wb = load_w(ring, wl, gate_off + m * 128)
                qap, qbs = proj_fm(wt, wb, hT, HB, 0)
                kb.op("act", lambda e: e.activation(out=sg, in_=qap, func=AF.Sigmoid), reads=qbs, writes=[SGB])
                wt2, wb2 = load_w(ring, wmat, m * 128)
                qap2, qbs2 = proj_fm(wt2, wb2, oT, OB, 1)
                if first:
                    kb.op("dve", lambda e, m=m: e.tensor_tensor(out=yT[:, m, :], in0=qap2, in1=sg, op=ALU.mult),
                          reads=qbs2 + [SGB], writes=[YB])
                else:
                    kb.op("dve", lambda e, m=m: e.tensor_tensor(out=raw[2], in0=qap2, in1=sg, op=ALU.mult),
                          reads=qbs2 + [SGB], writes=[RAWB[2]])
                    kb.op("pool", lambda e, m=m: e.tensor_tensor(out=yT[:, m, :], in0=yT[:, m, :], in1=raw[2], op=ALU.add),
                          reads=[RAWB[2], YB], writes=[YB])

        ypass(OFF_GA, w_a[l], True)
        if stop == "ya":
            dump("yT", yT, [128, 8, S], [YB])
            return finish()

        kb.barrier()
        ax = Alloc(X_OFF, X_OFF + X_SZ)
        raw = [ax.get([128, S], BF16, f"raw{i}") for i in range(3)]
        qbT, kbT = raw[0], raw[1]
        QBB, KBB = RAWB[0], RAWB[1]
        sg = raw[1]
        vbt = ax.get([128, 16, 128], BF16, "vbt")
        VBB = Buf("vbt")
        biasb = [ax.get([128, 5, 128], F32, f"bias{i}") for i in range(2)]
        BIB = [Buf("bias0"), Buf("bias1")]
        stmp = [ax.get([128, 5, 128], F32, f"stmp{i}") for i in range(2)]
        STB = [Buf("st0"), Buf("st1")]
        PTt = [ax.get([128, 5, 128], BF16, f"PT{i}") for i in range(2)]
        PTB = [Buf("PT0"), Buf("PT1")]
        rcp = ax.get([128, 512], F32, "rcp")
        RCB = Buf("rcp")
        it = 0
        for hp in range(8):
            wt, wb = load_w(ring, wl, OFF_QB + hp * 128)
            qap, qbs = proj_fm(wt, wb, hT, HB, 0)
            kb.op("act", lambda e: e.activation(out=qbT, in_=qap, func=AF.Copy, scale=0.125), reads=qbs, writes=[QBB])
            wt, wb = load_w(ring, wl, OFF_KB + hp * 128)
            qap, qbs = proj_fm(wt, wb, hT, HB, 1)
            kb.op("dve", lambda e: e.tensor_copy(out=kbT, in_=qap), reads=qbs, writes=[KBB])
            wt, wb = load_w(ring, wl, OFF_VB + hp * 128)
            for t in range(NT):
                bi = t // 4
                for k in range(8):
                    kb.op("pe", lambda e, t=t, k=k, bi=bi: e.matmul(bank(bi)[:, (t % 4) * 128:(t % 4 + 1) * 128],
                                                                    hT[:, k, t * 128:(t + 1) * 128], wt[:, k, :],
                                                                    start=(k == 0), stop=(k == 7)),
                          reads=[HB, wb], writes=[PB[bi]], signal=(k == 7))
                if t % 4 == 3:
                    kb.op("act", lambda e, bi=bi: e.activation(out=vbt[:, bi * 4:(bi + 1) * 4, :],
                                                               in_=bank(bi).rearrange("p (a b) -> p a b", b=128), func=AF.Copy),
                          reads=[PB[bi]], writes=[VBB])
            for hh in range(2):
                head = hp * 2 + hh
                base = hh * 64
                kb.dma("sp", biasb[hh].rearrange("p a b -> p (a b)"), bias_d[head], writes=[BIB[hh]])
                kb.op("pool", lambda e, hh=hh: e.memset(biasb[hh][0:64, 0, 64:128], -30000.0), writes=[BIB[hh]])
                kb.op("pool", lambda e, hh=hh: e.memset(biasb[hh][64:128, 4, 0:64], -30000.0), writes=[BIB[hh]])
                for qg in range(4):
                    po = bank(4 + hh * 2)
                    psm = bank(5 + hh * 2)
                    for qq in range(4):
                        qb_i = qg * 4 + qq
                        r0 = max(0, 4 - qb_i)
                        sl = it % 2
                        it += 1
                        pst = psum[:, sl * 1024: sl * 1024 + 640].rearrange("p (a b) -> p a b", b=128)
                        pstb = [PB[sl * 2], PB[sl * 2 + 1]]
                        for r in range(r0, 5):
                            kblk = qb_i - 4 + r
                            kb.op("pe", lambda e, r=r, kblk=kblk, qb_i=qb_i: e.matmul(
                                pst[:, r, :], kbT[base:base + 64, kblk * 128:(kblk + 1) * 128],
                                qbT[base:base + 64, qb_i * 128:(qb_i + 1) * 128], start=True, stop=True),
                                reads=[KBB, QBB], writes=pstb, signal=(r == 4))
                        kb.op("dve", lambda e, sl=sl, r0=r0, hh=hh: e.tensor_tensor(out=stmp[sl][:, r0:5, :], in0=pst[:, r0:5, :],
                                                                                in1=biasb[hh][:, r0:5, :], op=ALU.add),
                              reads=pstb + [BIB[hh]], writes=[STB[sl]])
                        kb.op("act", lambda e, sl=sl, r0=r0: e.activation(out=PTt[sl][:, r0:5, :], in_=stmp[sl][:, r0:5, :], func=AF.Exp),
                              reads=[STB[sl]], writes=[PTB[sl]])
                        for r in range(r0, 5):
                            kblk = qb_i - 4 + r
                            kb.op("pe", lambda e, r=r, kblk=kblk, qq=qq, sl=sl: e.matmul(
                                po[:, qq * 128:(qq + 1) * 128], vbt[:, kblk, :], PTt[sl][:, r, :], start=(r == r0), stop=(r == 4)),
                                reads=[VBB, PTB[sl]], writes=[PB[4 + hh * 2]], signal=(r == 4))
                        for r in range(r0, 5):
                            kb.op("pe", lambda e, r=r, qq=qq, sl=sl: e.matmul(
                                psm[:, qq * 128:(qq + 1) * 128], ones_b, PTt[sl][:, r, :], start=(r == r0), stop=(r == 4)),
                                reads=[CB, PTB[sl]], writes=[PB[5 + hh * 2]], signal=(r == 4))
                    kb.op("dve", lambda e: e.reciprocal(out=rcp, in_=psm), reads=[PB[5 + hh * 2]], writes=[RCB])
                    kb.op("dve", lambda e, qg=qg, hp=hp, base=base: e.tensor_tensor(
                        out=oT[base:base + 64, hp, qg * 512:(qg + 1) * 512], in0=po[base:base + 64, :], in1=rcp[base:base + 64, :],
                        op=ALU.mult), reads=[PB[4 + hh * 2], RCB], writes=[OB])
            if stop == "band0":
                dump("oT", oT, [128, 8, S], [OB])
                return finish()
        if stop == "band":
            dump("oT", oT, [128, 8, S], [OB])
            return finish()
        raw2_keep = raw[2]
        ypass(OFF_GB, w_b[l], False)
        if stop == "yb":
            dump("yT", yT, [128, 8, S], [YB])
            return finish()

        kb.barrier()
        for t in range(NT):
            kb.dma("sp", x_sb[:, t, :], xs_d[t * 128:(t + 1) * 128, :], writes=[XT[t]])
        al.p = mark
        alh = Alloc(X_OFF + X_SZ, X_OFF + X_SZ + 32 * 1024)
        resid_proj(w_o[l], 8, yT, YB, alh)
        kb.barrier()
        if stop == "mix":
            dump("x", x_sb, [128, NT, D], XT)
            return finish()

        al = Alloc(X_OFF + X_SZ, ARENA)
        hT = al.get([128, 8, S], BF16, "hT")
        oT = al.get([128, 8, S], BF16, "oT")
        HB, OB = Buf("hT"), Buf("oT")
        memT = al.get([128, 8, 256], BF16, "memT")
        MTB = Buf("memT")
        MFB = [Buf("memf0"), Buf("memf1")]
        kT = al.get([128, 8, 256], BF16, "kT")
        vm = al.get([128, 2, D], BF16, "vm")
        KTB, VMB = Buf("kT"), Buf("vm")
        qT = al.get([128, 2, S], BF16, "qT")
        QTB = Buf("qT")
        PTm = [al.get([128, 2, 512], BF16, f"PTm{i}") for i in range(2)]
        PMB = [Buf("PTm0"), Buf("PTm1")]
        rcp = al.get([128, 512], F32, "rcp")
        RCB = Buf("rcp")
        ring = weight_ring(al, n=3)
        ring5 = weight_ring(al, n=1, ncol=512)
        mark2 = al.p
        memf = al.get([128, 2, D], F32, "memf")
        for mt in range(2):
            kb.dma("sp", memf[:, mt, :], mem_d[mt * 128:(mt + 1) * 128, :], writes=[MFB[mt]])
        norm_to_T(al, l, SP_GMEM, [memf[:, 0, :], memf[:, 1, :]], MFB, memT, MTB)
        al.p = mark2
        norm_to_T(al, l, SP_GXA, [x_sb[:, t, :] for t in range(NT)], XT, hT, HB)
        al.p = mark2
        for c in range(8):
            wt, wb = load_w(ring, w_mkv[l], c * 128)
            qap, qbs = proj_fm(wt, wb, memT, MTB, 0, ntok=256)
            kb.op("act", lambda e, c=c: e.activation(out=kT[:, c, :], in_=qap[:, 0:256], func=AF.Copy), reads=qbs, writes=[KTB])
        for hlf in range(2):
            wt, wb = load_w(ring5, w_mkv[l], D + hlf * 512, ncol=512)
            for mt in range(2):
                bi = 4 + hlf * 2 + mt
                for k in range(8):
                    kb.op("pe", lambda e, mt=mt, k=k, bi=bi: e.matmul(bank(bi), memT[:, k, mt * 128:(mt + 1) * 128], wt[:, k, :],
                                                                      start=(k == 0), stop=(k == 7)),
                          reads=[MTB, wb], writes=[PB[bi]], signal=(k == 7))
                kb.op("dve", lambda e, mt=mt, hlf=hlf, bi=bi: e.tensor_copy(out=vm[:, mt, hlf * 512:(hlf + 1) * 512], in_=bank(bi)),
                      reads=[PB[bi]], writes=[VMB])
        it = 0
        for hd in range(4):
            for ci in range(2):
                wt, wb = load_w(ring, w_mq[l], (hd * 2 + ci) * 128)
                qap, qbs = proj_fm(wt, wb, hT, HB, ci)
                kb.op("act", lambda e, ci=ci: e.activation(out=qT[:, ci, :], in_=qap, func=AF.Copy, scale=1.0 / 16), reads=qbs, writes=[QTB])
            for g in range(4):
                sl = it % 2
                it += 1
                for mt in range(2):
                    for ci in range(2):
                        kb.op("pe", lambda e, mt=mt, ci=ci, g=g: e.matmul(
                            bank(mt), kT[:, hd * 2 + ci, mt * 128:(mt + 1) * 128], qT[:, ci, g * 512:(g + 1) * 512],
                            start=(ci == 0), stop=(ci == 1)), reads=[KTB, QTB], writes=[PB[mt]], signal=(ci == 1))
                    kb.op("act", lambda e, mt=mt, sl=sl: e.activation(out=PTm[sl][:, mt, :], in_=bank(mt), func=AF.Exp),
                          reads=[PB[mt]], writes=[PMB[sl]])
                for mt in range(2):
                    kb.op("pe", lambda e, mt=mt, sl=sl: e.matmul(bank(2), ones_b, PTm[sl][:, mt, :], start=(mt == 0), stop=(mt == 1)),
                          reads=[CB, PMB[sl]], writes=[PB[2]], signal=(mt == 1))
                kb.op("dve", lambda e: e.reciprocal(out=rcp, in_=bank(2)), reads=[PB[2]], writes=[RCB])
                for ci in range(2):
                    for mt in range(2):
                        kb.op("pe", lambda e, mt=mt, ci=ci, sl=sl: e.matmul(
                            bank(3 + ci), vm[:, mt, (hd * 2 + ci) * 128:(hd * 2 + ci + 1) * 128], PTm[sl][:, mt, :],
                            start=(mt == 0), stop=(mt == 1)), reads=[VMB, PMB[sl]], writes=[PB[3 + ci]], signal=(mt == 1))
                    kb.op("dve", lambda e, ci=ci, g=g: e.tensor_tensor(out=oT[:, hd * 2 + ci, g * 512:(g + 1) * 512], in0=bank(3 + ci),
                                                                   in1=rcp, op=ALU.mult), reads=[PB[3 + ci], RCB], writes=[OB])
        al.p = mark2
        resid_proj(w_mo[l], 8, oT, OB, al)
        kb.barrier()
        if stop == "xa":
            dump("x", x_sb, [128, NT, D], XT)
            return finish()

        al = Alloc(X_OFF + X_SZ, ARENA)
        hT = al.get([128, 8, S], BF16, "hT")
        HB = Buf("hT")
        act = al.get([128, 11, S], BF16, "act")
        ACB = Buf("act")
        sgt = al.get([128, S], BF16, "sgt")
        SGB = Buf("sgt")
        ring = weight_ring(al)
        mark3 = al.p
        norm_to_T(al, l, SP_GFFN, [x_sb[:, t, :] for t in range(NT)], XT, hT, HB)
        for hf in range(2):
            for jj in range(11):
                j = hf * 11 + jj
                wt, wb = load_w(ring, w_gu[l], j * 128)
                qap, qbs = proj_fm(wt, wb, hT, HB, 0)
                kb.op("act", lambda e: e.activation(out=sgt, in_=qap, func=AF.Silu), reads=qbs, writes=[SGB])
                wt, wb = load_w(ring, w_gu[l], DFF + j * 128)
                qap2, qbs2 = proj_fm(wt, wb, hT, HB, 1)
                kb.op("dve", lambda e, jj=jj: e.tensor_tensor(out=act[:, jj, :], in0=qap2, in1=sgt, op=ALU.mult),
                      reads=qbs2 + [SGB], writes=[ACB])
            al.p = mark3
            resid_proj(w_dn[l], 11, act, ACB, al, rows0=hf * 11 * 128)
        kb.barrier()
        if stop == "ffn":
            dump("x", x_sb, [128, NT, D], XT)
            return finish()

    al = Alloc(X_OFF + X_SZ, ARENA)
    gf = al.get([128, D], F32, "gf")
    GFB = Buf("gf")
    kb.dma("sp", gf, gfin_d, writes=[GFB])
    ss = al.get([128, 16], F32, "ss")
    junk = al.get([128, D], BF16, "junk")
    yo = [al.get([128, D], F32, f"yo{i}") for i in range(2)]
    YOB = [Buf("yo0"), Buf("yo1")]
    SB, JB = Buf("ss"), Buf("junk")
    for t in range(NT):
        kb.op("act", lambda e, t=t: e.activation(out=junk, in_=x_sb[:, t, :], func=AF.Square, accum_out=ss[:, t:t + 1]),
              reads=[XT[t]], writes=[JB, SB])
    kb.op("dve", lambda e: e.tensor_scalar(out=ss, in0=ss, scalar1=1.0 / D, scalar2=EPS, op0=ALU.mult, op1=ALU.add), reads=[SB], writes=[SB])
    kb.op("act", lambda e: e.activation(out=ss, in_=ss, func=AF.Sqrt), reads=[SB], writes=[SB])
    kb.op("dve", lambda e: e.reciprocal(out=ss, in_=ss), reads=[SB], writes=[SB])
    for t in range(NT):
        kb.op("dve", lambda e, t=t: e.scalar_tensor_tensor(out=yo[t % 2], in0=x_sb[:, t, :], scalar=ss[:, t:t + 1], in1=gf,
                                                           op0=ALU.mult, op1=ALU.mult), reads=[XT[t], SB, GFB], writes=[YOB[t % 2]])
        kb.dma("sp", out_d[t * 128:(t + 1) * 128, :], yo[t % 2], reads=[YOB[t % 2]])
    return finish()


def _host_layout(inputs):
    f = np.float32
    g = lambda k: np.ascontiguousarray(np.asarray(inputs[k], dtype=f))
    smallp = np.zeros((DEPTH, 128, SP_N), f)
    for l in range(DEPTH):
        for c0, key in ((SP_GMIX, "norm_mix"), (SP_GXA, "norm_xattn"), (SP_GMEM, "norm_mem"), (SP_GFFN, "norm_ffn")):
            smallp[l, :, c0:c0 + 8] = g(key)[l].reshape(8, 128).T
        cw = g("conv_w")[l]
        smallp[l, :, SP_CW:SP_CW + 96] = cw.T.reshape(24, 128, 4).transpose(1, 0, 2).reshape(128, 96)
        smallp[l, :, SP_HN] = g("head_norm")[l]
        smallp[l, :, SP_ALOG:SP_ALOG + 8] = np.broadcast_to(g("a_log")[l][None, :], (128, 8))
        smallp[l, :, SP_DTB:SP_DTB + 8] = np.broadcast_to(g("dt_bias")[l][None, :], (128, 8))
    gfinal = np.ascontiguousarray(np.broadcast_to(g("norm_final")[None, :], (128, D)))
    kin = np.arange(128)[:, None, None]
    r = np.arange(5)[None, :, None]
    qin = np.arange(128)[None, None, :]
    idx = np.clip(qin - kin + 128 * (4 - r), -63, 256) + 63
    biasq = np.ascontiguousarray(g("rel_bias")[:, idx].reshape(16, 128, 640))
    shared = {k: g(k) for k in ("w_in", "w_a_out", "w_b_out", "w_o", "w_mq", "w_mkv", "w_mo", "w_gate_up", "w_down")}
    pp = np.arange(128)[:, None]
    ff = np.arange(128)[None, :]
    cmask = np.stack([(pp // 32 == ff // 32), (ff % 64 < 32) & (pp % 64 >= 32) & (pp // 64 == ff // 64),
                      (ff < 64) & (pp >= 64)], axis=1).astype(f)
    shared.update(smallp=smallp, gfinal=gfinal, biasq=biasq, cmask=np.ascontiguousarray(cmask))
    return shared


_CACHE = {}


def kernel(**inputs):
    shared = _host_layout(inputs)
    x = np.ascontiguousarray(np.asarray(inputs["x"], dtype=np.float32))
    mem = np.ascontiguousarray(np.asarray(inputs["mem"], dtype=np.float32))
    if "nc" not in _CACHE:
        _CACHE["nc"] = build_program()[0]
    nc = _CACHE["nc"]
    in_maps = []
    for b in range(8):
        m = dict(shared)
        m["x"] = x[b]
        m["mem"] = mem[b]
        in_maps.append(m)
    res = run_bass_kernel_spmd(nc, in_maps, core_ids=list(range(8)))
    return np.stack([np.asarray(r["out"], dtype=np.float32) for r in res.results], axis=0)
```

```python
import numpy as np
import concourse.bass as bass
import concourse.mybir as mybir
from concourse.bass_utils import run_bass_kernel_spmd

F32 = mybir.dt.float32
BF16 = mybir.dt.bfloat16
U8 = mybir.dt.uint8
AF = mybir.ActivationFunctionType
ALU = mybir.AluOpType
AX = mybir.AxisListType

S = 2048
D = 1024
NT = 16
DEPTH = 2
EPS = 1e-6
NIN = 9232
OFF_QA, OFF_KA, OFF_VA, OFF_ZA, OFF_AL, OFF_QB, OFF_KB, OFF_VB, OFF_GA, OFF_GB = (
    0, 1024, 2048, 3072, 4096, 4112, 5136, 6160, 7184, 8208)
DFF = 2816
SP_GMIX, SP_GXA, SP_GMEM, SP_GFFN, SP_CW, SP_HN, SP_ALOG, SP_DTB = 0, 8, 16, 24, 32, 128, 129, 137
SP_N = 145
ARENA = 200 * 1024


class Buf:
    __slots__ = ("name", "w", "r")

    def __init__(self, name):
        self.name = name
        self.w = None
        self.r = []


class Eng:
    def __init__(self, name, handle, sem):
        self.name = name
        self.h = handle
        self.sem = sem
        self.count = 0
        self.seen = {}
        self.pending = False


class KB:
    def __init__(self, nc, same_engine_sync=True, dq_n=20):
        self.nc = nc
        self.same = same_engine_sync
        self.sems = {}
        self.E = {}
        for nm, h in (("pe", nc.tensor), ("act", nc.scalar), ("dve", nc.vector), ("pool", nc.gpsimd)):
            s = nc.semaphore("sem_" + nm).__enter__()
            self.E[nm] = Eng(nm, h, s)
        self.E["sp"] = Eng("sp", nc.sync, None)
        self.dq = {}
        for q in ("sp", "pool"):
            sl = [nc.semaphore(f"dq_{q}{i}").__enter__() for i in range(dq_n)]
            self.dq[q] = {"sems": sl, "cum": [0] * len(sl), "i": 0}
        self.all_dma_tokens = []
        self.fresh = False
        self.fresh_sems = []

    def _wait(self, e, tok):
        if tok is None:
            return
        sem, val, owner = tok
        key = id(sem)
        if e.seen.get(key, 0) >= val:
            return
        if owner is not None:
            oe = self.E[owner]
            if owner == e.name:
                if e.name == "pe" or not self.same:
                    return
            assert oe.count >= val, f"wait on unsignalled future {owner} {val} > {oe.count}"
        e.h.wait_ge(sem, val)
        e.seen[key] = val

    def _deps(self, e, reads, writes):
        for b in reads:
            self._wait(e, b.w)
        for b in writes:
            self._wait(e, b.w)
            for t in b.r:
                self._wait(e, t)

    def _commit(self, tok, reads, writes):
        for b in reads:
            b.r.append(tok)
            if len(b.r) > 24:
                b.r = b.r[-24:] if False else self._compact(b.r)
        for b in writes:
            b.w = tok
            b.r = []

    @staticmethod
    def _compact(toks):
        best = {}
        for t in toks:
            k = id(t[0])
            if k not in best or best[k][1] < t[1]:
                best[k] = t
        return list(best.values())

    def op(self, eng, fn, reads=(), writes=(), signal=True):
        e = self.E[eng]
        self._deps(e, reads, writes)
        ins = fn(e.h)
        tok = (e.sem, e.count + 1, eng)
        if signal:
            ins.then_inc(e.sem, 1)
            e.count += 1
            e.pending = False
        else:
            e.pending = True
        self._commit(tok, reads, writes)
        return ins

    def dma(self, q, out, in_, reads=(), writes=()):
        e = self.E[q]
        if self.fresh:
            sem = self.nc.semaphore(f"dqf_{len(self.fresh_sems)}").__enter__()
            self.fresh_sems.append(sem)
            self._deps(e, reads, writes)
            e.h.dma_start(out=out, in_=in_).then_inc(sem, 16)
            tok = (sem, 16, None)
            self.all_dma_tokens.append(tok)
            self._commit(tok, reads, writes)
            return tok
        d = self.dq[q]
        i = d["i"] % len(d["sems"])
        d["i"] += 1
        sem = d["sems"][i]
        self._wait(e, (sem, d["cum"][i], None))
        self._deps(e, reads, writes)
        d["cum"][i] += 16
        e.h.dma_start(out=out, in_=in_).then_inc(sem, 16)
        tok = (sem, d["cum"][i], None)
        self._commit(tok, reads, writes)
        return tok

    def barrier(self):
        toks = []
        for nm in ("pe", "act", "dve", "pool"):
            oe = self.E[nm]
            assert not oe.pending, f"{nm} has unsignalled trailing instruction at barrier"
            if oe.count:
                toks.append((oe.sem, oe.count, nm))
        for q in ("sp", "pool"):
            d = self.dq[q]
            for s, c in zip(d["sems"], d["cum"]):
                if c:
                    toks.append((s, c, None))
        toks.extend(self.all_dma_tokens)
        self.all_dma_tokens = []
        for nm, e in self.E.items():
            for t in toks:
                if t[2] == nm:
                    continue
                self._wait(e, t)


def build_program(dbg=None):
    dbg = dbg or {}
    stop = dbg.get("stop")
    nlayers = dbg.get("layers", DEPTH)
    nc = bass.Bass("TRN2", target_bir_lowering=False)
    kb = KB(nc, same_engine_sync=dbg.get("same", True), dq_n=dbg.get("dq_n", 20))

    def din(name, shape):
        return nc.dram_tensor(name, list(shape), F32, kind="ExternalInput").ap()

    x_d = din("x", [S, D])
    mem_d = din("mem", [256, D])
    w_in = din("w_in", [DEPTH, D, NIN])
    w_a = din("w_a_out", [DEPTH, D, D])
    w_b = din("w_b_out", [DEPTH, D, D])
    w_o = din("w_o", [DEPTH, D, D])
    w_mq = din("w_mq", [DEPTH, D, D])
    w_mkv = din("w_mkv", [DEPTH, D, 2 * D])
    w_mo = din("w_mo", [DEPTH, D, D])
    w_gu = din("w_gate_up", [DEPTH, D, 2 * DFF])
    w_dn = din("w_down", [DEPTH, DFF, D])
    smallp_d = din("smallp", [DEPTH, 128, SP_N])
    gfin_d = din("gfinal", [128, D])
    bias_d = din("biasq", [16, 128, 640])
    cmask_d = din("cmask", [128, 3, 128])
    out_d = nc.dram_tensor("out", [S, D], F32, kind="ExternalOutput").ap()
    xs_d = nc.dram_tensor("x_spill", [S, D], F32, kind="Internal").ap()
    dumps = {}

    arena = nc.sbuf_tensor("arena", [128, ARENA], U8).__enter__()
    psum = nc.psum_tensor("psum", [128, 4096], F32).__enter__()
    PB = [Buf(f"pb{i}") for i in range(8)]

    def bank(i, dt=F32, n=None):
        ap = psum[:, i * 512:(i + 1) * 512]
        if dt == BF16:
            ap = ap.bitcast(BF16)
        return ap

    def quad(qi):
        return psum[:, qi * 2048:(qi + 1) * 2048], PB[qi * 4:(qi + 1) * 4]

    class Alloc:
        def __init__(self, lo, hi):
            self.lo, self.hi, self.p = lo, hi, lo

        def get(self, shape, dt, name=None):
            esz = 4 if dt == F32 else 2
            n = int(np.prod(shape[1:])) * esz
            off = (self.p + 63) // 64 * 64
            assert off + n <= self.hi, f"arena overflow {name} {off + n} > {self.hi}"
            self.p = off + n
            ap = arena[:, off:off + n].bitcast(dt)
            if len(shape) == 3:
                ap = ap.rearrange("p (a b) -> p a b", b=shape[2])
            elif len(shape) == 4:
                ap = ap.rearrange("p (a b c) -> p a b c", b=shape[2], c=shape[3])
            return ap

    CONST_SZ = 13 * 1024
    X_OFF = CONST_SZ
    X_SZ = 64 * 1024
    calloc = Alloc(0, CONST_SZ)
    ident = calloc.get([128, 128], BF16)
    ones_b = calloc.get([128, 128], BF16)
    ones_f = calloc.get([128, 128], F32)
    maskI = calloc.get([128, 128], F32)
    maskS = calloc.get([128, 128], F32)
    maskL = calloc.get([128, 128], F32)
    ident4 = calloc.get([128, 4, 128], BF16)
    maskI4 = calloc.get([128, 4, 128], F32)
    maskS4 = calloc.get([128, 4, 128], F32)
    smallp = calloc.get([128, DEPTH, SP_N], F32)
    bd32 = calloc.get([128, 4, 128], BF16)
    m64 = calloc.get([128, 4, 128], BF16)
    m128 = calloc.get([128, 4, 128], BF16)
    CB = Buf("consts")
    x_sb = arena[:, X_OFF:X_OFF + X_SZ].bitcast(F32).rearrange("p (a b) -> p a b", b=D)
    XT = [Buf(f"x{t}") for t in range(NT)]

    tmpf = arena[:, X_OFF:X_OFF + 512].bitcast(F32)
    TB = Buf("tmpf")

    def mk_mask(dst, cmp_op, base, chmul, patt_step):
        kb.op("pool", lambda e: e.memset(dst, 1.0), writes=[CB])
        kb.op("pool", lambda e: e.affine_select(out=dst, in_=dst, pattern=[[patt_step, 128]],
                                                 compare_op=cmp_op, fill=0.0, base=base,
                                                 channel_multiplier=chmul), reads=[CB], writes=[CB])

    mk_mask(maskI, ALU.is_ge, 0, -1, 1)
    mk_mask(maskS, ALU.is_gt, 0, -1, 1)
    mk_mask(maskL, ALU.is_gt, 0, 1, -1)
    kb.op("pool", lambda e: e.memset(ones_f, 1.0), writes=[CB])
    kb.op("pool", lambda e: e.memset(ones_b, 1.0), writes=[CB])
    kb.op("pool", lambda e: e.memset(tmpf, 1.0), writes=[TB])
    kb.op("pool", lambda e: e.affine_select(out=tmpf, in_=tmpf, pattern=[[-1, 128]], compare_op=ALU.is_equal,
                                             fill=0.0, base=0, channel_multiplier=1), reads=[TB], writes=[TB])
    kb.op("dve", lambda e: e.tensor_copy(out=ident, in_=tmpf), reads=[TB], writes=[CB])
    for c in range(4):
        kb.op("dve", lambda e, c=c: e.tensor_copy(out=ident4[:, c, :], in_=tmpf), reads=[TB], writes=[CB])
        kb.op("dve", lambda e, c=c: e.tensor_copy(out=maskI4[:, c, :], in_=maskI), reads=[CB], writes=[CB])
        kb.op("dve", lambda e, c=c: e.tensor_copy(out=maskS4[:, c, :], in_=maskS), reads=[CB], writes=[CB])
    kb.dma("sp", smallp, smallp_d.rearrange("l p n -> p l n"), writes=[CB])
    cm_stage = arena[:, X_OFF + 4096:X_OFF + 4096 + 3 * 128 * 4].bitcast(F32).rearrange("p (a b) -> p a b", b=128)
    kb.dma("sp", cm_stage, cmask_d, writes=[TB])
    for mi, dstm in enumerate((bd32, m64, m128)):
        for c in range(4):
            kb.op("dve", lambda e: e.tensor_copy(out=dstm[:, c, :], in_=cm_stage[:, mi, :]), reads=[TB], writes=[CB])
    kb.barrier()

    def sp_col(l, c0, n=1):
        return smallp[:, l, c0:c0 + n]

    for t in range(NT):
        kb.dma("sp", x_sb[:, t, :], x_d[t * 128:(t + 1) * 128, :], writes=[XT[t]])

    WSLOTS = 4

    class Ctx:
        pass

    def weight_ring(al, n=WSLOTS, kch=8, ncol=128):
        r = Ctx()
        r.aps = [al.get([128, kch, ncol], BF16, "wslot") for _ in range(n)]
        r.bufs = [Buf(f"ws{i}") for i in range(n)]
        r.i = 0
        return r

    def load_w(ring, w2d, c0, ncol=128, kch=8, r0=0):
        i = ring.i % len(ring.aps)
        ring.i += 1
        src = w2d[r0:r0 + kch * 128, c0:c0 + ncol].rearrange("(k p) n -> p k n", p=128)
        kb.dma("pool", ring.aps[i][:, 0:kch, 0:ncol], src, writes=[ring.bufs[i]])
        return ring.aps[i], ring.bufs[i]

    def proj_fm(wt, wb, rhsT, rhsb, qi, ntok=2048, kch=8):
        qap, qb = quad(qi)
        ng = (ntok + 511) // 512
        for g in range(ng):
            n = min(512, ntok - g * 512)
            for k in range(kch):
                kb.op("pe", lambda e, g=g, k=k, n=n: e.matmul(qap[:, g * 512:g * 512 + n], wt[:, k, :],
                                                              rhsT[:, k, g * 512:g * 512 + n],
                                                              start=(k == 0), stop=(k == kch - 1)),
                      reads=[wb, rhsb], writes=[qb[g]], signal=(k == kch - 1))
        return qap, qb[:ng]

    def norm_to_T(al, l, gcol0, src_tiles, src_bufs, dstT, dstb, pbank=7):
        nt = len(src_tiles)
        ss = al.get([128, 16], F32, "ss")
        rs = al.get([128, 16], F32, "rs")
        junk = al.get([128, D], BF16, "junk")
        xn = [al.get([128, D], BF16, "xn") for _ in range(2)]
        gexp = al.get([128, 8, 128], F32, "gexp")
        SB, JB, GB = Buf("ss"), Buf("junk"), Buf("gexp")
        XN = [Buf("xn0"), Buf("xn1")]
        kb.op("dve", lambda e: e.tensor_copy(out=gexp, in_=sp_col(l, gcol0, 8).unsqueeze(2).to_broadcast([128, 8, 128])),
              reads=[CB], writes=[GB])
        for t in range(nt):
            kb.op("act", lambda e, t=t: e.activation(out=junk, in_=src_tiles[t], func=AF.Square,
                                                     accum_out=ss[:, t:t + 1]),
                  reads=[src_bufs[t]], writes=[JB, SB])
        kb.op("dve", lambda e: e.tensor_scalar(out=rs[:, 0:nt], in0=ss[:, 0:nt], scalar1=1.0 / D, scalar2=EPS,
                                               op0=ALU.mult, op1=ALU.add), reads=[SB], writes=[SB])
        kb.op("act", lambda e: e.activation(out=rs[:, 0:nt], in_=rs[:, 0:nt], func=AF.Sqrt), reads=[SB], writes=[SB])
        kb.op("dve", lambda e: e.reciprocal(out=rs[:, 0:nt], in_=rs[:, 0:nt]), reads=[SB], writes=[SB])
        pb = bank(pbank, BF16)
        for t in range(nt):
            kb.op("act", lambda e, t=t: e.activation(out=xn[t % 2], in_=src_tiles[t], func=AF.Copy,
                                                     scale=rs[:, t:t + 1]),
                  reads=[src_bufs[t], SB], writes=[XN[t % 2]])
            for c in range(8):
                kb.op("pe", lambda e, t=t, c=c: e.transpose(pb[:, c * 128:(c + 1) * 128], xn[t % 2][:, c * 128:(c + 1) * 128], ident),
                      reads=[XN[t % 2], CB], writes=[PB[pbank]], signal=(c == 7))
            kb.op("dve", lambda e, t=t: e.tensor_tensor(out=dstT[:, :, t * 128:(t + 1) * 128],
                                                        in0=pb.rearrange("p (a b) -> p a b", b=128), in1=gexp, op=ALU.mult),
                  reads=[PB[pbank], GB], writes=[dstb])

    def resid_proj(l_w2d, kch, lhsT, lhsb, al, rows0=0):
        wres = al.get([128, kch, D], BF16, "wres")
        WB = Buf("wres")
        for hlf in range(2):
            src = l_w2d[rows0:rows0 + kch * 128, hlf * 512:(hlf + 1) * 512].rearrange("(k p) n -> p k n", p=128)
            kb.dma("pool", wres[:, :, hlf * 512:(hlf + 1) * 512], src, writes=[WB])
        for t in range(NT):
            for hlf in range(2):
                bi = (t * 2 + hlf) % 8
                for k in range(kch):
                    kb.op("pe", lambda e, t=t, hlf=hlf, k=k, bi=bi: e.matmul(
                        bank(bi), lhsT[:, k, t * 128:(t + 1) * 128], wres[:, k, hlf * 512:(hlf + 1) * 512],
                        start=(k == 0), stop=(k == kch - 1)),
                        reads=[lhsb, WB], writes=[PB[bi]], signal=(k == kch - 1))
                kb.op("dve", lambda e, t=t, hlf=hlf, bi=bi: e.tensor_tensor(
                    out=x_sb[:, t, hlf * 512:(hlf + 1) * 512], in0=bank(bi), in1=x_sb[:, t, hlf * 512:(hlf + 1) * 512],
                    op=ALU.add), reads=[PB[bi], XT[t]], writes=[XT[t]])

    def dump(name, ap, shape, bufs):
        dt = nc.dram_tensor("dbg_" + name, list(shape), ap.dtype, kind="ExternalOutput").ap()
        kb.dma("sp", dt, ap, reads=bufs)
        dumps[name] = "dbg_" + name

    def finish():
        kb.barrier()
        return nc, dumps

    stop_req = stop
    for l in range(nlayers):
        stop = stop_req if l == nlayers - 1 else None
        al = Alloc(X_OFF + X_SZ, ARENA)
        hT = al.get([128, 8, S], BF16, "hT")
        HB = Buf("hT")
        mark = al.p
        norm_to_T(al, l, SP_GMIX, [x_sb[:, t, :] for t in range(NT)], XT, hT, HB)
        for t in range(NT):
            kb.dma("sp", xs_d[t * 128:(t + 1) * 128, :], x_sb[:, t, :], reads=[XT[t]])
        kb.barrier()
        if stop == "norm":
            dump("hT", hT, [128, 8, S], [HB])
            return finish()
        al.p = mark
        oT = al.get([128, 8, S], BF16, "oT")
        OB = Buf("oT")
        ring = weight_ring(al)
        ax = Alloc(X_OFF, X_OFF + X_SZ)
        wl = w_in[l]

        wab, wabb = load_w(ring, wl, OFF_AL, ncol=16)
        gall = al.get([128, 16, 8], F32, "gall")
        beta = al.get([128, 16, 8], F32, "beta")
        Gc = al.get([128, 16, 8], F32, "Gc")
        eG = al.get([128, 16, 8], F32, "eG")
        eD = al.get([128, 16, 8], F32, "eD")
        eL = al.get([128, 16, 8], F32, "eL")
        tz = al.get([128, 16, 8], F32, "tz")
        tz2 = al.get([128, 16, 8], F32, "tz2")
        AB = Buf("ab")
        pab = bank(0).rearrange("p (a b) -> p a b", b=32)
        for t in range(NT):
            for k in range(8):
                kb.op("pe", lambda e, t=t, k=k: e.matmul(pab[:, t, 0:16], hT[:, k, t * 128:(t + 1) * 128], wab[:, k, 0:16],
                                                         start=(k == 0), stop=(k == 7)),
                      reads=[HB, wabb], writes=[PB[0]], signal=(k == 7))
        kb.op("act", lambda e: e.activation(out=beta, in_=pab[:, :, 8:16], func=AF.Sigmoid), reads=[PB[0]], writes=[AB])
        kb.op("dve", lambda e: e.tensor_tensor(out=tz, in0=pab[:, :, 0:8],
                                               in1=sp_col(l, SP_DTB, 8).unsqueeze(1).to_broadcast([128, 16, 8]), op=ALU.add),
              reads=[PB[0], CB], writes=[AB])
        kb.op("act", lambda e: e.activation(out=tz2, in_=tz, func=AF.Abs), reads=[AB], writes=[AB])
        kb.op("act", lambda e: e.activation(out=tz2, in_=tz2, func=AF.Exp, scale=-1.0), reads=[AB], writes=[AB])
        kb.op("act", lambda e: e.activation(out=tz2, in_=tz2, func=AF.Ln, bias=1.0), reads=[AB], writes=[AB])
        kb.op("dve", lambda e: e.scalar_tensor_tensor(out=tz, in0=tz, scalar=0.0, in1=tz2, op0=ALU.max, op1=ALU.add),
              reads=[AB], writes=[AB])
        kb.op("act", lambda e: e.activation(out=eL[:, 0, :], in_=sp_col(l, SP_ALOG, 8), func=AF.Exp), reads=[CB], writes=[AB])
        kb.op("dve", lambda e: e.scalar_tensor_tensor(out=gall, in0=tz, scalar=-1.0,
                                                      in1=eL[:, 0:1, :].to_broadcast([128, 16, 8]),
                                                      op0=ALU.mult, op1=ALU.mult), reads=[AB], writes=[AB])
        g2 = gall.rearrange("p a b -> p (a b)")
        kb.op("pe", lambda e: e.matmul(bank(1)[:, 0:128], maskI, g2, start=True, stop=True), reads=[CB, AB], writes=[PB[1]])
        kb.op("pe", lambda e: e.matmul(bank(1)[:, 128:256], ones_f, g2, start=True, stop=True), reads=[CB, AB], writes=[PB[1]])
        kb.op("dve", lambda e: e.tensor_copy(out=Gc.rearrange("p a b -> p (a b)"), in_=bank(1)[:, 0:128]), reads=[PB[1]], writes=[AB])
        kb.op("act", lambda e: e.activation(out=eG.rearrange("p a b -> p (a b)"), in_=bank(1)[:, 0:128], func=AF.Exp),
              reads=[PB[1]], writes=[AB])
        kb.op("act", lambda e: e.activation(out=eL.rearrange("p a b -> p (a b)"), in_=bank(1)[:, 128:256], func=AF.Exp),
              reads=[PB[1]], writes=[AB])
        kb.op("dve", lambda e: e.tensor_tensor(out=tz.rearrange("p a b -> p (a b)"), in0=bank(1)[:, 128:256],
                                               in1=Gc.rearrange("p a b -> p (a b)"), op=ALU.subtract),
              reads=[PB[1], AB], writes=[AB])
        kb.op("act", lambda e: e.activation(out=eD, in_=tz, func=AF.Exp), reads=[AB], writes=[AB])
        if stop == "ab":
            dump("gall", gall, [128, 16, 8], [AB]); dump("beta", beta, [128, 16, 8], [AB])
            dump("eG", eG, [128, 16, 8], [AB]); dump("eD", eD, [128, 16, 8], [AB]); dump("eL", eL, [128, 16, 8], [AB])
            return finish()

        class Alloc2:
            def __init__(self, allocs):
                self.allocs = allocs

            def get(self, shape, dt, name=None):
                for a in self.allocs:
                    save = a.p
                    try:
                        return a.get(shape, dt, name)
                    except AssertionError:
                        a.p = save
                raise AssertionError(f"arena overflow (multi) {name}")

        dmark = al.p
        A2 = Alloc2([ax, al])

        def mk_slot(si):
            s = Ctx()
            s.qi = si
            s.banks = [si * 4 + j for j in range(4)]
            s.raw = [A2.get([128, S], BF16, "raw") for _ in range(3)]
            s.RAWB = [Buf(f"raw{si}{i}") for i in range(3)]
            s.cvt = A2.get([128, 1024], F32, "cvt")
            s.CVB = Buf("cvt")
            s.ut = A2.get([128, 4, 128], F32, "ut")
            s.wT = A2.get([128, 4, 128], BF16, "wT")
            s.pT = A2.get([128, 4, 128], BF16, "pT")
            s.qdT = A2.get([128, 4, 128], BF16, "qdT")
            s.kd = A2.get([128, 4, 128], BF16, "kd")
            s.SCB = Buf("scan_in")
            s.tm = [A2.get([128, 4, 128], BF16, "tm") for _ in range(6)]
            s.TMB = [Buf(f"tm{i}") for i in range(6)]
            s.fm = [A2.get([128, 4, 128], BF16, "fm") for _ in range(3)]
            s.FMB = [Buf(f"fm{i}") for i in range(3)]
            s.gtri = A2.get([128, 4, 128], F32, "gtri")
            s.dS = s.gtri
            s.decT = A2.get([128, 4, 128], F32, "decT")
            s.dI = A2.get([128, 4, 128], F32, "dI")
            s.DCB = Buf("dec")
            s.Am = [A2.get([128, 4, 128], BF16, "Am") for _ in range(2)]
            s.Bm = [A2.get([128, 4, 128], BF16, "Bm") for _ in range(2)]
            s.AMB = [Buf("A0"), Buf("A1")]
            s.BMB = [Buf("B0"), Buf("B1")]
            s.A0 = A2.get([128, 4, 128], BF16, "A0")
            s.P64 = A2.get([128, 4, 128], BF16, "P64")
            s.P128 = A2.get([128, 4, 128], BF16, "P128")
            s.A0B, s.PMB = Buf("A0"), Buf("Pm")
            s.Rf = A2.get([128, 4, 128], F32, "Rf")
            s.Rh = A2.get([128, 4, 128], BF16, "Rh")
            s.Rl = A2.get([128, 4, 128], BF16, "Rl")
            s.RFB, s.RHB, s.RLB = Buf("Rf"), Buf("Rh"), Buf("Rl")
            s.Sf = [A2.get([128, 128], F32, "Sf") for _ in range(2)]
            s.SFB = [Buf("S0"), Buf("S1")]
            s.Sb = A2.get([128, 128], BF16, "Sb")
            s.Ub = A2.get([128, 128], BF16, "Ub")
            s.SBB, s.UBB = Buf("Sb"), Buf("Ub")
            s.obuf = A2.get([128, 4, 128], F32, "obuf")
            s.onb = A2.get([128, 4, 128], BF16, "onb")
            s.OBB, s.ONB, s.OSB = Buf("obuf"), Buf("onb"), Buf("os8")
            s.os8 = A2.get([128, 8], F32, "os8")
            s.ss8 = A2.get([128, 8], F32, "ss8")
            s.sc = A2.get([128, 7, 4], F32, "sc")
            s.GT = Buf("grp_tmp")
            return s

        def proj_fm_g(wt, wb, rhsT, rhsb, qi, kch=8):
            qap, qb = quad(qi)
            for g in range(4):
                for k in range(kch):
                    kb.op("pe", lambda e: e.matmul(qap[:, g * 512:(g + 1) * 512], wt[:, k, :], rhsT[:, k, g * 512:(g + 1) * 512],
                                                   start=(k == 0), stop=(k == kch - 1)),
                          reads=[wb, rhsb], writes=[qb[g]], signal=(k == kch - 1))
                yield

        def v4(bi, dt=F32):
            return bank(bi, dt).rearrange("p (a b) -> p a b", b=128)

        def head_gen(h, s):
            b0, b1, b2, b3 = s.banks
            qap, qbs = quad(s.qi)
            raw, RAWB, cvt, CVB = s.raw, s.RAWB, s.cvt, s.CVB
            tm, TMB, fm, FMB = s.tm, s.TMB, s.fm, s.FMB
            Am, Bm, AMB, BMB = s.Am, s.Bm, s.AMB, s.BMB
            sc, ss8, GT, DCB, SCB = s.sc, s.ss8, s.GT, s.DCB, s.SCB
            for i, c0 in enumerate((OFF_QA, OFF_KA, OFF_VA)):
                wt, wb = load_w(ring, wl, c0 + h * 128)
                yield from proj_fm_g(wt, wb, hT, HB, s.qi)
                cwc = SP_CW + (i * 8 + h) * 4
                for half in range(2):
                    t0 = half * 1024
                    kb.op("dve", lambda e: e.tensor_scalar(out=cvt, in0=qap[:, t0:t0 + 1024], scalar1=sp_col(l, cwc + 3), scalar2=None,
                                                           op0=ALU.mult), reads=qbs + [CB], writes=[CVB])
                    for sh in (1, 2, 3):
                        lo = max(t0, sh)
                        kb.op("dve", lambda e: e.scalar_tensor_tensor(
                            out=cvt[:, lo - t0:1024], in0=qap[:, lo - sh:t0 + 1024 - sh], scalar=sp_col(l, cwc + 3 - sh),
                            in1=cvt[:, lo - t0:1024], op0=ALU.mult, op1=ALU.add), reads=qbs + [CB, CVB], writes=[CVB])
                    kb.op("act", lambda e: e.activation(out=raw[i][:, t0:t0 + 1024], in_=cvt, func=AF.Silu), reads=[CVB], writes=[RAWB[i]])
                    yield
            kb.op("dve", lambda e: e.memset(s.Sf[0], 0.0), writes=[s.SFB[0]])
            kb.op("dve", lambda e: e.memset(s.Sb, 0.0), writes=[s.SBB])
            scur = 0
            for g in range(4):
                t0 = g * 512
                c0 = g * 4
                pqk = v4(b0, BF16)
                pv = v4(b1, BF16)
                for i in range(2):
                    for cc in range(4):
                        kb.op("pe", lambda e: e.transpose(pqk[:, i * 4 + cc, :], raw[i][:, t0 + cc * 128:t0 + (cc + 1) * 128], ident),
                              reads=[RAWB[i], CB], writes=[PB[b0]], signal=(i == 1 and cc == 3))
                for cc in range(4):
                    kb.op("pe", lambda e: e.transpose(pv[:, cc, :], raw[2][:, t0 + cc * 128:t0 + (cc + 1) * 128], ident),
                          reads=[RAWB[2], CB], writes=[PB[b1]], signal=(cc == 3))
                yield
                sqv = cvt.rearrange("p (a b) -> p a b", b=128)
                kb.op("act", lambda e: e.activation(out=sqv, in_=pqk, func=AF.Square), reads=[PB[b0]], writes=[CVB])
                kb.op("dve", lambda e: e.tensor_reduce(out=ss8, in_=sqv, op=ALU.add, axis=AX.X), reads=[CVB], writes=[GT])
                kb.op("dve", lambda e: e.tensor_scalar(out=ss8, in0=ss8, scalar1=EPS, scalar2=None, op0=ALU.add), reads=[GT], writes=[GT])
                kb.op("act", lambda e: e.activation(out=ss8, in_=ss8, func=AF.Sqrt), reads=[GT], writes=[GT])
                kb.op("dve", lambda e: e.reciprocal(out=ss8, in_=ss8), reads=[GT], writes=[GT])
                yield
                bh = beta[:, c0:c0 + 4, h]
                egh = eG[:, c0:c0 + 4, h]
                edh = eD[:, c0:c0 + 4, h]
                kb.op("dve", lambda e: e.tensor_scalar(out=sc[:, 0, :], in0=ss8[:, 0:4], scalar1=float(128 ** -0.5), scalar2=None, op0=ALU.mult),
                      reads=[GT], writes=[GT])
                kb.op("dve", lambda e: e.tensor_tensor(out=sc[:, 1, :], in0=sc[:, 0, :], in1=egh, op=ALU.mult), reads=[GT, AB], writes=[GT])
                kb.op("dve", lambda e: e.tensor_copy(out=sc[:, 2, :], in_=ss8[:, 4:8]), reads=[GT], writes=[GT])
                kb.op("dve", lambda e: e.tensor_tensor(out=sc[:, 3, :], in0=ss8[:, 4:8], in1=bh, op=ALU.mult), reads=[GT, AB], writes=[GT])
                kb.op("dve", lambda e: e.tensor_tensor(out=sc[:, 4, :], in0=sc[:, 3, :], in1=egh, op=ALU.mult), reads=[GT, AB], writes=[GT])
                kb.op("dve", lambda e: e.tensor_tensor(out=sc[:, 5, :], in0=ss8[:, 4:8], in1=edh, op=ALU.mult), reads=[GT, AB], writes=[GT])
                kb.op("dve", lambda e: e.tensor_copy(out=sc[:, 6, :], in_=bh), reads=[GT, AB], writes=[GT])
                yield

                def scaled(dst, dstb, src, srcb, row):
                    kb.op("dve", lambda e: e.tensor_tensor(out=dst, in0=src, in1=sc[:, row, :].unsqueeze(2).to_broadcast([128, 4, 128]),
                                                           op=ALU.mult), reads=[srcb, GT], writes=[dstb])
                scaled(tm[0], TMB[0], pqk[:, 0:4, :], PB[b0], 0)
                scaled(tm[1], TMB[1], pqk[:, 0:4, :], PB[b0], 1)
                scaled(tm[2], TMB[2], pqk[:, 4:8, :], PB[b0], 2)
                yield
                scaled(tm[3], TMB[3], pqk[:, 4:8, :], PB[b0], 3)
                scaled(tm[4], TMB[4], pqk[:, 4:8, :], PB[b0], 4)
                scaled(s.kd, SCB, pqk[:, 4:8, :], PB[b0], 5)
                scaled(tm[5], TMB[5], pv[:, 0:4, :], PB[b1], 6)
                yield
                p4 = v4(b2, BF16)
                p5 = v4(b3, BF16)
                for (src, dstp, pbi, slot) in ((0, p4, b2, 0), (1, p4, b2, 4), (2, p5, b3, 0), (3, p5, b3, 4)):
                    for cc in range(4):
                        kb.op("pe", lambda e: e.transpose(dstp[:, slot + cc, :], tm[src][:, cc, :], ident),
                              reads=[TMB[src], CB], writes=[PB[pbi]], signal=(cc == 3))
                yield
                kb.op("act", lambda e: e.activation(out=fm[0], in_=p4[:, 0:4, :], func=AF.Copy), reads=[PB[b2]], writes=[FMB[0]])
                kb.op("act", lambda e: e.activation(out=s.qdT, in_=p4[:, 4:8, :], func=AF.Copy), reads=[PB[b2]], writes=[SCB])
                kb.op("dve", lambda e: e.tensor_copy(out=fm[1], in_=p5[:, 0:4, :]), reads=[PB[b3]], writes=[FMB[1]])
                kb.op("dve", lambda e: e.tensor_copy(out=fm[2], in_=p5[:, 4:8, :]), reads=[PB[b3]], writes=[FMB[2]])
                yield
                pB = v4(b0)
                pQ = v4(b1)
                for cc in range(4):
                    kb.op("pe", lambda e: e.matmul(pB[:, cc, :], fm[1][:, cc, :], fm[2][:, cc, :], start=True, stop=True),
                          reads=[FMB[1], FMB[2]], writes=[PB[b0]], signal=(cc == 3))
                for cc in range(4):
                    kb.op("pe", lambda e: e.matmul(pQ[:, cc, :], fm[1][:, cc, :], fm[0][:, cc, :], start=True, stop=True),
                          reads=[FMB[1], FMB[0]], writes=[PB[b1]], signal=(cc == 3))
                kb.op("dve", lambda e: e.tensor_tensor(out=s.gtri, in0=maskI4,
                                                       in1=gall[:, c0:c0 + 4, h].unsqueeze(2).to_broadcast([128, 4, 128]), op=ALU.mult),
                      reads=[CB, AB], writes=[DCB])
                kb.op("pe", lambda e: e.matmul(bank(b2), maskL, s.gtri.rearrange("p a b -> p (a b)"), start=True, stop=True),
                      reads=[CB, DCB], writes=[PB[b2]])
                yield
                kb.op("act", lambda e: e.activation(out=s.decT.rearrange("p a b -> p (a b)"), in_=bank(b2), func=AF.Exp),
                      reads=[PB[b2]], writes=[DCB])
                kb.op("dve", lambda e: e.tensor_tensor(out=s.dS, in0=s.decT, in1=maskS4, op=ALU.mult), reads=[DCB, CB], writes=[DCB])
                kb.op("dve", lambda e: e.tensor_tensor(out=s.dI, in0=s.decT, in1=maskI4, op=ALU.mult), reads=[DCB, CB], writes=[DCB])
                kb.op("dve", lambda e: e.scalar_tensor_tensor(out=Bm[0], in0=pB, scalar=-1.0, in1=s.dS, op0=ALU.mult, op1=ALU.mult),
                      reads=[PB[b0], DCB], writes=[BMB[0]])
                kb.op("dve", lambda e: e.tensor_tensor(out=s.pT, in0=pQ, in1=s.dI, op=ALU.mult),
                      reads=[PB[b1], DCB], writes=[SCB])
                yield
                p3 = v4(b3, BF16)
                for cc in range(4):
                    kb.op("pe", lambda e: e.transpose(p3[:, cc, :], Bm[0][:, cc, :], ident), reads=[BMB[0], CB], writes=[PB[b3]],
                          signal=(cc == 3))
                kb.op("act", lambda e: e.activation(out=s.A0, in_=p3[:, 0:4, :], func=AF.Copy), reads=[PB[b3]], writes=[s.A0B])
                kb.op("dve", lambda e: e.tensor_tensor(out=Am[1], in0=s.A0, in1=bd32, op=ALU.mult), reads=[s.A0B, CB], writes=[AMB[1]])
                kb.op("dve", lambda e: e.tensor_tensor(out=Bm[1], in0=Bm[0], in1=bd32, op=ALU.mult), reads=[BMB[0], CB], writes=[BMB[1]])
                kb.op("dve", lambda e: e.tensor_tensor(out=s.Rh, in0=Bm[1], in1=ident4, op=ALU.add), reads=[BMB[1], CB], writes=[s.RHB])
                yield
                cur = 1
                for lev in range(1, 5):
                    nxt = 1 - cur
                    pa, pbk, pr = v4(b0), v4(b1), v4(b2)
                    for cc in range(4):
                        kb.op("pe", lambda e: e.matmul(pa[:, cc, :], Bm[cur][:, cc, :], Am[cur][:, cc, :], start=True, stop=True),
                              reads=[BMB[cur], AMB[cur]], writes=[PB[b0]], signal=(cc == 3))
                    if lev < 4:
                        for cc in range(4):
                            kb.op("pe", lambda e: e.matmul(pbk[:, cc, :], Am[cur][:, cc, :], Bm[cur][:, cc, :], start=True, stop=True),
                                  reads=[BMB[cur], AMB[cur]], writes=[PB[b1]], signal=(cc == 3))
                    kb.op("act", lambda e: e.activation(out=Am[nxt], in_=pa, func=AF.Copy), reads=[PB[b0]], writes=[AMB[nxt]])
                    if lev < 4:
                        kb.op("dve", lambda e: e.tensor_copy(out=Bm[nxt], in_=pbk), reads=[PB[b1]], writes=[BMB[nxt]])
                    yield
                    for cc in range(4):
                        kb.op("pe", lambda e: e.matmul(pr[:, cc, :], Am[nxt][:, cc, :], s.Rh[:, cc, :], start=True, stop=True),
                              reads=[AMB[nxt], s.RHB], writes=[PB[b2]], signal=(cc == 3))
                    if lev == 1:
                        kb.op("dve", lambda e: e.tensor_tensor(out=s.Rf, in0=pr, in1=s.Rh, op=ALU.add),
                              reads=[PB[b2], s.RHB], writes=[s.RFB])
                    else:
                        kb.op("dve", lambda e: e.tensor_tensor(out=s.Rf, in0=pr, in1=s.Rf, op=ALU.add),
                              reads=[PB[b2], s.RFB], writes=[s.RFB])
                    kb.op("act", lambda e: e.activation(out=s.Rh, in_=s.Rf, func=AF.Copy), reads=[s.RFB], writes=[s.RHB])
                    cur = nxt
                    yield
                for Mk in (m64, m128):
                    px, pd, py = v4(b0), v4(b1, BF16), v4(b2)
                    for cc in range(4):
                        kb.op("pe", lambda e: e.matmul(px[:, cc, :], s.A0[:, cc, :], s.Rh[:, cc, :], start=True, stop=True),
                              reads=[s.A0B, s.RHB], writes=[PB[b0]], signal=(cc == 3))
                    for cc in range(4):
                        kb.op("pe", lambda e: e.transpose(pd[:, cc, :], s.Rh[:, cc, :], ident), reads=[s.RHB, CB], writes=[PB[b1]],
                              signal=(cc == 3))
                    kb.op("dve", lambda e: e.tensor_tensor(out=Bm[0], in0=px, in1=Mk, op=ALU.mult), reads=[PB[b0], CB], writes=[BMB[0]])
                    kb.op("act", lambda e: e.activation(out=Am[0], in_=pd[:, 0:4, :], func=AF.Copy), reads=[PB[b1]], writes=[AMB[0]])
                    yield
                    for cc in range(4):
                        kb.op("pe", lambda e: e.matmul(py[:, cc, :], Am[0][:, cc, :], Bm[0][:, cc, :], start=True, stop=True),
                              reads=[AMB[0], BMB[0]], writes=[PB[b2]], signal=(cc == 3))
                    kb.op("dve", lambda e: e.tensor_tensor(out=s.Rf, in0=py, in1=s.Rf, op=ALU.add),
                          reads=[PB[b2], s.RFB], writes=[s.RFB])
                    kb.op("act", lambda e: e.activation(out=s.Rh, in_=s.Rf, func=AF.Copy), reads=[s.RFB], writes=[s.RHB])
                    yield
                kb.op("dve", lambda e: e.tensor_tensor(out=s.Rl, in0=s.Rf, in1=s.Rh, op=ALU.subtract), reads=[s.RFB, s.RHB], writes=[s.RLB])
                pu, pw = v4(b3), v4(b0)
                for cc in range(4):
                    kb.op("pe", lambda e: e.matmul(pu[:, cc, :], s.Rh[:, cc, :], tm[5][:, cc, :], start=True, stop=False),
                          reads=[s.RHB, TMB[5]], writes=[PB[b3]], signal=False)
                    kb.op("pe", lambda e: e.matmul(pu[:, cc, :], s.Rl[:, cc, :], tm[5][:, cc, :], start=False, stop=True),
                          reads=[s.RLB, TMB[5]], writes=[PB[b3]], signal=(cc == 3))
                for cc in range(4):
                    kb.op("pe", lambda e: e.matmul(pw[:, cc, :], tm[4][:, cc, :], s.Rh[:, cc, :], start=True, stop=False),
                          reads=[s.RHB, TMB[4]], writes=[PB[b0]], signal=False)
                    kb.op("pe", lambda e: e.matmul(pw[:, cc, :], tm[4][:, cc, :], s.Rl[:, cc, :], start=False, stop=True),
                          reads=[s.RLB, TMB[4]], writes=[PB[b0]], signal=(cc == 3))
                kb.op("act", lambda e: e.activation(out=s.ut, in_=pu, func=AF.Copy), reads=[PB[b3]], writes=[SCB])
                kb.op("dve", lambda e: e.tensor_copy(out=s.wT, in_=pw), reads=[PB[b0]], writes=[SCB])
                yield
                for cc in range(4):
                    c = c0 + cc
                    if c > 0:
                        kb.op("pe", lambda e: e.matmul(bank(b1)[:, 0:128], s.wT[:, cc, :], s.Sb, start=True, stop=True),
                              reads=[SCB, s.SBB], writes=[PB[b1]])
                        kb.op("dve", lambda e: e.tensor_tensor(out=s.Ub, in0=s.ut[:, cc, :], in1=bank(b1)[:, 0:128], op=ALU.subtract),
                              reads=[SCB, PB[b1]], writes=[s.UBB])
                    else:
                        kb.op("dve", lambda e: e.tensor_copy(out=s.Ub, in_=s.ut[:, cc, :]), reads=[SCB], writes=[s.UBB])
                    po = bank(b2)[:, cc * 128:(cc + 1) * 128]
                    if c > 0:
                        kb.op("pe", lambda e: e.matmul(po, s.qdT[:, cc, :], s.Sb, start=True, stop=False),
                              reads=[SCB, s.SBB], writes=[PB[b2]], signal=False)
                    kb.op("pe", lambda e: e.matmul(po, s.pT[:, cc, :], s.Ub, start=(c == 0), stop=True),
                          reads=[SCB, s.UBB], writes=[PB[b2]])
                    kb.op("pe", lambda e: e.matmul(bank(b3)[:, 0:128], s.kd[:, cc, :], s.Ub, start=True, stop=True),
                          reads=[SCB, s.UBB], writes=[PB[b3]])
                    snx = 1 - scur
                    kb.op("dve", lambda e: e.scalar_tensor_tensor(
                        out=s.Sf[snx], in0=s.Sf[scur], scalar=eL[:, c, h:h + 1], in1=bank(b3)[:, 0:128], op0=ALU.mult, op1=ALU.add),
                        reads=[s.SFB[scur], PB[b3], AB], writes=[s.SFB[snx]])
                    kb.op("act", lambda e: e.activation(out=s.Sb, in_=s.Sf[snx], func=AF.Copy), reads=[s.SFB[snx]], writes=[s.SBB])
                    scur = snx
                    yield
                kb.op("act", lambda e: e.activation(out=s.obuf, in_=v4(b2), func=AF.Copy), reads=[PB[b2]], writes=[s.OBB])
                sqo = cvt[:, 0:512].rearrange("p (a b) -> p a b", b=128)
                kb.op("act", lambda e: e.activation(out=sqo, in_=s.obuf, func=AF.Square), reads=[s.OBB], writes=[CVB])
                kb.op("dve", lambda e: e.tensor_reduce(out=s.os8[:, 0:4], in_=sqo, op=ALU.add, axis=AX.X), reads=[CVB], writes=[s.OSB])
                kb.op("dve", lambda e: e.tensor_scalar(out=s.os8[:, 0:4], in0=s.os8[:, 0:4], scalar1=1.0 / 128, scalar2=EPS,
                                                       op0=ALU.mult, op1=ALU.add), reads=[s.OSB], writes=[s.OSB])
                kb.op("act", lambda e: e.activation(out=s.os8[:, 0:4], in_=s.os8[:, 0:4], func=AF.Sqrt), reads=[s.OSB], writes=[s.OSB])
                kb.op("dve", lambda e: e.reciprocal(out=s.os8[:, 0:4], in_=s.os8[:, 0:4]), reads=[s.OSB], writes=[s.OSB])
                kb.op("dve", lambda e: e.tensor_tensor(out=s.onb, in0=s.obuf, in1=s.os8[:, 0:4].unsqueeze(2).to_broadcast([128, 4, 128]),
                                                       op=ALU.mult), reads=[s.OBB, s.OSB], writes=[s.ONB])
                yield
                p3 = v4(b1, BF16)
                for c4 in range(4):
                    kb.op("pe", lambda e: e.transpose(p3[:, c4, :], s.onb[:, c4, :], ident), reads=[s.ONB, CB], writes=[PB[b1]],
                          signal=(c4 == 3))
                kb.op("act", lambda e: e.activation(out=oT[:, h, g * 512:(g + 1) * 512],
                                                    in_=bank(b1, BF16)[:, 0:512], func=AF.Copy, scale=sp_col(l, SP_HN)),
                      reads=[PB[b1], CB], writes=[OTB[h]])
                yield
            wt, wb = load_w(ring, wl, OFF_ZA + h * 128)
            yield from proj_fm_g(wt, wb, hT, HB, s.qi)
            kb.op("act", lambda e: e.activation(out=raw[0], in_=qap, func=AF.Silu), reads=qbs, writes=[RAWB[0]])
            kb.op("dve", lambda e: e.tensor_tensor(out=oT[:, h, :], in0=oT[:, h, :], in1=raw[0], op=ALU.mult),
                  reads=[RAWB[0], OTB[h]], writes=[OTB[h]])
            yield

        def slot_gen(s, heads):
            for h in heads:
                yield from head_gen(h, s)

        OTB = [Buf(f"oT{h}") for h in range(8)]
        slots = [mk_slot(0), mk_slot(1)]
        kb.fresh = dbg.get("fresh", False)
        nheads = dbg.get("nheads", 8 if stop != "head0" else 2)
        if dbg.get("single", True):
            gens = [slot_gen(slots[dbg.get("slot", 0)], list(range(nheads)))]
        else:
            gens = [slot_gen(slots[0], list(range(0, nheads, 2))), slot_gen(slots[1], list(range(1, nheads, 2)))]
        lead = dbg.get("lead", 40)
        for _ in range(lead):
            try:
                next(gens[0])
            except StopIteration:
                break
        active = list(gens)
        while active:
            for gsel in list(active):
                try:
                    next(gsel)
                except StopIteration:
                    active.remove(gsel)
        kb.fresh = False
        kb.barrier()
        if stop in ("head0", "delta"):
            dump("oT", oT, [128, 8, S], [OB])
            return finish()
        al.p = dmark
        yT = al.get([128, 8, S], BF16, "yT")
        YB = Buf("yT")
        raw = [arena[:, X_OFF + i * 4096:X_OFF + (i + 1) * 4096].bitcast(BF16) for i in range(3)]
        RAWB = [Buf(f"rawy{i}") for i in range(3)]


        sg = raw[1]
        SGB = RAWB[1]

        def ypass(gate_off, wmat, first):
            for m in range(8):
                wt, wb = load_w(ring, wl, gate_off + m * 128)
                qap, qbs = proj_fm(wt, wb, hT, HB, 0)
                kb.op("act", lambda e: e.activation(out=sg, in_=qap, func=AF.Sigmoid), reads=qbs, writes=[SGB])
                wt2, wb2 = load_w(ring, wmat, m * 128)
                qap2, qbs2 = proj_fm(wt2, wb2, oT, OB, 1)
                if first:
                    kb.op("dve", lambda e, m=m: e.tensor_tensor(out=yT[:, m, :], in0=qap2, in1=sg, op=ALU.mult),
                          reads=qbs2 + [SGB], writes=[YB])
                else:
                    kb.op("dve", lambda e, m=m: e.tensor_tensor(out=raw[2], in0=qap2, in1=sg, op=ALU.mult),
                          reads=qbs2 + [SGB], writes=[RAWB[2]])
                    kb.op("dve", lambda e, m=m: e.tensor_tensor(out=yT[:, m, :], in0=yT[:, m, :], in1=raw[2], op=ALU.add),
                          reads=[RAWB[2], YB], writes=[YB])

        ypass(OFF_GA, w_a[l], True)
        if stop == "ya":
            dump("yT", yT, [128, 8, S], [YB])
            return finish()

        kb.barrier()
        ax = Alloc(X_OFF, X_OFF + X_SZ)
        raw = [ax.get([128, S], BF16, f"raw{i}") for i in range(3)]
        qbT, kbT = raw[0], raw[1]
        QBB, KBB = RAWB[0], RAWB[1]
        sg = raw[1]
        vbt = ax.get([128, 16, 128], BF16, "vbt")
        VBB = Buf("vbt")
        biasb = [ax.get([128, 5, 128], F32, f"bias{i}") for i in range(2)]
        BIB = [Buf("bias0"), Buf("bias1")]
        stmp = [ax.get([128, 5, 128], F32, f"stmp{i}") for i in range(2)]
        STB = [Buf("st0"), Buf("st1")]
        PTt = [ax.get([128, 5, 128], BF16, f"PT{i}") for i in range(2)]
        PTB = [Buf("PT0"), Buf("PT1")]
        rcp = ax.get([128, 512], F32, "rcp")
        RCB = Buf("rcp")
        it = 0
        for hp in range(8):
            wt, wb = load_w(ring, wl, OFF_QB + hp * 128)
            qap, qbs = proj_fm(wt, wb, hT, HB, 0)
            kb.op("act", lambda e: e.activation(out=qbT, in_=qap, func=AF.Copy, scale=0.125), reads=qbs, writes=[QBB])
            wt, wb = load_w(ring, wl, OFF_KB + hp * 128)
            qap, qbs = proj_fm(wt, wb, hT, HB, 1)
            kb.op("dve", lambda e: e.tensor_copy(out=kbT, in_=qap), reads=qbs, writes=[KBB])
            wt, wb = load_w(ring, wl, OFF_VB + hp * 128)
            for t in range(NT):
                bi = t // 4
                for k in range(8):
                    kb.op("pe", lambda e, t=t, k=k, bi=bi: e.matmul(bank(bi)[:, (t % 4) * 128:(t % 4 + 1) * 128],
                                                                    hT[:, k, t * 128:(t + 1) * 128], wt[:, k, :],
                                                                    start=(k == 0), stop=(k == 7)),
                          reads=[HB, wb], writes=[PB[bi]], signal=(k == 7))
                if t % 4 == 3:
                    kb.op("act", lambda e, bi=bi: e.activation(out=vbt[:, bi * 4:(bi + 1) * 4, :],
                                                               in_=bank(bi).rearrange("p (a b) -> p a b", b=128), func=AF.Copy),
                          reads=[PB[bi]], writes=[VBB])
            for hh in range(2):
                head = hp * 2 + hh
                kb.dma("sp", biasb[hh].rearrange("p a b -> p (a b)"), bias_d[head], writes=[BIB[hh]])
                kb.op("dve", lambda e: e.memset(biasb[hh][0:64, 0, 64:128], -30000.0), writes=[BIB[hh]])
                kb.op("dve", lambda e: e.memset(biasb[hh][64:128, 4, 0:64], -30000.0), writes=[BIB[hh]])
            its = [(hh, qb_i) for qb_i in range(16) for hh in range(2)]

            def stage1(ii):
                hh, qb_i = its[ii]
                base = hh * 64
                sl = ii % 2
                r0 = max(0, 4 - qb_i)
                pst = psum[:, sl * 1024: sl * 1024 + 640].rearrange("p (a b) -> p a b", b=128)
                pstb = [PB[sl * 2], PB[sl * 2 + 1]]
                for r in range(r0, 5):
                    kblk = qb_i - 4 + r
                    kb.op("pe", lambda e: e.matmul(
                        pst[:, r, :], kbT[base:base + 64, kblk * 128:(kblk + 1) * 128],
                        qbT[base:base + 64, qb_i * 128:(qb_i + 1) * 128], start=True, stop=True),
                        reads=[KBB, QBB], writes=pstb, signal=(r == 4))
                kb.op("dve", lambda e: e.tensor_tensor(out=stmp[sl][:, r0:5, :], in0=pst[:, r0:5, :],
                                                       in1=biasb[hh][:, r0:5, :], op=ALU.add),
                      reads=pstb + [BIB[hh]], writes=[STB[sl]])
                kb.op("act", lambda e: e.activation(out=PTt[sl][:, r0:5, :], in_=stmp[sl][:, r0:5, :], func=AF.Exp),
                      reads=[STB[sl]], writes=[PTB[sl]])

            def stage2(ii):
                hh, qb_i = its[ii]
                base = hh * 64
                sl = ii % 2
                r0 = max(0, 4 - qb_i)
                qg, qq = qb_i // 4, qb_i % 4
                po = bank(4 + hh * 2)
                psm = bank(5 + hh * 2)
                for r in range(r0, 5):
                    kblk = qb_i - 4 + r
                    kb.op("pe", lambda e: e.matmul(
                        po[:, qq * 128:(qq + 1) * 128], vbt[:, kblk, :], PTt[sl][:, r, :], start=(r == r0), stop=(r == 4)),
                        reads=[VBB, PTB[sl]], writes=[PB[4 + hh * 2]], signal=(r == 4))
                for r in range(r0, 5):
                    kb.op("pe", lambda e: e.matmul(
                        psm[:, qq * 128:(qq + 1) * 128], ones_b, PTt[sl][:, r, :], start=(r == r0), stop=(r == 4)),
                        reads=[CB, PTB[sl]], writes=[PB[5 + hh * 2]], signal=(r == 4))
                if qq == 3:
                    kb.op("dve", lambda e: e.reciprocal(out=rcp, in_=psm), reads=[PB[5 + hh * 2]], writes=[RCB])
                    kb.op("dve", lambda e: e.tensor_tensor(
                        out=oT[base:base + 64, hp, qg * 512:(qg + 1) * 512], in0=po[base:base + 64, :], in1=rcp[base:base + 64, :],
                        op=ALU.mult), reads=[PB[4 + hh * 2], RCB], writes=[OB])

            stage1(0)
            for ii in range(len(its)):
                if ii + 1 < len(its):
                    stage1(ii + 1)
                stage2(ii)
            if stop == "band0":
                dump("oT", oT, [128, 8, S], [OB])
                return finish()
        if stop == "band":
            dump("oT", oT, [128, 8, S], [OB])
            return finish()
        raw2_keep = raw[2]
        ypass(OFF_GB, w_b[l], False)
        if stop == "yb":
            dump("yT", yT, [128, 8, S], [YB])
            return finish()

        kb.barrier()
        for t in range(NT):
            kb.dma("sp", x_sb[:, t, :], xs_d[t * 128:(t + 1) * 128, :], writes=[XT[t]])
        al.p = mark
        alh = Alloc(X_OFF + X_SZ, X_OFF + X_SZ + 32 * 1024)
        resid_proj(w_o[l], 8, yT, YB, alh)
        kb.barrier()
        if stop == "mix":
            dump("x", x_sb, [128, NT, D], XT)
            return finish()

        al = Alloc(X_OFF + X_SZ, ARENA)
        hT = al.get([128, 8, S], BF16, "hT")
        oT = al.get([128, 8, S], BF16, "oT")
        HB, OB = Buf("hT"), Buf("oT")
        memT = al.get([128, 8, 256], BF16, "memT")
        MTB = Buf("memT")
        MFB = [Buf("memf0"), Buf("memf1")]
        kT = al.get([128, 8, 256], BF16, "kT")
        vm = al.get([128, 2, D], BF16, "vm")
        KTB, VMB = Buf("kT"), Buf("vm")
        qT = al.get([128, 2, S], BF16, "qT")
        QTB = Buf("qT")
        PTm = [al.get([128, 2, 512], BF16, f"PTm{i}") for i in range(2)]
        PMB = [Buf("PTm0"), Buf("PTm1")]
        rcp = al.get([128, 512], F32, "rcp")
        RCB = Buf("rcp")
        ring = weight_ring(al, n=3)
        ring5 = weight_ring(al, n=1, ncol=512)
        mark2 = al.p
        memf = al.get([128, 2, D], F32, "memf")
        for mt in range(2):
            kb.dma("sp", memf[:, mt, :], mem_d[mt * 128:(mt + 1) * 128, :], writes=[MFB[mt]])
        norm_to_T(al, l, SP_GMEM, [memf[:, 0, :], memf[:, 1, :]], MFB, memT, MTB)
        al.p = mark2
        norm_to_T(al, l, SP_GXA, [x_sb[:, t, :] for t in range(NT)], XT, hT, HB)
        al.p = mark2
        for c in range(8):
            wt, wb = load_w(ring, w_mkv[l], c * 128)
            qap, qbs = proj_fm(wt, wb, memT, MTB, 0, ntok=256)
            kb.op("act", lambda e, c=c: e.activation(out=kT[:, c, :], in_=qap[:, 0:256], func=AF.Copy), reads=qbs, writes=[KTB])
        for hlf in range(2):
            wt, wb = load_w(ring5, w_mkv[l], D + hlf * 512, ncol=512)
            for mt in range(2):
                bi = 4 + hlf * 2 + mt
                for k in range(8):
                    kb.op("pe", lambda e, mt=mt, k=k, bi=bi: e.matmul(bank(bi), memT[:, k, mt * 128:(mt + 1) * 128], wt[:, k, :],
                                                                      start=(k == 0), stop=(k == 7)),
                          reads=[MTB, wb], writes=[PB[bi]], signal=(k == 7))
                kb.op("dve", lambda e, mt=mt, hlf=hlf, bi=bi: e.tensor_copy(out=vm[:, mt, hlf * 512:(hlf + 1) * 512], in_=bank(bi)),
                      reads=[PB[bi]], writes=[VMB])
        it = 0
        for hd in range(4):
            for ci in range(2):
                wt, wb = load_w(ring, w_mq[l], (hd * 2 + ci) * 128)
                qap, qbs = proj_fm(wt, wb, hT, HB, ci)
                kb.op("act", lambda e, ci=ci: e.activation(out=qT[:, ci, :], in_=qap, func=AF.Copy, scale=1.0 / 16), reads=qbs, writes=[QTB])
            for g in range(4):
                sl = it % 2
                it += 1
                for mt in range(2):
                    for ci in range(2):
                        kb.op("pe", lambda e, mt=mt, ci=ci, g=g: e.matmul(
                            bank(mt), kT[:, hd * 2 + ci, mt * 128:(mt + 1) * 128], qT[:, ci, g * 512:(g + 1) * 512],
                            start=(ci == 0), stop=(ci == 1)), reads=[KTB, QTB], writes=[PB[mt]], signal=(ci == 1))
                    kb.op("act", lambda e, mt=mt, sl=sl: e.activation(out=PTm[sl][:, mt, :], in_=bank(mt), func=AF.Exp),
                          reads=[PB[mt]], writes=[PMB[sl]])
                for mt in range(2):
                    kb.op("pe", lambda e, mt=mt, sl=sl: e.matmul(bank(2), ones_b, PTm[sl][:, mt, :], start=(mt == 0), stop=(mt == 1)),
                          reads=[CB, PMB[sl]], writes=[PB[2]], signal=(mt == 1))
                kb.op("dve", lambda e: e.reciprocal(out=rcp, in_=bank(2)), reads=[PB[2]], writes=[RCB])
                for ci in range(2):
                    for mt in range(2):
                        kb.op("pe", lambda e, mt=mt, ci=ci, sl=sl: e.matmul(
                            bank(3 + ci), vm[:, mt, (hd * 2 + ci) * 128:(hd * 2 + ci + 1) * 128], PTm[sl][:, mt, :],
                            start=(mt == 0), stop=(mt == 1)), reads=[VMB, PMB[sl]], writes=[PB[3 + ci]], signal=(mt == 1))
                    kb.op("dve", lambda e, ci=ci, g=g: e.tensor_tensor(out=oT[:, hd * 2 + ci, g * 512:(g + 1) * 512], in0=bank(3 + ci),
                                                                   in1=rcp, op=ALU.mult), reads=[PB[3 + ci], RCB], writes=[OB])
        al.p = mark2
        resid_proj(w_mo[l], 8, oT, OB, al)
        kb.barrier()
        if stop == "xa":
            dump("x", x_sb, [128, NT, D], XT)
            return finish()

        al = Alloc(X_OFF + X_SZ, ARENA)
        hT = al.get([128, 8, S], BF16, "hT")
        HB = Buf("hT")
        act = al.get([128, 11, S], BF16, "act")
        ACB = Buf("act")
        sgt = al.get([128, S], BF16, "sgt")
        SGB = Buf("sgt")
        ring = weight_ring(al)
        mark3 = al.p
        norm_to_T(al, l, SP_GFFN, [x_sb[:, t, :] for t in range(NT)], XT, hT, HB)
        for hf in range(2):
            for jj in range(11):
                j = hf * 11 + jj
                wt, wb = load_w(ring, w_gu[l], j * 128)
                qap, qbs = proj_fm(wt, wb, hT, HB, 0)
                kb.op("act", lambda e: e.activation(out=sgt, in_=qap, func=AF.Silu), reads=qbs, writes=[SGB])
                wt, wb = load_w(ring, w_gu[l], DFF + j * 128)
                qap2, qbs2 = proj_fm(wt, wb, hT, HB, 1)
                kb.op("dve", lambda e, jj=jj: e.tensor_tensor(out=act[:, jj, :], in0=qap2, in1=sgt, op=ALU.mult),
                      reads=qbs2 + [SGB], writes=[ACB])
            al.p = mark3
            resid_proj(w_dn[l], 11, act, ACB, al, rows0=hf * 11 * 128)
        kb.barrier()
        if stop == "ffn":
            dump("x", x_sb, [128, NT, D], XT)
            return finish()

    al = Alloc(X_OFF + X_SZ, ARENA)
    gf = al.get([128, D], F32, "gf")
    GFB = Buf("gf")
    kb.dma("sp", gf, gfin_d, writes=[GFB])
    ss = al.get([128, 16], F32, "ss")
    junk = al.get([128, D], BF16, "junk")
    yo = [al.get([128, D], F32, f"yo{i}") for i in range(2)]
    YOB = [Buf("yo0"), Buf("yo1")]
    SB, JB = Buf("ss"), Buf("junk")
    for t in range(NT):
        kb.op("act", lambda e, t=t: e.activation(out=junk, in_=x_sb[:, t, :], func=AF.Square, accum_out=ss[:, t:t + 1]),
              reads=[XT[t]], writes=[JB, SB])
    kb.op("dve", lambda e: e.tensor_scalar(out=ss, in0=ss, scalar1=1.0 / D, scalar2=EPS, op0=ALU.mult, op1=ALU.add), reads=[SB], writes=[SB])
    kb.op("act", lambda e: e.activation(out=ss, in_=ss, func=AF.Sqrt), reads=[SB], writes=[SB])
    kb.op("dve", lambda e: e.reciprocal(out=ss, in_=ss), reads=[SB], writes=[SB])
    for t in range(NT):
        kb.op("dve", lambda e, t=t: e.scalar_tensor_tensor(out=yo[t % 2], in0=x_sb[:, t, :], scalar=ss[:, t:t + 1], in1=gf,
                                                           op0=ALU.mult, op1=ALU.mult), reads=[XT[t], SB, GFB], writes=[YOB[t % 2]])
        kb.dma("sp", out_d[t * 128:(t + 1) * 128, :], yo[t % 2], reads=[YOB[t % 2]])
    return finish()


def _host_layout(inputs):
    f = np.float32
    g = lambda k: np.ascontiguousarray(np.asarray(inputs[k], dtype=f))
    smallp = np.zeros((DEPTH, 128, SP_N), f)
    for l in range(DEPTH):
        for c0, key in ((SP_GMIX, "norm_mix"), (SP_GXA, "norm_xattn"), (SP_GMEM, "norm_mem"), (SP_GFFN, "norm_ffn")):
            smallp[l, :, c0:c0 + 8] = g(key)[l].reshape(8, 128).T
        cw = g("conv_w")[l]
        smallp[l, :, SP_CW:SP_CW + 96] = cw.T.reshape(24, 128, 4).transpose(1, 0, 2).reshape(128, 96)
        smallp[l, :, SP_HN] = g("head_norm")[l]
        smallp[l, :, SP_ALOG:SP_ALOG + 8] = np.broadcast_to(g("a_log")[l][None, :], (128, 8))
        smallp[l, :, SP_DTB:SP_DTB + 8] = np.broadcast_to(g("dt_bias")[l][None, :], (128, 8))
    gfinal = np.ascontiguousarray(np.broadcast_to(g("norm_final")[None, :], (128, D)))
    kin = np.arange(128)[:, None, None]
    r = np.arange(5)[None, :, None]
    qin = np.arange(128)[None, None, :]
    idx = np.clip(qin - kin + 128 * (4 - r), -63, 256) + 63
    biasq = np.ascontiguousarray(g("rel_bias")[:, idx].reshape(16, 128, 640))
    shared = {k: g(k) for k in ("w_in", "w_a_out", "w_b_out", "w_o", "w_mq", "w_mkv", "w_mo", "w_gate_up", "w_down")}
    pp = np.arange(128)[:, None]
    ff = np.arange(128)[None, :]
    cmask = np.stack([(pp // 32 == ff // 32), (pp % 64 < 32) & (ff % 64 >= 32) & (pp // 64 == ff // 64),
                      (pp < 64) & (ff >= 64)], axis=1).astype(f)
    shared.update(smallp=smallp, gfinal=gfinal, biasq=biasq, cmask=np.ascontiguousarray(cmask))
    return shared


_CACHE = {}


def kernel(**inputs):
    shared = _host_layout(inputs)
    x = np.ascontiguousarray(np.asarray(inputs["x"], dtype=np.float32))
    mem = np.ascontiguousarray(np.asarray(inputs["mem"], dtype=np.float32))
    if "nc" not in _CACHE:
        _CACHE["nc"] = build_program()[0]
    nc = _CACHE["nc"]
    in_maps = []
    for b in range(8):
        m = dict(shared)
        m["x"] = x[b]
        m["mem"] = mem[b]
        in_maps.append(m)
    res = run_bass_kernel_spmd(nc, in_maps, core_ids=list(range(8)))
    return np.stack([np.asarray(r["out"], dtype=np.float32) for r in res.results], axis=0)
```

```python
import numpy as np
import concourse.bass as bass
import concourse.mybir as mybir
from concourse.bass_utils import run_bass_kernel_spmd

F32 = mybir.dt.float32
BF16 = mybir.dt.bfloat16
U8 = mybir.dt.uint8
AF = mybir.ActivationFunctionType
ALU = mybir.AluOpType
AX = mybir.AxisListType

S = 2048
D = 1024
NT = 16
DEPTH = 2
EPS = 1e-6
NIN = 9232
OFF_QA, OFF_KA, OFF_VA, OFF_ZA, OFF_AL, OFF_QB, OFF_KB, OFF_VB, OFF_GA, OFF_GB = (
    0, 1024, 2048, 3072, 4096, 4112, 5136, 6160, 7184, 8208)
DFF = 2816
SP_GMIX, SP_GXA, SP_GMEM, SP_GFFN, SP_CW, SP_HN, SP_ALOG, SP_DTB = 0, 8, 16, 24, 32, 128, 129, 137
SP_N = 145
ARENA = 200 * 1024


class Buf:
    __slots__ = ("name", "w", "r")

    def __init__(self, name):
        self.name = name
        self.w = None
        self.r = []


class Eng:
    def __init__(self, name, handle, sem):
        self.name = name
        self.h = handle
        self.sem = sem
        self.count = 0
        self.seen = {}
        self.pending = False


class KB:
    def __init__(self, nc, same_engine_sync=True, dq_n=20):
        self.nc = nc
        self.same = same_engine_sync
        self.sems = {}
        self.E = {}
        for nm, h in (("pe", nc.tensor), ("act", nc.scalar), ("dve", nc.vector), ("pool", nc.gpsimd)):
            s = nc.semaphore("sem_" + nm).__enter__()
            self.E[nm] = Eng(nm, h, s)
        self.E["sp"] = Eng("sp", nc.sync, None)
        self.dq = {}
        for q in ("sp", "pool"):
            sl = [nc.semaphore(f"dq_{q}{i}").__enter__() for i in range(dq_n)]
            self.dq[q] = {"sems": sl, "cum": [0] * len(sl), "i": 0}
        self.all_dma_tokens = []
        self.fresh = False
        self.fresh_sems = []

    def _wait(self, e, tok):
        if tok is None:
            return
        sem, val, owner = tok
        key = id(sem)
        if e.seen.get(key, 0) >= val:
            return
        if owner is not None:
            oe = self.E[owner]
            if owner == e.name:
                if e.name == "pe" or not self.same:
                    return
            assert oe.count >= val, f"wait on unsignalled future {owner} {val} > {oe.count}"
        e.h.wait_ge(sem, val)
        e.seen[key] = val

    def _deps(self, e, reads, writes):
        for b in reads:
            self._wait(e, b.w)
        for b in writes:
            self._wait(e, b.w)
            for t in b.r:
                self._wait(e, t)

    def _commit(self, tok, reads, writes):
        for b in reads:
            b.r.append(tok)
            if len(b.r) > 24:
                b.r = b.r[-24:] if False else self._compact(b.r)
        for b in writes:
            b.w = tok
            b.r = []

    @staticmethod
    def _compact(toks):
        best = {}
        for t in toks:
            k = id(t[0])
            if k not in best or best[k][1] < t[1]:
                best[k] = t
        return list(best.values())

    def op(self, eng, fn, reads=(), writes=(), signal=True):
        e = self.E[eng]
        self._deps(e, reads, writes)
        ins = fn(e.h)
        tok = (e.sem, e.count + 1, eng)
        if signal:
            ins.then_inc(e.sem, 1)
            e.count += 1
            e.pending = False
        else:
            e.pending = True
        self._commit(tok, reads, writes)
        return ins

    def dma(self, q, out, in_, reads=(), writes=()):
        e = self.E[q]
        if self.fresh:
            sem = self.nc.semaphore(f"dqf_{len(self.fresh_sems)}").__enter__()
            self.fresh_sems.append(sem)
            self._deps(e, reads, writes)
            e.h.dma_start(out=out, in_=in_).then_inc(sem, 16)
            tok = (sem, 16, None)
            self.all_dma_tokens.append(tok)
            self._commit(tok, reads, writes)
            return tok
        d = self.dq[q]
        i = d["i"] % len(d["sems"])
        d["i"] += 1
        sem = d["sems"][i]
        self._wait(e, (sem, d["cum"][i], None))
        self._deps(e, reads, writes)
        d["cum"][i] += 16
        e.h.dma_start(out=out, in_=in_).then_inc(sem, 16)
        tok = (sem, d["cum"][i], None)
        self._commit(tok, reads, writes)
        return tok

    def barrier(self):
        toks = []
        for nm in ("pe", "act", "dve", "pool"):
            oe = self.E[nm]
            assert not oe.pending, f"{nm} has unsignalled trailing instruction at barrier"
            if oe.count:
                toks.append((oe.sem, oe.count, nm))
        for q in ("sp", "pool"):
            d = self.dq[q]
            for s, c in zip(d["sems"], d["cum"]):
                if c:
                    toks.append((s, c, None))
        toks.extend(self.all_dma_tokens)
        self.all_dma_tokens = []
        for nm, e in self.E.items():
            for t in toks:
                if t[2] == nm:
                    continue
                self._wait(e, t)


def build_program(dbg=None):
    dbg = dbg or {}
    stop = dbg.get("stop")
    nlayers = dbg.get("layers", DEPTH)
    nc = bass.Bass("TRN2", target_bir_lowering=False)
    kb = KB(nc, same_engine_sync=dbg.get("same", True), dq_n=dbg.get("dq_n", 20))

    def din(name, shape):
        return nc.dram_tensor(name, list(shape), F32, kind="ExternalInput").ap()

    x_d = din("x", [S, D])
    mem_d = din("mem", [256, D])
    w_in = din("w_in", [DEPTH, D, NIN])
    w_a = din("w_a_out", [DEPTH, D, D])
    w_b = din("w_b_out", [DEPTH, D, D])
    w_o = din("w_o", [DEPTH, D, D])
    w_mq = din("w_mq", [DEPTH, D, D])
    w_mkv = din("w_mkv", [DEPTH, D, 2 * D])
    w_mo = din("w_mo", [DEPTH, D, D])
    w_gu = din("w_gate_up", [DEPTH, D, 2 * DFF])
    w_dn = din("w_down", [DEPTH, DFF, D])
    smallp_d = din("smallp", [DEPTH, 128, SP_N])
    gfin_d = din("gfinal", [128, D])
    bias_d = din("biasq", [16, 128, 640])
    cmask_d = din("cmask", [128, 3, 128])
    out_d = nc.dram_tensor("out", [S, D], F32, kind="ExternalOutput").ap()
    xs_d = nc.dram_tensor("x_spill", [S, D], F32, kind="Internal").ap()
    dumps = {}

    arena = nc.sbuf_tensor("arena", [128, ARENA], U8).__enter__()
    psum = nc.psum_tensor("psum", [128, 4096], F32).__enter__()
    PB = [Buf(f"pb{i}") for i in range(8)]

    def bank(i, dt=F32, n=None):
        ap = psum[:, i * 512:(i + 1) * 512]
        if dt == BF16:
            ap = ap.bitcast(BF16)
        return ap

    def quad(qi):
        return psum[:, qi * 2048:(qi + 1) * 2048], PB[qi * 4:(qi + 1) * 4]

    class Alloc:
        def __init__(self, lo, hi):
            self.lo, self.hi, self.p = lo, hi, lo

        def get(self, shape, dt, name=None):
            esz = 4 if dt == F32 else 2
            n = int(np.prod(shape[1:])) * esz
            off = (self.p + 63) // 64 * 64
            assert off + n <= self.hi, f"arena overflow {name} {off + n} > {self.hi}"
            self.p = off + n
            ap = arena[:, off:off + n].bitcast(dt)
            if len(shape) == 3:
                ap = ap.rearrange("p (a b) -> p a b", b=shape[2])
            elif len(shape) == 4:
                ap = ap.rearrange("p (a b c) -> p a b c", b=shape[2], c=shape[3])
            return ap

    CONST_SZ = 13 * 1024
    X_OFF = CONST_SZ
    X_SZ = 64 * 1024
    calloc = Alloc(0, CONST_SZ)
    ident = calloc.get([128, 128], BF16)
    ones_b = calloc.get([128, 128], BF16)
    ones_f = calloc.get([128, 128], F32)
    maskI = calloc.get([128, 128], F32)
    maskS = calloc.get([128, 128], F32)
    maskL = calloc.get([128, 128], F32)
    ident4 = calloc.get([128, 4, 128], BF16)
    maskI4 = calloc.get([128, 4, 128], F32)
    maskS4 = calloc.get([128, 4, 128], F32)
    smallp = calloc.get([128, DEPTH, SP_N], F32)
    bd32 = calloc.get([128, 4, 128], BF16)
    m64 = calloc.get([128, 4, 128], BF16)
    m128 = calloc.get([128, 4, 128], BF16)
    CB = Buf("consts")
    x_sb = arena[:, X_OFF:X_OFF + X_SZ].bitcast(F32).rearrange("p (a b) -> p a b", b=D)
    XT = [Buf(f"x{t}") for t in range(NT)]

    tmpf = arena[:, X_OFF:X_OFF + 512].bitcast(F32)
    TB = Buf("tmpf")

    def mk_mask(dst, cmp_op, base, chmul, patt_step):
        kb.op("pool", lambda e: e.memset(dst, 1.0), writes=[CB])
        kb.op("pool", lambda e: e.affine_select(out=dst, in_=dst, pattern=[[patt_step, 128]],
                                                 compare_op=cmp_op, fill=0.0, base=base,
                                                 channel_multiplier=chmul), reads=[CB], writes=[CB])

    mk_mask(maskI, ALU.is_ge, 0, -1, 1)
    mk_mask(maskS, ALU.is_gt, 0, -1, 1)
    mk_mask(maskL, ALU.is_gt, 0, 1, -1)
    kb.op("pool", lambda e: e.memset(ones_f, 1.0), writes=[CB])
    kb.op("pool", lambda e: e.memset(ones_b, 1.0), writes=[CB])
    kb.op("pool", lambda e: e.memset(tmpf, 1.0), writes=[TB])
    kb.op("pool", lambda e: e.affine_select(out=tmpf, in_=tmpf, pattern=[[-1, 128]], compare_op=ALU.is_equal,
                                             fill=0.0, base=0, channel_multiplier=1), reads=[TB], writes=[TB])
    kb.op("dve", lambda e: e.tensor_copy(out=ident, in_=tmpf), reads=[TB], writes=[CB])
    for c in range(4):
        kb.op("dve", lambda e, c=c: e.tensor_copy(out=ident4[:, c, :], in_=tmpf), reads=[TB], writes=[CB])
        kb.op("dve", lambda e, c=c: e.tensor_copy(out=maskI4[:, c, :], in_=maskI), reads=[CB], writes=[CB])
        kb.op("dve", lambda e, c=c: e.tensor_copy(out=maskS4[:, c, :], in_=maskS), reads=[CB], writes=[CB])
    kb.dma("sp", smallp, smallp_d.rearrange("l p n -> p l n"), writes=[CB])
    cm_stage = arena[:, X_OFF + 4096:X_OFF + 4096 + 3 * 128 * 4].bitcast(F32).rearrange("p (a b) -> p a b", b=128)
    kb.dma("sp", cm_stage, cmask_d, writes=[TB])
    for mi, dstm in enumerate((bd32, m64, m128)):
        for c in range(4):
            kb.op("dve", lambda e: e.tensor_copy(out=dstm[:, c, :], in_=cm_stage[:, mi, :]), reads=[TB], writes=[CB])
    kb.barrier()

    def sp_col(l, c0, n=1):
        return smallp[:, l, c0:c0 + n]

    for t in range(NT):
        kb.dma("sp", x_sb[:, t, :], x_d[t * 128:(t + 1) * 128, :], writes=[XT[t]])

    WSLOTS = 4

    class Ctx:
        pass

    def weight_ring(al, n=WSLOTS, kch=8, ncol=128):
        r = Ctx()
        r.aps = [al.get([128, kch, ncol], BF16, "wslot") for _ in range(n)]
        r.bufs = [Buf(f"ws{i}") for i in range(n)]
        r.i = 0
        return r

    def load_w(ring, w2d, c0, ncol=128, kch=8, r0=0):
        i = ring.i % len(ring.aps)
        ring.i += 1
        src = w2d[r0:r0 + kch * 128, c0:c0 + ncol].rearrange("(k p) n -> p k n", p=128)
        kb.dma("pool", ring.aps[i][:, 0:kch, 0:ncol], src, writes=[ring.bufs[i]])
        return ring.aps[i], ring.bufs[i]

    def proj_fm(wt, wb, rhsT, rhsb, qi, ntok=2048, kch=8):
        qap, qb = quad(qi)
        ng = (ntok + 511) // 512
        for g in range(ng):
            n = min(512, ntok - g * 512)
            for k in range(kch):
                kb.op("pe", lambda e, g=g, k=k, n=n: e.matmul(qap[:, g * 512:g * 512 + n], wt[:, k, :],
                                                              rhsT[:, k, g * 512:g * 512 + n],
                                                              start=(k == 0), stop=(k == kch - 1)),
                      reads=[wb, rhsb], writes=[qb[g]], signal=(k == kch - 1))
        return qap, qb[:ng]

    def norm_to_T(al, l, gcol0, src_tiles, src_bufs, dstT, dstb, pbank=7):
        nt = len(src_tiles)
        ss = al.get([128, 16], F32, "ss")
        rs = al.get([128, 16], F32, "rs")
        junk = al.get([128, D], BF16, "junk")
        xn = [al.get([128, D], BF16, "xn") for _ in range(2)]
        gexp = al.get([128, 8, 128], F32, "gexp")
        SB, JB, GB = Buf("ss"), Buf("junk"), Buf("gexp")
        XN = [Buf("xn0"), Buf("xn1")]
        kb.op("dve", lambda e: e.tensor_copy(out=gexp, in_=sp_col(l, gcol0, 8).unsqueeze(2).to_broadcast([128, 8, 128])),
              reads=[CB], writes=[GB])
        for t in range(nt):
            kb.op("act", lambda e, t=t: e.activation(out=junk, in_=src_tiles[t], func=AF.Square,
                                                     accum_out=ss[:, t:t + 1]),
                  reads=[src_bufs[t]], writes=[JB, SB])
        kb.op("dve", lambda e: e.tensor_scalar(out=rs[:, 0:nt], in0=ss[:, 0:nt], scalar1=1.0 / D, scalar2=EPS,
                                               op0=ALU.mult, op1=ALU.add), reads=[SB], writes=[SB])
        kb.op("act", lambda e: e.activation(out=rs[:, 0:nt], in_=rs[:, 0:nt], func=AF.Sqrt), reads=[SB], writes=[SB])
        kb.op("dve", lambda e: e.reciprocal(out=rs[:, 0:nt], in_=rs[:, 0:nt]), reads=[SB], writes=[SB])
        pb = bank(pbank, BF16)
        for t in range(nt):
            kb.op("act", lambda e, t=t: e.activation(out=xn[t % 2], in_=src_tiles[t], func=AF.Copy,
                                                     scale=rs[:, t:t + 1]),
                  reads=[src_bufs[t], SB], writes=[XN[t % 2]])
            for c in range(8):
                kb.op("pe", lambda e, t=t, c=c: e.transpose(pb[:, c * 128:(c + 1) * 128], xn[t % 2][:, c * 128:(c + 1) * 128], ident),
                      reads=[XN[t % 2], CB], writes=[PB[pbank]], signal=(c == 7))
            kb.op("dve", lambda e, t=t: e.tensor_tensor(out=dstT[:, :, t * 128:(t + 1) * 128],
                                                        in0=pb.rearrange("p (a b) -> p a b", b=128), in1=gexp, op=ALU.mult),
                  reads=[PB[pbank], GB], writes=[dstb])

    def resid_proj(l_w2d, kch, lhsT, lhsb, al, rows0=0):
        wres = al.get([128, kch, D], BF16, "wres")
        WB = Buf("wres")
        for hlf in range(2):
            src = l_w2d[rows0:rows0 + kch * 128, hlf * 512:(hlf + 1) * 512].rearrange("(k p) n -> p k n", p=128)
            kb.dma("pool", wres[:, :, hlf * 512:(hlf + 1) * 512], src, writes=[WB])
        for t in range(NT):
            for hlf in range(2):
                bi = (t * 2 + hlf) % 8
                for k in range(kch):
                    kb.op("pe", lambda e, t=t, hlf=hlf, k=k, bi=bi: e.matmul(
                        bank(bi), lhsT[:, k, t * 128:(t + 1) * 128], wres[:, k, hlf * 512:(hlf + 1) * 512],
                        start=(k == 0), stop=(k == kch - 1)),
                        reads=[lhsb, WB], writes=[PB[bi]], signal=(k == kch - 1))
                kb.op("dve", lambda e, t=t, hlf=hlf, bi=bi: e.tensor_tensor(
                    out=x_sb[:, t, hlf * 512:(hlf + 1) * 512], in0=bank(bi), in1=x_sb[:, t, hlf * 512:(hlf + 1) * 512],
                    op=ALU.add), reads=[PB[bi], XT[t]], writes=[XT[t]])

    def dump(name, ap, shape, bufs):
        dt = nc.dram_tensor("dbg_" + name, list(shape), ap.dtype, kind="ExternalOutput").ap()
        kb.dma("sp", dt, ap, reads=bufs)
        dumps[name] = "dbg_" + name

    def finish():
        kb.barrier()
        return nc, dumps

    stop_req = stop
    for l in range(nlayers):
        stop = stop_req if l == nlayers - 1 else None
        al = Alloc(X_OFF + X_SZ, ARENA)
        hT = al.get([128, 8, S], BF16, "hT")
        HB = Buf("hT")
        mark = al.p
        norm_to_T(al, l, SP_GMIX, [x_sb[:, t, :] for t in range(NT)], XT, hT, HB)
        for t in range(NT):
            kb.dma("sp", xs_d[t * 128:(t + 1) * 128, :], x_sb[:, t, :], reads=[XT[t]])
        kb.barrier()
        if stop == "norm":
            dump("hT", hT, [128, 8, S], [HB])
            return finish()
        al.p = mark
        oT = al.get([128, 8, S], BF16, "oT")
        OB = Buf("oT")
        ring = weight_ring(al)
        ax = Alloc(X_OFF, X_OFF + X_SZ)
        wl = w_in[l]

        wab, wabb = load_w(ring, wl, OFF_AL, ncol=16)
        gall = al.get([128, 16, 8], F32, "gall")
        beta = al.get([128, 16, 8], F32, "beta")
        Gc = al.get([128, 16, 8], F32, "Gc")
        eG = al.get([128, 16, 8], F32, "eG")
        eD = al.get([128, 16, 8], F32, "eD")
        eL = al.get([128, 16, 8], F32, "eL")
        tz = al.get([128, 16, 8], F32, "tz")
        tz2 = al.get([128, 16, 8], F32, "tz2")
        AB = Buf("ab")
        pab = bank(0).rearrange("p (a b) -> p a b", b=32)
        for t in range(NT):
            for k in range(8):
                kb.op("pe", lambda e, t=t, k=k: e.matmul(pab[:, t, 0:16], hT[:, k, t * 128:(t + 1) * 128], wab[:, k, 0:16],
                                                         start=(k == 0), stop=(k == 7)),
                      reads=[HB, wabb], writes=[PB[0]], signal=(k == 7))
        kb.op("act", lambda e: e.activation(out=beta, in_=pab[:, :, 8:16], func=AF.Sigmoid), reads=[PB[0]], writes=[AB])
        kb.op("dve", lambda e: e.tensor_tensor(out=tz, in0=pab[:, :, 0:8],
                                               in1=sp_col(l, SP_DTB, 8).unsqueeze(1).to_broadcast([128, 16, 8]), op=ALU.add),
              reads=[PB[0], CB], writes=[AB])
        kb.op("act", lambda e: e.activation(out=tz2, in_=tz, func=AF.Abs), reads=[AB], writes=[AB])
        kb.op("act", lambda e: e.activation(out=tz2, in_=tz2, func=AF.Exp, scale=-1.0), reads=[AB], writes=[AB])
        kb.op("act", lambda e: e.activation(out=tz2, in_=tz2, func=AF.Ln, bias=1.0), reads=[AB], writes=[AB])
        kb.op("dve", lambda e: e.scalar_tensor_tensor(out=tz, in0=tz, scalar=0.0, in1=tz2, op0=ALU.max, op1=ALU.add),
              reads=[AB], writes=[AB])
        kb.op("act", lambda e: e.activation(out=eL[:, 0, :], in_=sp_col(l, SP_ALOG, 8), func=AF.Exp), reads=[CB], writes=[AB])
        kb.op("dve", lambda e: e.scalar_tensor_tensor(out=gall, in0=tz, scalar=-1.0,
                                                      in1=eL[:, 0:1, :].to_broadcast([128, 16, 8]),
                                                      op0=ALU.mult, op1=ALU.mult), reads=[AB], writes=[AB])
        g2 = gall.rearrange("p a b -> p (a b)")
        kb.op("pe", lambda e: e.matmul(bank(1)[:, 0:128], maskI, g2, start=True, stop=True), reads=[CB, AB], writes=[PB[1]])
        kb.op("pe", lambda e: e.matmul(bank(1)[:, 128:256], ones_f, g2, start=True, stop=True), reads=[CB, AB], writes=[PB[1]])
        kb.op("dve", lambda e: e.tensor_copy(out=Gc.rearrange("p a b -> p (a b)"), in_=bank(1)[:, 0:128]), reads=[PB[1]], writes=[AB])
        kb.op("act", lambda e: e.activation(out=eG.rearrange("p a b -> p (a b)"), in_=bank(1)[:, 0:128], func=AF.Exp),
              reads=[PB[1]], writes=[AB])
        kb.op("act", lambda e: e.activation(out=eL.rearrange("p a b -> p (a b)"), in_=bank(1)[:, 128:256], func=AF.Exp),
              reads=[PB[1]], writes=[AB])
        kb.op("dve", lambda e: e.tensor_tensor(out=tz.rearrange("p a b -> p (a b)"), in0=bank(1)[:, 128:256],
                                               in1=Gc.rearrange("p a b -> p (a b)"), op=ALU.subtract),
              reads=[PB[1], AB], writes=[AB])
        kb.op("act", lambda e: e.activation(out=eD, in_=tz, func=AF.Exp), reads=[AB], writes=[AB])
        if stop == "ab":
            dump("gall", gall, [128, 16, 8], [AB]); dump("beta", beta, [128, 16, 8], [AB])
            dump("eG", eG, [128, 16, 8], [AB]); dump("eD", eD, [128, 16, 8], [AB]); dump("eL", eL, [128, 16, 8], [AB])
            return finish()

        class Alloc2:
            def __init__(self, allocs):
                self.allocs = allocs

            def get(self, shape, dt, name=None):
                for a in self.allocs:
                    save = a.p
                    try:
                        return a.get(shape, dt, name)
                    except AssertionError:
                        a.p = save
                raise AssertionError(f"arena overflow (multi) {name}")

        dmark = al.p
        A2 = Alloc2([ax, al])

        def mk_slot(si):
            s = Ctx()
            s.qi = si
            s.banks = [si * 4 + j for j in range(4)]
            s.raw = [A2.get([128, S], BF16, "raw") for _ in range(3)]
            s.RAWB = [Buf(f"raw{si}{i}") for i in range(3)]
            s.cvt = A2.get([128, 1024], F32, "cvt")
            s.CVB = Buf("cvt")
            s.ut = A2.get([128, 4, 128], F32, "ut")
            s.wT = A2.get([128, 4, 128], BF16, "wT")
            s.pT = A2.get([128, 4, 128], BF16, "pT")
            s.qdT = A2.get([128, 4, 128], BF16, "qdT")
            s.kd = A2.get([128, 4, 128], BF16, "kd")
            s.SCB = Buf("scan_in")
            s.tm = [A2.get([128, 4, 128], BF16, "tm") for _ in range(6)]
            s.TMB = [Buf(f"tm{i}") for i in range(6)]
            s.fm = [A2.get([128, 4, 128], BF16, "fm") for _ in range(3)]
            s.FMB = [Buf(f"fm{i}") for i in range(3)]
            s.gtri = A2.get([128, 4, 128], F32, "gtri")
            s.dS = s.gtri
            s.decT = A2.get([128, 4, 128], F32, "decT")
            s.dI = A2.get([128, 4, 128], F32, "dI")
            s.DCB = Buf("dec")
            s.Am = [A2.get([128, 4, 128], BF16, "Am") for _ in range(2)]
            s.Bm = [A2.get([128, 4, 128], BF16, "Bm") for _ in range(2)]
            s.AMB = [Buf("A0"), Buf("A1")]
            s.BMB = [Buf("B0"), Buf("B1")]
            s.A0 = A2.get([128, 4, 128], BF16, "A0")
            s.P64 = A2.get([128, 4, 128], BF16, "P64")
            s.P128 = A2.get([128, 4, 128], BF16, "P128")
            s.A0B, s.PMB = Buf("A0"), Buf("Pm")
            s.Rf = A2.get([128, 4, 128], F32, "Rf")
            s.Rh = A2.get([128, 4, 128], BF16, "Rh")
            s.Rl = A2.get([128, 4, 128], BF16, "Rl")
            s.RFB, s.RHB, s.RLB = Buf("Rf"), Buf("Rh"), Buf("Rl")
            s.Sf = [A2.get([128, 128], F32, "Sf") for _ in range(2)]
            s.SFB = [Buf("S0"), Buf("S1")]
            s.Sb = A2.get([128, 128], BF16, "Sb")
            s.Ub = A2.get([128, 128], BF16, "Ub")
            s.SBB, s.UBB = Buf("Sb"), Buf("Ub")
            s.obuf = A2.get([128, 4, 128], F32, "obuf")
            s.onb = A2.get([128, 4, 128], BF16, "onb")
            s.OBB, s.ONB, s.OSB = Buf("obuf"), Buf("onb"), Buf("os8")
            s.os8 = A2.get([128, 8], F32, "os8")
            s.ss8 = A2.get([128, 8], F32, "ss8")
            s.sc = A2.get([128, 7, 4], F32, "sc")
            s.GT = Buf("grp_tmp")
            return s

        def proj_fm_g(wt, wb, rhsT, rhsb, qi, kch=8):
            qap, qb = quad(qi)
            for g in range(4):
                for k in range(kch):
                    kb.op("pe", lambda e: e.matmul(qap[:, g * 512:(g + 1) * 512], wt[:, k, :], rhsT[:, k, g * 512:(g + 1) * 512],
                                                   start=(k == 0), stop=(k == kch - 1)),
                          reads=[wb, rhsb], writes=[qb[g]], signal=(k == kch - 1))
                yield

        def v4(bi, dt=F32):
            return bank(bi, dt).rearrange("p (a b) -> p a b", b=128)

        def head_gen(h, s):
            b0, b1, b2, b3 = s.banks
            qap, qbs = quad(s.qi)
            raw, RAWB, cvt, CVB = s.raw, s.RAWB, s.cvt, s.CVB
            tm, TMB, fm, FMB = s.tm, s.TMB, s.fm, s.FMB
            Am, Bm, AMB, BMB = s.Am, s.Bm, s.AMB, s.BMB
            sc, ss8, GT, DCB, SCB = s.sc, s.ss8, s.GT, s.DCB, s.SCB
            for i, c0 in enumerate((OFF_QA, OFF_KA, OFF_VA)):
                wt, wb = load_w(ring, wl, c0 + h * 128)
                yield from proj_fm_g(wt, wb, hT, HB, s.qi)
                cwc = SP_CW + (i * 8 + h) * 4
                for half in range(2):
                    t0 = half * 1024
                    kb.op("dve", lambda e: e.tensor_scalar(out=cvt, in0=qap[:, t0:t0 + 1024], scalar1=sp_col(l, cwc + 3), scalar2=None,
                                                           op0=ALU.mult), reads=qbs + [CB], writes=[CVB])
                    for sh in (1, 2, 3):
                        lo = max(t0, sh)
                        kb.op("dve", lambda e: e.scalar_tensor_tensor(
                            out=cvt[:, lo - t0:1024], in0=qap[:, lo - sh:t0 + 1024 - sh], scalar=sp_col(l, cwc + 3 - sh),
                            in1=cvt[:, lo - t0:1024], op0=ALU.mult, op1=ALU.add), reads=qbs + [CB, CVB], writes=[CVB])
                    kb.op("act", lambda e: e.activation(out=raw[i][:, t0:t0 + 1024], in_=cvt, func=AF.Silu), reads=[CVB], writes=[RAWB[i]])
                    yield
            kb.op("dve", lambda e: e.memset(s.Sf[0], 0.0), writes=[s.SFB[0]])
            kb.op("dve", lambda e: e.memset(s.Sb, 0.0), writes=[s.SBB])
            scur = 0
            for g in range(4):
                t0 = g * 512
                c0 = g * 4
                pqk = v4(b0, BF16)
                pv = v4(b1, BF16)
                for i in range(2):
                    for cc in range(4):
                        kb.op("pe", lambda e: e.transpose(pqk[:, i * 4 + cc, :], raw[i][:, t0 + cc * 128:t0 + (cc + 1) * 128], ident),
                              reads=[RAWB[i], CB], writes=[PB[b0]], signal=(i == 1 and cc == 3))
                for cc in range(4):
                    kb.op("pe", lambda e: e.transpose(pv[:, cc, :], raw[2][:, t0 + cc * 128:t0 + (cc + 1) * 128], ident),
                          reads=[RAWB[2], CB], writes=[PB[b1]], signal=(cc == 3))
                yield
                sqv = cvt.rearrange("p (a b) -> p a b", b=128)
                kb.op("act", lambda e: e.activation(out=sqv, in_=pqk, func=AF.Square), reads=[PB[b0]], writes=[CVB])
                kb.op("dve", lambda e: e.tensor_reduce(out=ss8, in_=sqv, op=ALU.add, axis=AX.X), reads=[CVB], writes=[GT])
                kb.op("dve", lambda e: e.tensor_scalar(out=ss8, in0=ss8, scalar1=EPS, scalar2=None, op0=ALU.add), reads=[GT], writes=[GT])
                kb.op("act", lambda e: e.activation(out=ss8, in_=ss8, func=AF.Sqrt), reads=[GT], writes=[GT])
                kb.op("dve", lambda e: e.reciprocal(out=ss8, in_=ss8), reads=[GT], writes=[GT])
                yield
                bh = beta[:, c0:c0 + 4, h]
                egh = eG[:, c0:c0 + 4, h]
                edh = eD[:, c0:c0 + 4, h]
                kb.op("dve", lambda e: e.tensor_scalar(out=sc[:, 0, :], in0=ss8[:, 0:4], scalar1=float(128 ** -0.5), scalar2=None, op0=ALU.mult),
                      reads=[GT], writes=[GT])
                kb.op("dve", lambda e: e.tensor_tensor(out=sc[:, 1, :], in0=sc[:, 0, :], in1=egh, op=ALU.mult), reads=[GT, AB], writes=[GT])
                kb.op("dve", lambda e: e.tensor_copy(out=sc[:, 2, :], in_=ss8[:, 4:8]), reads=[GT], writes=[GT])
                kb.op("dve", lambda e: e.tensor_tensor(out=sc[:, 3, :], in0=ss8[:, 4:8], in1=bh, op=ALU.mult), reads=[GT, AB], writes=[GT])
                kb.op("dve", lambda e: e.tensor_tensor(out=sc[:, 4, :], in0=sc[:, 3, :], in1=egh, op=ALU.mult), reads=[GT, AB], writes=[GT])
                kb.op("dve", lambda e: e.tensor_tensor(out=sc[:, 5, :], in0=ss8[:, 4:8], in1=edh, op=ALU.mult), reads=[GT, AB], writes=[GT])
                kb.op("dve", lambda e: e.tensor_copy(out=sc[:, 6, :], in_=bh), reads=[GT, AB], writes=[GT])
                yield

                def scaled(dst, dstb, src, srcb, row):
                    kb.op("dve", lambda e: e.tensor_tensor(out=dst, in0=src, in1=sc[:, row, :].unsqueeze(2).to_broadcast([128, 4, 128]),
                                                           op=ALU.mult), reads=[srcb, GT], writes=[dstb])
                scaled(tm[0], TMB[0], pqk[:, 0:4, :], PB[b0], 0)
                scaled(tm[1], TMB[1], pqk[:, 0:4, :], PB[b0], 1)
                scaled(tm[2], TMB[2], pqk[:, 4:8, :], PB[b0], 2)
                yield
                scaled(tm[3], TMB[3], pqk[:, 4:8, :], PB[b0], 3)
                scaled(tm[4], TMB[4], pqk[:, 4:8, :], PB[b0], 4)
                scaled(s.kd, SCB, pqk[:, 4:8, :], PB[b0], 5)
                scaled(tm[5], TMB[5], pv[:, 0:4, :], PB[b1], 6)
                yield
                p4 = v4(b2, BF16)
                p5 = v4(b3, BF16)
                for (src, dstp, pbi, slot) in ((0, p4, b2, 0), (1, p4, b2, 4), (2, p5, b3, 0), (3, p5, b3, 4)):
                    for cc in range(4):
                        kb.op("pe", lambda e: e.transpose(dstp[:, slot + cc, :], tm[src][:, cc, :], ident),
                              reads=[TMB[src], CB], writes=[PB[pbi]], signal=(cc == 3))
                yield
                kb.op("act", lambda e: e.activation(out=fm[0], in_=p4[:, 0:4, :], func=AF.Copy), reads=[PB[b2]], writes=[FMB[0]])
                kb.op("act", lambda e: e.activation(out=s.qdT, in_=p4[:, 4:8, :], func=AF.Copy), reads=[PB[b2]], writes=[SCB])
                kb.op("dve", lambda e: e.tensor_copy(out=fm[1], in_=p5[:, 0:4, :]), reads=[PB[b3]], writes=[FMB[1]])
                kb.op("dve", lambda e: e.tensor_copy(out=fm[2], in_=p5[:, 4:8, :]), reads=[PB[b3]], writes=[FMB[2]])
                yield
                pB = v4(b0)
                pQ = v4(b1)
                for cc in range(4):
                    kb.op("pe", lambda e: e.matmul(pB[:, cc, :], fm[1][:, cc, :], fm[2][:, cc, :], start=True, stop=True),
                          reads=[FMB[1], FMB[2]], writes=[PB[b0]], signal=(cc == 3))
                for cc in range(4):
                    kb.op("pe", lambda e: e.matmul(pQ[:, cc, :], fm[1][:, cc, :], fm[0][:, cc, :], start=True, stop=True),
                          reads=[FMB[1], FMB[0]], writes=[PB[b1]], signal=(cc == 3))
                kb.op("dve", lambda e: e.tensor_tensor(out=s.gtri, in0=maskI4,
                                                       in1=gall[:, c0:c0 + 4, h].unsqueeze(2).to_broadcast([128, 4, 128]), op=ALU.mult),
                      reads=[CB, AB], writes=[DCB])
                kb.op("pe", lambda e: e.matmul(bank(b2), maskL, s.gtri.rearrange("p a b -> p (a b)"), start=True, stop=True),
                      reads=[CB, DCB], writes=[PB[b2]])
                yield
                kb.op("act", lambda e: e.activation(out=s.decT.rearrange("p a b -> p (a b)"), in_=bank(b2), func=AF.Exp),
                      reads=[PB[b2]], writes=[DCB])
                kb.op("dve", lambda e: e.tensor_tensor(out=s.dS, in0=s.decT, in1=maskS4, op=ALU.mult), reads=[DCB, CB], writes=[DCB])
                kb.op("dve", lambda e: e.tensor_tensor(out=s.dI, in0=s.decT, in1=maskI4, op=ALU.mult), reads=[DCB, CB], writes=[DCB])
                kb.op("dve", lambda e: e.scalar_tensor_tensor(out=Bm[0], in0=pB, scalar=-1.0, in1=s.dS, op0=ALU.mult, op1=ALU.mult),
                      reads=[PB[b0], DCB], writes=[BMB[0]])
                kb.op("dve", lambda e: e.tensor_tensor(out=s.pT, in0=pQ, in1=s.dI, op=ALU.mult),
                      reads=[PB[b1], DCB], writes=[SCB])
                yield
                p3 = v4(b3, BF16)
                for cc in range(4):
                    kb.op("pe", lambda e: e.transpose(p3[:, cc, :], Bm[0][:, cc, :], ident), reads=[BMB[0], CB], writes=[PB[b3]],
                          signal=(cc == 3))
                kb.op("act", lambda e: e.activation(out=s.A0, in_=p3[:, 0:4, :], func=AF.Copy), reads=[PB[b3]], writes=[s.A0B])
                kb.op("dve", lambda e: e.tensor_tensor(out=Am[1], in0=s.A0, in1=bd32, op=ALU.mult), reads=[s.A0B, CB], writes=[AMB[1]])
                kb.op("dve", lambda e: e.tensor_tensor(out=Bm[1], in0=Bm[0], in1=bd32, op=ALU.mult), reads=[BMB[0], CB], writes=[BMB[1]])
                kb.op("dve", lambda e: e.tensor_tensor(out=s.Rh, in0=Bm[1], in1=ident4, op=ALU.add), reads=[BMB[1], CB], writes=[s.RHB])
                yield
                cur = 1
                for lev in range(1, 5):
                    nxt = 1 - cur
                    pa, pbk, pr = v4(b0), v4(b1), v4(b2)
                    for cc in range(4):
                        kb.op("pe", lambda e: e.matmul(pa[:, cc, :], Bm[cur][:, cc, :], Am[cur][:, cc, :], start=True, stop=True),
                              reads=[BMB[cur], AMB[cur]], writes=[PB[b0]], signal=(cc == 3))
                    if lev < 4:
                        for cc in range(4):
                            kb.op("pe", lambda e: e.matmul(pbk[:, cc, :], Am[cur][:, cc, :], Bm[cur][:, cc, :], start=True, stop=True),
                                  reads=[BMB[cur], AMB[cur]], writes=[PB[b1]], signal=(cc == 3))
                    kb.op("act", lambda e: e.activation(out=Am[nxt], in_=pa, func=AF.Copy), reads=[PB[b0]], writes=[AMB[nxt]])
                    if lev < 4:
                        kb.op("dve", lambda e: e.tensor_copy(out=Bm[nxt], in_=pbk), reads=[PB[b1]], writes=[BMB[nxt]])
                    yield
                    for cc in range(4):
                        kb.op("pe", lambda e: e.matmul(pr[:, cc, :], Am[nxt][:, cc, :], s.Rh[:, cc, :], start=True, stop=True),
                              reads=[AMB[nxt], s.RHB], writes=[PB[b2]], signal=(cc == 3))
                    if lev == 1:
                        kb.op("dve", lambda e: e.tensor_tensor(out=s.Rf, in0=pr, in1=s.Rh, op=ALU.add),
                              reads=[PB[b2], s.RHB], writes=[s.RFB])
                    else:
                        kb.op("dve", lambda e: e.tensor_tensor(out=s.Rf, in0=pr, in1=s.Rf, op=ALU.add),
                              reads=[PB[b2], s.RFB], writes=[s.RFB])
                    kb.op("act", lambda e: e.activation(out=s.Rh, in_=s.Rf, func=AF.Copy), reads=[s.RFB], writes=[s.RHB])
                    cur = nxt
                    yield
                for Mk in (m64, m128):
                    px, pd, py = v4(b0), v4(b1, BF16), v4(b2)
                    for cc in range(4):
                        kb.op("pe", lambda e: e.matmul(px[:, cc, :], s.A0[:, cc, :], s.Rh[:, cc, :], start=True, stop=True),
                              reads=[s.A0B, s.RHB], writes=[PB[b0]], signal=(cc == 3))
                    for cc in range(4):
                        kb.op("pe", lambda e: e.transpose(pd[:, cc, :], s.Rh[:, cc, :], ident), reads=[s.RHB, CB], writes=[PB[b1]],
                              signal=(cc == 3))
                    kb.op("dve", lambda e: e.tensor_tensor(out=Bm[0], in0=px, in1=Mk, op=ALU.mult), reads=[PB[b0], CB], writes=[BMB[0]])
                    kb.op("act", lambda e: e.activation(out=Am[0], in_=pd[:, 0:4, :], func=AF.Copy), reads=[PB[b1]], writes=[AMB[0]])
                    yield
                    for cc in range(4):
                        kb.op("pe", lambda e: e.matmul(py[:, cc, :], Am[0][:, cc, :], Bm[0][:, cc, :], start=True, stop=True),
                              reads=[AMB[0], BMB[0]], writes=[PB[b2]], signal=(cc == 3))
                    kb.op("dve", lambda e: e.tensor_tensor(out=s.Rf, in0=py, in1=s.Rf, op=ALU.add),
                          reads=[PB[b2], s.RFB], writes=[s.RFB])
                    kb.op("act", lambda e: e.activation(out=s.Rh, in_=s.Rf, func=AF.Copy), reads=[s.RFB], writes=[s.RHB])
                    yield
                kb.op("dve", lambda e: e.tensor_tensor(out=s.Rl, in0=s.Rf, in1=s.Rh, op=ALU.subtract), reads=[s.RFB, s.RHB], writes=[s.RLB])
                pu, pw = v4(b3), v4(b0)
                for cc in range(4):
                    kb.op("pe", lambda e: e.matmul(pu[:, cc, :], s.Rh[:, cc, :], tm[5][:, cc, :], start=True, stop=False),
                          reads=[s.RHB, TMB[5]], writes=[PB[b3]], signal=False)
                    kb.op("pe", lambda e: e.matmul(pu[:, cc, :], s.Rl[:, cc, :], tm[5][:, cc, :], start=False, stop=True),
                          reads=[s.RLB, TMB[5]], writes=[PB[b3]], signal=(cc == 3))
                for cc in range(4):
                    kb.op("pe", lambda e: e.matmul(pw[:, cc, :], tm[4][:, cc, :], s.Rh[:, cc, :], start=True, stop=False),
                          reads=[s.RHB, TMB[4]], writes=[PB[b0]], signal=False)
                    kb.op("pe", lambda e: e.matmul(pw[:, cc, :], tm[4][:, cc, :], s.Rl[:, cc, :], start=False, stop=True),
                          reads=[s.RLB, TMB[4]], writes=[PB[b0]], signal=(cc == 3))
                kb.op("act", lambda e: e.activation(out=s.ut, in_=pu, func=AF.Copy), reads=[PB[b3]], writes=[SCB])
                kb.op("dve", lambda e: e.tensor_copy(out=s.wT, in_=pw), reads=[PB[b0]], writes=[SCB])
                yield
                for cc in range(4):
                    c = c0 + cc
                    if c > 0:
                        kb.op("pe", lambda e: e.matmul(bank(b1)[:, 0:128], s.wT[:, cc, :], s.Sb, start=True, stop=True),
                              reads=[SCB, s.SBB], writes=[PB[b1]])
                        kb.op("dve", lambda e: e.tensor_tensor(out=s.Ub, in0=s.ut[:, cc, :], in1=bank(b1)[:, 0:128], op=ALU.subtract),
                              reads=[SCB, PB[b1]], writes=[s.UBB])
                    else:
                        kb.op("dve", lambda e: e.tensor_copy(out=s.Ub, in_=s.ut[:, cc, :]), reads=[SCB], writes=[s.UBB])
                    po = bank(b2)[:, cc * 128:(cc + 1) * 128]
                    if c > 0:
                        kb.op("pe", lambda e: e.matmul(po, s.qdT[:, cc, :], s.Sb, start=True, stop=False),
                              reads=[SCB, s.SBB], writes=[PB[b2]], signal=False)
                    kb.op("pe", lambda e: e.matmul(po, s.pT[:, cc, :], s.Ub, start=(c == 0), stop=True),
                          reads=[SCB, s.UBB], writes=[PB[b2]])
                    kb.op("pe", lambda e: e.matmul(bank(b3)[:, 0:128], s.kd[:, cc, :], s.Ub, start=True, stop=True),
                          reads=[SCB, s.UBB], writes=[PB[b3]])
                    snx = 1 - scur
                    kb.op("dve", lambda e: e.scalar_tensor_tensor(
                        out=s.Sf[snx], in0=s.Sf[scur], scalar=eL[:, c, h:h + 1], in1=bank(b3)[:, 0:128], op0=ALU.mult, op1=ALU.add),
                        reads=[s.SFB[scur], PB[b3], AB], writes=[s.SFB[snx]])
                    kb.op("act", lambda e: e.activation(out=s.Sb, in_=s.Sf[snx], func=AF.Copy), reads=[s.SFB[snx]], writes=[s.SBB])
                    scur = snx
                    yield
                kb.op("act", lambda e: e.activation(out=s.obuf, in_=v4(b2), func=AF.Copy), reads=[PB[b2]], writes=[s.OBB])
                sqo = cvt[:, 0:512].rearrange("p (a b) -> p a b", b=128)
                kb.op("act", lambda e: e.activation(out=sqo, in_=s.obuf, func=AF.Square), reads=[s.OBB], writes=[CVB])
                kb.op("dve", lambda e: e.tensor_reduce(out=s.os8[:, 0:4], in_=sqo, op=ALU.add, axis=AX.X), reads=[CVB], writes=[s.OSB])
                kb.op("dve", lambda e: e.tensor_scalar(out=s.os8[:, 0:4], in0=s.os8[:, 0:4], scalar1=1.0 / 128, scalar2=EPS,
                                                       op0=ALU.mult, op1=ALU.add), reads=[s.OSB], writes=[s.OSB])
                kb.op("act", lambda e: e.activation(out=s.os8[:, 0:4], in_=s.os8[:, 0:4], func=AF.Sqrt), reads=[s.OSB], writes=[s.OSB])
                kb.op("dve", lambda e: e.reciprocal(out=s.os8[:, 0:4], in_=s.os8[:, 0:4]), reads=[s.OSB], writes=[s.OSB])
                kb.op("dve", lambda e: e.tensor_tensor(out=s.onb, in0=s.obuf, in1=s.os8[:, 0:4].unsqueeze(2).to_broadcast([128, 4, 128]),
                                                       op=ALU.mult), reads=[s.OBB, s.OSB], writes=[s.ONB])
                yield
                p3 = v4(b1, BF16)
                for c4 in range(4):
                    kb.op("pe", lambda e: e.transpose(p3[:, c4, :], s.onb[:, c4, :], ident), reads=[s.ONB, CB], writes=[PB[b1]],
                          signal=(c4 == 3))
                kb.op("act", lambda e: e.activation(out=oT[:, h, g * 512:(g + 1) * 512],
                                                    in_=bank(b1, BF16)[:, 0:512], func=AF.Copy, scale=sp_col(l, SP_HN)),
                      reads=[PB[b1], CB], writes=[OTB[h]])
                yield
            wt, wb = load_w(ring, wl, OFF_ZA + h * 128)
            yield from proj_fm_g(wt, wb, hT, HB, s.qi)
            kb.op("act", lambda e: e.activation(out=raw[0], in_=qap, func=AF.Silu), reads=qbs, writes=[RAWB[0]])
            kb.op("dve", lambda e: e.tensor_tensor(out=oT[:, h, :], in0=oT[:, h, :], in1=raw[0], op=ALU.mult),
                  reads=[RAWB[0], OTB[h]], writes=[OTB[h]])
            yield

        def slot_gen(s, heads):
            for h in heads:
                yield from head_gen(h, s)

        OTB = [Buf(f"oT{h}") for h in range(8)]
        slots = [mk_slot(0), mk_slot(1)]
        kb.fresh = dbg.get("fresh", False)
        nheads = dbg.get("nheads", 8 if stop != "head0" else 2)
        if dbg.get("single", False):
            gens = [slot_gen(slots[dbg.get("slot", 0)], list(range(nheads)))]
        else:
            gens = [slot_gen(slots[0], list(range(0, nheads, 2))), slot_gen(slots[1], list(range(1, nheads, 2)))]
        lead = dbg.get("lead", 40)
        for _ in range(lead):
            try:
                next(gens[0])
            except StopIteration:
                break
        active = list(gens)
        while active:
            for gsel in list(active):
                try:
                    next(gsel)
                except StopIteration:
                    active.remove(gsel)
        kb.fresh = False
        kb.barrier()
        if stop in ("head0", "delta"):
            dump("oT", oT, [128, 8, S], [OB])
            return finish()
        al.p = dmark
        yT = al.get([128, 8, S], BF16, "yT")
        YB = Buf("yT")
        raw = [arena[:, X_OFF + i * 4096:X_OFF + (i + 1) * 4096].bitcast(BF16) for i in range(3)]
        RAWB = [Buf(f"rawy{i}") for i in range(3)]


        sg = raw[1]
        SGB = RAWB[1]

        def ypass(gate_off, wmat, first):
            for m in range(8):
                wt, wb = load_w(ring, wl, gate_off + m * 128)
                qap, qbs = proj_fm(wt, wb, hT, HB, 0)
                kb.op("act", lambda e: e.activation(out=sg, in_=qap, func=AF.Sigmoid), reads=qbs, writes=[SGB])
                wt2, wb2 = load_w(ring, wmat, m * 128)
                qap2, qbs2 = proj_fm(wt2, wb2, oT, OB, 1)
                if first:
                    kb.op("dve", lambda e, m=m: e.tensor_tensor(out=yT[:, m, :], in0=qap2, in1=sg, op=ALU.mult),
                          reads=qbs2 + [SGB], writes=[YB])
                else:
                    kb.op("dve", lambda e, m=m: e.tensor_tensor(out=raw[2], in0=qap2, in1=sg, op=ALU.mult),
                          reads=qbs2 + [SGB], writes=[RAWB[2]])
                    kb.op("dve", lambda e, m=m: e.tensor_tensor(out=yT[:, m, :], in0=yT[:, m, :], in1=raw[2], op=ALU.add),
                          reads=[RAWB[2], YB], writes=[YB])

        ypass(OFF_GA, w_a[l], True)
        if stop == "ya":
            dump("yT", yT, [128, 8, S], [YB])
            return finish()

        kb.barrier()
        ax = Alloc(X_OFF, X_OFF + X_SZ)
        raw = [ax.get([128, S], BF16, f"raw{i}") for i in range(3)]
        qbT, kbT = raw[0], raw[1]
        QBB, KBB = RAWB[0], RAWB[1]
        sg = raw[1]
        vbt = ax.get([128, 16, 128], BF16, "vbt")
        VBB = Buf("vbt")
        biasb = [ax.get([128, 5, 128], F32, f"bias{i}") for i in range(2)]
        BIB = [Buf("bias0"), Buf("bias1")]
        stmp = [ax.get([128, 5, 128], F32, f"stmp{i}") for i in range(2)]
        STB = [Buf("st0"), Buf("st1")]
        PTt = [ax.get([128, 5, 128], BF16, f"PT{i}") for i in range(2)]
        PTB = [Buf("PT0"), Buf("PT1")]
        rcp = ax.get([128, 512], F32, "rcp")
        RCB = Buf("rcp")
        it = 0
        for hp in range(8):
            wt, wb = load_w(ring, wl, OFF_QB + hp * 128)
            qap, qbs = proj_fm(wt, wb, hT, HB, 0)
            kb.op("act", lambda e: e.activation(out=qbT, in_=qap, func=AF.Copy, scale=0.125), reads=qbs, writes=[QBB])
            wt, wb = load_w(ring, wl, OFF_KB + hp * 128)
            qap, qbs = proj_fm(wt, wb, hT, HB, 1)
            kb.op("dve", lambda e: e.tensor_copy(out=kbT, in_=qap), reads=qbs, writes=[KBB])
            wt, wb = load_w(ring, wl, OFF_VB + hp * 128)
            for t in range(NT):
                bi = t // 4
                for k in range(8):
                    kb.op("pe", lambda e, t=t, k=k, bi=bi: e.matmul(bank(bi)[:, (t % 4) * 128:(t % 4 + 1) * 128],
                                                                    hT[:, k, t * 128:(t + 1) * 128], wt[:, k, :],
                                                                    start=(k == 0), stop=(k == 7)),
                          reads=[HB, wb], writes=[PB[bi]], signal=(k == 7))
                if t % 4 == 3:
                    kb.op("act", lambda e, bi=bi: e.activation(out=vbt[:, bi * 4:(bi + 1) * 4, :],
                                                               in_=bank(bi).rearrange("p (a b) -> p a b", b=128), func=AF.Copy),
                          reads=[PB[bi]], writes=[VBB])
            for hh in range(2):
                head = hp * 2 + hh
                kb.dma("sp", biasb[hh].rearrange("p a b -> p (a b)"), bias_d[head], writes=[BIB[hh]])
                kb.op("dve", lambda e: e.memset(biasb[hh][0:64, 0, 64:128], -30000.0), writes=[BIB[hh]])
                kb.op("dve", lambda e: e.memset(biasb[hh][64:128, 4, 0:64], -30000.0), writes=[BIB[hh]])
            its = [(hh, qb_i) for qb_i in range(16) for hh in range(2)]

            def stage1(ii):
                hh, qb_i = its[ii]
                base = hh * 64
                sl = ii % 2
                r0 = max(0, 4 - qb_i)
                pst = psum[:, sl * 1024: sl * 1024 + 640].rearrange("p (a b) -> p a b", b=128)
                pstb = [PB[sl * 2], PB[sl * 2 + 1]]
                for r in range(r0, 5):
                    kblk = qb_i - 4 + r
                    kb.op("pe", lambda e: e.matmul(
                        pst[:, r, :], kbT[base:base + 64, kblk * 128:(kblk + 1) * 128],
                        qbT[base:base + 64, qb_i * 128:(qb_i + 1) * 128], start=True, stop=True),
                        reads=[KBB, QBB], writes=pstb, signal=(r == 4))
                kb.op("dve", lambda e: e.tensor_tensor(out=stmp[sl][:, r0:5, :], in0=pst[:, r0:5, :],
                                                       in1=biasb[hh][:, r0:5, :], op=ALU.add),
                      reads=pstb + [BIB[hh]], writes=[STB[sl]])
                kb.op("act", lambda e: e.activation(out=PTt[sl][:, r0:5, :], in_=stmp[sl][:, r0:5, :], func=AF.Exp),
                      reads=[STB[sl]], writes=[PTB[sl]])

            def stage2(ii):
                hh, qb_i = its[ii]
                base = hh * 64
                sl = ii % 2
                r0 = max(0, 4 - qb_i)
                qg, qq = qb_i // 4, qb_i % 4
                po = bank(4 + hh * 2)
                psm = bank(5 + hh * 2)
                for r in range(r0, 5):
                    kblk = qb_i - 4 + r
                    kb.op("pe", lambda e: e.matmul(
                        po[:, qq * 128:(qq + 1) * 128], vbt[:, kblk, :], PTt[sl][:, r, :], start=(r == r0), stop=(r == 4)),
                        reads=[VBB, PTB[sl]], writes=[PB[4 + hh * 2]], signal=(r == 4))
                for r in range(r0, 5):
                    kb.op("pe", lambda e: e.matmul(
                        psm[:, qq * 128:(qq + 1) * 128], ones_b, PTt[sl][:, r, :], start=(r == r0), stop=(r == 4)),
                        reads=[CB, PTB[sl]], writes=[PB[5 + hh * 2]], signal=(r == 4))
                if qq == 3:
                    kb.op("dve", lambda e: e.reciprocal(out=rcp, in_=psm), reads=[PB[5 + hh * 2]], writes=[RCB])
                    kb.op("dve", lambda e: e.tensor_tensor(
                        out=oT[base:base + 64, hp, qg * 512:(qg + 1) * 512], in0=po[base:base + 64, :], in1=rcp[base:base + 64, :],
                        op=ALU.mult), reads=[PB[4 + hh * 2], RCB], writes=[OB])

            stage1(0)
            for ii in range(len(its)):
                if ii + 1 < len(its):
                    stage1(ii + 1)
                stage2(ii)
            if stop == "band0":
                dump("oT", oT, [128, 8, S], [OB])
                return finish()
        if stop == "band":
            dump("oT", oT, [128, 8, S], [OB])
            return finish()
        raw2_keep = raw[2]
        ypass(OFF_GB, w_b[l], False)
        if stop == "yb":
            dump("yT", yT, [128, 8, S], [YB])
            return finish()

        kb.barrier()
        for t in range(NT):
            kb.dma("sp", x_sb[:, t, :], xs_d[t * 128:(t + 1) * 128, :], writes=[XT[t]])
        al.p = mark
        alh = Alloc(X_OFF + X_SZ, X_OFF + X_SZ + 32 * 1024)
        resid_proj(w_o[l], 8, yT, YB, alh)
        kb.barrier()
        if stop == "mix":
            dump("x", x_sb, [128, NT, D], XT)
            return finish()

        al = Alloc(X_OFF + X_SZ, ARENA)
        hT = al.get([128, 8, S], BF16, "hT")
        oT = al.get([128, 8, S], BF16, "oT")
        HB, OB = Buf("hT"), Buf("oT")
        memT = al.get([128, 8, 256], BF16, "memT")
        MTB = Buf("memT")
        MFB = [Buf("memf0"), Buf("memf1")]
        kT = al.get([128, 8, 256], BF16, "kT")
        vm = al.get([128, 2, D], BF16, "vm")
        KTB, VMB = Buf("kT"), Buf("vm")
        qT = al.get([128, 2, S], BF16, "qT")
        QTB = Buf("qT")
        PTm = [al.get([128, 2, 512], BF16, f"PTm{i}") for i in range(2)]
        PMB = [Buf("PTm0"), Buf("PTm1")]
        rcp = al.get([128, 512], F32, "rcp")
        RCB = Buf("rcp")
        ring = weight_ring(al, n=3)
        ring5 = weight_ring(al, n=1, ncol=512)
        mark2 = al.p
        memf = al.get([128, 2, D], F32, "memf")
        for mt in range(2):
            kb.dma("sp", memf[:, mt, :], mem_d[mt * 128:(mt + 1) * 128, :], writes=[MFB[mt]])
        norm_to_T(al, l, SP_GMEM, [memf[:, 0, :], memf[:, 1, :]], MFB, memT, MTB)
        al.p = mark2
        norm_to_T(al, l, SP_GXA, [x_sb[:, t, :] for t in range(NT)], XT, hT, HB)
        al.p = mark2
        for c in range(8):
            wt, wb = load_w(ring, w_mkv[l], c * 128)
            qap, qbs = proj_fm(wt, wb, memT, MTB, 0, ntok=256)
            kb.op("act", lambda e, c=c: e.activation(out=kT[:, c, :], in_=qap[:, 0:256], func=AF.Copy), reads=qbs, writes=[KTB])
        for hlf in range(2):
            wt, wb = load_w(ring5, w_mkv[l], D + hlf * 512, ncol=512)
            for mt in range(2):
                bi = 4 + hlf * 2 + mt
                for k in range(8):
                    kb.op("pe", lambda e, mt=mt, k=k, bi=bi: e.matmul(bank(bi), memT[:, k, mt * 128:(mt + 1) * 128], wt[:, k, :],
                                                                      start=(k == 0), stop=(k == 7)),
                          reads=[MTB, wb], writes=[PB[bi]], signal=(k == 7))
                kb.op("dve", lambda e, mt=mt, hlf=hlf, bi=bi: e.tensor_copy(out=vm[:, mt, hlf * 512:(hlf + 1) * 512], in_=bank(bi)),
                      reads=[PB[bi]], writes=[VMB])
        it = 0
        for hd in range(4):
            for ci in range(2):
                wt, wb = load_w(ring, w_mq[l], (hd * 2 + ci) * 128)
                qap, qbs = proj_fm(wt, wb, hT, HB, ci)
                kb.op("act", lambda e, ci=ci: e.activation(out=qT[:, ci, :], in_=qap, func=AF.Copy, scale=1.0 / 16), reads=qbs, writes=[QTB])
            for g in range(4):
                sl = it % 2
                it += 1
                for mt in range(2):
                    for ci in range(2):
                        kb.op("pe", lambda e, mt=mt, ci=ci, g=g: e.matmul(
                            bank(mt), kT[:, hd * 2 + ci, mt * 128:(mt + 1) * 128], qT[:, ci, g * 512:(g + 1) * 512],
                            start=(ci == 0), stop=(ci == 1)), reads=[KTB, QTB], writes=[PB[mt]], signal=(ci == 1))
                    kb.op("act", lambda e, mt=mt, sl=sl: e.activation(out=PTm[sl][:, mt, :], in_=bank(mt), func=AF.Exp),
                          reads=[PB[mt]], writes=[PMB[sl]])
                for mt in range(2):
                    kb.op("pe", lambda e, mt=mt, sl=sl: e.matmul(bank(2), ones_b, PTm[sl][:, mt, :], start=(mt == 0), stop=(mt == 1)),
                          reads=[CB, PMB[sl]], writes=[PB[2]], signal=(mt == 1))
                kb.op("dve", lambda e: e.reciprocal(out=rcp, in_=bank(2)), reads=[PB[2]], writes=[RCB])
                for ci in range(2):
                    for mt in range(2):
                        kb.op("pe", lambda e, mt=mt, ci=ci, sl=sl: e.matmul(
                            bank(3 + ci), vm[:, mt, (hd * 2 + ci) * 128:(hd * 2 + ci + 1) * 128], PTm[sl][:, mt, :],
                            start=(mt == 0), stop=(mt == 1)), reads=[VMB, PMB[sl]], writes=[PB[3 + ci]], signal=(mt == 1))
                    kb.op("dve", lambda e, ci=ci, g=g: e.tensor_tensor(out=oT[:, hd * 2 + ci, g * 512:(g + 1) * 512], in0=bank(3 + ci),
                                                                   in1=rcp, op=ALU.mult), reads=[PB[3 + ci], RCB], writes=[OB])
        al.p = mark2
        resid_proj(w_mo[l], 8, oT, OB, al)
        kb.barrier()
        if stop == "xa":
            dump("x", x_sb, [128, NT, D], XT)
            return finish()

        al = Alloc(X_OFF + X_SZ, ARENA)
        hT = al.get([128, 8, S], BF16, "hT")
        HB = Buf("hT")
        act = al.get([128, 11, S], BF16, "act")
        ACB = Buf("act")
        sgt = al.get([128, S], BF16, "sgt")
        SGB = Buf("sgt")
        ring = weight_ring(al)
        mark3 = al.p
        norm_to_T(al, l, SP_GFFN, [x_sb[:, t, :] for t in range(NT)], XT, hT, HB)
        for hf in range(2):
            for jj in range(11):
                j = hf * 11 + jj
                wt, wb = load_w(ring, w_gu[l], j * 128)
                qap, qbs = proj_fm(wt, wb, hT, HB, 0)
                kb.op("act", lambda e: e.activation(out=sgt, in_=qap, func=AF.Silu), reads=qbs, writes=[SGB])
                wt, wb = load_w(ring, w_gu[l], DFF + j * 128)
                qap2, qbs2 = proj_fm(wt, wb, hT, HB, 1)
                kb.op("dve", lambda e, jj=jj: e.tensor_tensor(out=act[:, jj, :], in0=qap2, in1=sgt, op=ALU.mult),
                      reads=qbs2 + [SGB], writes=[ACB])
            al.p = mark3
            resid_proj(w_dn[l], 11, act, ACB, al, rows0=hf * 11 * 128)
        kb.barrier()
        if stop == "ffn":
            dump("x", x_sb, [128, NT, D], XT)
            return finish()

    al = Alloc(X_OFF + X_SZ, ARENA)
    gf = al.get([128, D], F32, "gf")
    GFB = Buf("gf")
    kb.dma("sp", gf, gfin_d, writes=[GFB])
    ss = al.get([128, 16], F32, "ss")
    junk = al.get([128, D], BF16, "junk")
    yo = [al.get([128, D], F32, f"yo{i}") for i in range(2)]
    YOB = [Buf("yo0"), Buf("yo1")]
    SB, JB = Buf("ss"), Buf("junk")
    for t in range(NT):
        kb.op("act", lambda e, t=t: e.activation(out=junk, in_=x_sb[:, t, :], func=AF.Square, accum_out=ss[:, t:t + 1]),
              reads=[XT[t]], writes=[JB, SB])
    kb.op("dve", lambda e: e.tensor_scalar(out=ss, in0=ss, scalar1=1.0 / D, scalar2=EPS, op0=ALU.mult, op1=ALU.add), reads=[SB], writes=[SB])
    kb.op("act", lambda e: e.activation(out=ss, in_=ss, func=AF.Sqrt), reads=[SB], writes=[SB])
    kb.op("dve", lambda e: e.reciprocal(out=ss, in_=ss), reads=[SB], writes=[SB])
    for t in range(NT):
        kb.op("dve", lambda e, t=t: e.scalar_tensor_tensor(out=yo[t % 2], in0=x_sb[:, t, :], scalar=ss[:, t:t + 1], in1=gf,
                                                           op0=ALU.mult, op1=ALU.mult), reads=[XT[t], SB, GFB], writes=[YOB[t % 2]])
        kb.dma("sp", out_d[t * 128:(t + 1) * 128, :], yo[t % 2], reads=[YOB[t % 2]])
    return finish()


def _host_layout(inputs):
    f = np.float32
    g = lambda k: np.ascontiguousarray(np.asarray(inputs[k], dtype=f))
    smallp = np.zeros((DEPTH, 128, SP_N), f)
    for l in range(DEPTH):
        for c0, key in ((SP_GMIX, "norm_mix"), (SP_GXA, "norm_xattn"), (SP_GMEM, "norm_mem"), (SP_GFFN, "norm_ffn")):
            smallp[l, :, c0:c0 + 8] = g(key)[l].reshape(8, 128).T
        cw = g("conv_w")[l]
        smallp[l, :, SP_CW:SP_CW + 96] = cw.T.reshape(24, 128, 4).transpose(1, 0, 2).reshape(128, 96)
        smallp[l, :, SP_HN] = g("head_norm")[l]
        smallp[l, :, SP_ALOG:SP_ALOG + 8] = np.broadcast_to(g("a_log")[l][None, :], (128, 8))
        smallp[l, :, SP_DTB:SP_DTB + 8] = np.broadcast_to(g("dt_bias")[l][None, :], (128, 8))
    gfinal = np.ascontiguousarray(np.broadcast_to(g("norm_final")[None, :], (128, D)))
    kin = np.arange(128)[:, None, None]
    r = np.arange(5)[None, :, None]
    qin = np.arange(128)[None, None, :]
    idx = np.clip(qin - kin + 128 * (4 - r), -63, 256) + 63
    biasq = np.ascontiguousarray(g("rel_bias")[:, idx].reshape(16, 128, 640))
    shared = {k: g(k) for k in ("w_in", "w_a_out", "w_b_out", "w_o", "w_mq", "w_mkv", "w_mo", "w_gate_up", "w_down")}
    pp = np.arange(128)[:, None]
    ff = np.arange(128)[None, :]
    cmask = np.stack([(pp // 32 == ff // 32), (pp % 64 < 32) & (ff % 64 >= 32) & (pp // 64 == ff // 64),
                      (pp < 64) & (ff >= 64)], axis=1).astype(f)
    shared.update(smallp=smallp, gfinal=gfinal, biasq=biasq, cmask=np.ascontiguousarray(cmask))
    return shared


_CACHE = {}


def kernel(**inputs):
    shared = _host_layout(inputs)
    x = np.ascontiguousarray(np.asarray(inputs["x"], dtype=np.float32))
    mem = np.ascontiguousarray(np.asarray(inputs["mem"], dtype=np.float32))
    if "nc" not in _CACHE:
        _CACHE["nc"] = build_program()[0]
    nc = _CACHE["nc"]
    in_maps = []
    for b in range(8):
        m = dict(shared)
        m["x"] = x[b]
        m["mem"] = mem[b]
        in_maps.append(m)
    res = run_bass_kernel_spmd(nc, in_maps, core_ids=list(range(8)))
    return np.stack([np.asarray(r["out"], dtype=np.float32) for r in res.results], axis=0)
```
